# Optimizing a Trainium2 kernel written in Bass

```python
import math
import jax, jax.numpy as jnp
from jax import lax
import numpy as np

D_MODEL = 1024
BATCH = 4
SEQ = 8192
DEPTH = 2

N_EVEN = (DEPTH + 1) // 2
N_ODD = DEPTH // 2
ROPE_THETA = 10000.0
NORM_EPS = 1e-6
Q_BLOCK = 128

HG_W = D_MODEL // 2
HGRN_HEAD_DIM = 128
HGRN_HEADS = HG_W // HGRN_HEAD_DIM
HGRN_CHUNK = 64

MLA_NOPE_DIM = 128
MLA_ROPE_DIM = 64
MLA_V_DIM = 128
MLA_HEADS = (D_MODEL - HG_W) // MLA_V_DIM
MLA_Q_LORA = 3 * D_MODEL // 8
MLA_KV_LORA = D_MODEL // 4
MLA_QK_DIM = MLA_NOPE_DIM + MLA_ROPE_DIM
MLA_SCALE = MLA_QK_DIM ** -0.5

AB_SIZES = (HG_W,) * 5 + (MLA_Q_LORA, MLA_KV_LORA, MLA_ROPE_DIM)
AB_IN = sum(AB_SIZES)
AB_SPLITS = tuple(int(s) for s in np.cumsum(AB_SIZES)[:-1])
AB_MIX = HG_W + MLA_HEADS * MLA_V_DIM

DIFF_HEAD_DIM = 64
DIFF_HEADS = D_MODEL // (2 * DIFF_HEAD_DIM)
DIFF_SCALE = DIFF_HEAD_DIM ** -0.5
C_IN = 3 * D_MODEL

FFN_HIDDEN = ((8 * D_MODEL + 3 * 256 - 1) // (3 * 256)) * 256

kernel_name = 'hgrn2_mla_diffattn_hybrid_encoder'

F32 = jnp.float32


def rmsnorm(x, gain):
    xf = x.astype(F32)
    y = xf * lax.rsqrt(jnp.mean(xf * xf, axis=-1, keepdims=True) + NORM_EPS)
    return (y * gain.astype(F32)).astype(x.dtype)


def rope_tables(seq_len, dim):
    inv = 1.0 / (ROPE_THETA ** (jnp.arange(0, dim, 2, dtype=F32) / dim))
    ang = jnp.arange(seq_len, dtype=F32)[:, None] * inv[None, :]
    return jnp.cos(ang), jnp.sin(ang)


def apply_rope(x, cos, sin):
    shape = (1, x.shape[1]) + (1,) * (x.ndim - 3) + (cos.shape[-1],)
    c = cos.reshape(shape).astype(x.dtype)
    s = sin.reshape(shape).astype(x.dtype)
    x1, x2 = jnp.split(x, 2, axis=-1)
    return jnp.concatenate([x1 * c - x2 * s, x2 * c + x1 * s], axis=-1)


def sweep_query_blocks(fn, *qs):
    b, s = qs[0].shape[:2]
    nb = s // Q_BLOCK
    blocks = tuple(jnp.moveaxis(t.reshape((b, nb, Q_BLOCK) + t.shape[2:]), 1, 0) for t in qs)
    out = lax.map(lambda a: fn(*a), blocks)
    return jnp.moveaxis(out, 0, 1).reshape((b, s) + out.shape[3:])


def hgrn2_scan(q, k, v, log_f):
    b, s, h, dk = q.shape
    dv = v.shape[-1]
    L = HGRN_CHUNK
    nc = s // L

    def to_chunks(t):
        return t.reshape(b, nc, L, h, t.shape[-1]).transpose(1, 0, 3, 2, 4).astype(F32)

    incl = jnp.tril(jnp.ones((L, L), dtype=bool))[:, :, None]

    def step(state, inp):
        qc, kc, vc, gc = inp
        cum = jnp.cumsum(gc, axis=2)
        rel = jnp.exp(jnp.where(incl, cum[:, :, :, None, :] - cum[:, :, None, :, :], -jnp.inf))
        scores = jnp.einsum('bhtk,bhtsk,bhsk->bhts', qc, rel, kc)
        o = (jnp.einsum('bhts,bhsv->bhtv', scores, vc)
             + jnp.einsum('bhtk,bhkv->bhtv', qc * jnp.exp(cum), state))
        last = cum[:, :, -1:, :]
        state = (jnp.exp(last[:, :, 0, :, None]) * state
                 + jnp.einsum('bhsk,bhsv->bhkv', kc * jnp.exp(last - cum), vc))
        return state, o

    state0 = jnp.zeros((b, h, dk, dv), F32)
    _, o = lax.scan(step, state0, (to_chunks(q), to_chunks(k), to_chunks(v), to_chunks(log_f)))
    return o.transpose(1, 0, 3, 2, 4).reshape(b, s, h, dv).astype(v.dtype)


def hgrn2_mla_mixer(h, w_in, lb_fwd, lb_bwd, hgrn_norm, q_norm, w_uq, kv_norm, w_ukv, w_out, cos, sin):
    bsz, s, _ = h.shape
    z = h @ w_in
    q_h, f_fw, f_bw, i_h, g_h, cq, ckv, kr = jnp.split(z, AB_SPLITS, axis=-1)

    def heads(t):
        return t.reshape(bsz, s, HGRN_HEADS, HGRN_HEAD_DIM)

    q = heads(jax.nn.silu(q_h)) * (HGRN_HEAD_DIM ** -0.5)
    v = heads(i_h)

    def forget(f_pre, lb):
        fp = f_pre.astype(F32)
        log_f = jnp.log(lb + (1.0 - lb) * jax.nn.sigmoid(fp))
        k = (1.0 - lb) * jax.nn.sigmoid(-fp)
        return heads(k), heads(log_f)

    k_fw, lf_fw = forget(f_fw, lb_fwd)
    k_bw, lf_bw = forget(f_bw, lb_bwd)
    flip = lambda t: jnp.flip(t, axis=1)
    o_a = (hgrn2_scan(q, k_fw, v, lf_fw)
           + flip(hgrn2_scan(flip(q), flip(k_bw), flip(v), flip(lf_bw))))
    o_a = rmsnorm(o_a, hgrn_norm.reshape(HGRN_HEADS, HGRN_HEAD_DIM))
    o_a = o_a.reshape(bsz, s, HG_W) * jax.nn.silu(g_h)

    qf = (rmsnorm(cq, q_norm) @ w_uq).reshape(bsz, s, MLA_HEADS, MLA_QK_DIM)
    q_nope = qf[..., :MLA_NOPE_DIM]
    q_rope = apply_rope(qf[..., MLA_NOPE_DIM:], cos, sin)
    kvf = (rmsnorm(ckv, kv_norm) @ w_ukv).reshape(bsz, s, MLA_HEADS, MLA_NOPE_DIM + MLA_V_DIM)
    k_nope = kvf[..., :MLA_NOPE_DIM]
    v_b = kvf[..., MLA_NOPE_DIM:]
    k_rope = apply_rope(kr, cos, sin)

    def mla_block(qn, qr):
        sc = (jnp.einsum('bqhd,bkhd->bhqk', qn, k_nope)
              + jnp.einsum('bqhd,bkd->bhqk', qr, k_rope))
        p = jax.nn.softmax(sc.astype(F32) * MLA_SCALE, axis=-1)
        return jnp.einsum('bhqk,bkhd->bqhd', p.astype(v_b.dtype), v_b)

    o_b = sweep_query_blocks(mla_block, q_nope, q_rope).reshape(bsz, s, MLA_HEADS * MLA_V_DIM)

    return jnp.concatenate([o_a, o_b], axis=-1) @ w_out


def diff_attention_mixer(h, w_in, lq1, lk1, lq2, lk2, out_norm, w_out, lambda_init, cos, sin):
    bsz, s, _ = h.shape
    q, k, v = jnp.split(h @ w_in, 3, axis=-1)
    q = apply_rope(q.reshape(bsz, s, DIFF_HEADS, 2, DIFF_HEAD_DIM), cos, sin)
    k = apply_rope(k.reshape(bsz, s, DIFF_HEADS, 2, DIFF_HEAD_DIM), cos, sin)
    v = v.reshape(bsz, s, DIFF_HEADS, 2 * DIFF_HEAD_DIM)
    lam = (jnp.exp(jnp.sum(lq1.astype(F32) * lk1.astype(F32)))
           - jnp.exp(jnp.sum(lq2.astype(F32) * lk2.astype(F32))) + lambda_init)

    def diff_block(qb):
        sc = jnp.einsum('bqhcd,bkhcd->bhcqk', qb, k).astype(F32) * DIFF_SCALE
        p = jax.nn.softmax(sc, axis=-1)
        a = p[:, :, 0] - lam * p[:, :, 1]
        return jnp.einsum('bhqk,bkhd->bqhd', a.astype(v.dtype), v)

    o = sweep_query_blocks(diff_block, q)
    o = rmsnorm(o, out_norm) * (1.0 - lambda_init)
    return o.reshape(bsz, s, DIFF_HEADS * 2 * DIFF_HEAD_DIM) @ w_out


def swiglu(h, w_gate, w_up, w_down):
    return (jax.nn.silu(h @ w_gate) * (h @ w_up)) @ w_down


def setup_inputs(seed: int = 0) -> dict:
    key = jax.random.key(seed)
    ks = jax.random.split(key, 22)

    def w(i, shape, fan_in):
        return jax.random.normal(ks[i], shape, F32) * (fan_in ** -0.5)

    def gain(i, shape):
        return 1.0 + 0.02 * jax.random.normal(ks[i], shape, F32)

    return {
        'x': jax.random.normal(ks[0], (BATCH, SEQ, D_MODEL), F32),
        'norm_attn': gain(1, (DEPTH, D_MODEL)),
        'norm_ffn': gain(2, (DEPTH, D_MODEL)),
        'ffn_w_gate': w(3, (DEPTH, D_MODEL, FFN_HIDDEN), D_MODEL),
        'ffn_w_up': w(4, (DEPTH, D_MODEL, FFN_HIDDEN), D_MODEL),
        'ffn_w_down': w(5, (DEPTH, FFN_HIDDEN, D_MODEL), FFN_HIDDEN),
        'ab_w_in': w(6, (N_EVEN, D_MODEL, AB_IN), D_MODEL),
        'hgrn_lower_bound': 0.5 * jax.random.normal(ks[7], (2, N_EVEN + 1, HG_W), F32),
        'hgrn_out_norm': gain(8, (N_EVEN, HG_W)),
        'mla_q_norm': gain(9, (N_EVEN, MLA_Q_LORA)),
        'mla_w_uq': w(10, (N_EVEN, MLA_Q_LORA, MLA_HEADS * MLA_QK_DIM), MLA_Q_LORA),
        'mla_kv_norm': gain(11, (N_EVEN, MLA_KV_LORA)),
        'mla_w_ukv': w(12, (N_EVEN, MLA_KV_LORA, MLA_HEADS * (MLA_NOPE_DIM + MLA_V_DIM)), MLA_KV_LORA),
        'ab_w_out': w(13, (N_EVEN, AB_MIX, D_MODEL), AB_MIX),
        'c_w_in': w(14, (N_ODD, D_MODEL, C_IN), D_MODEL),
        'diff_lambda_q1': 0.1 * jax.random.normal(ks[15], (N_ODD, DIFF_HEAD_DIM), F32),
        'diff_lambda_k1': 0.1 * jax.random.normal(ks[16], (N_ODD, DIFF_HEAD_DIM), F32),
        'diff_lambda_q2': 0.1 * jax.random.normal(ks[17], (N_ODD, DIFF_HEAD_DIM), F32),
        'diff_lambda_k2': 0.1 * jax.random.normal(ks[18], (N_ODD, DIFF_HEAD_DIM), F32),
        'diff_out_norm': gain(19, (N_ODD, 2 * DIFF_HEAD_DIM)),
        'c_w_out': w(20, (N_ODD, D_MODEL, D_MODEL), D_MODEL),
        'final_norm': gain(21, (D_MODEL,)),
    }


def reference(x, norm_attn, norm_ffn, ffn_w_gate, ffn_w_up, ffn_w_down, ab_w_in, hgrn_lower_bound,
              hgrn_out_norm, mla_q_norm, mla_w_uq, mla_kv_norm, mla_w_ukv, ab_w_out, c_w_in,
              diff_lambda_q1, diff_lambda_k1, diff_lambda_q2, diff_lambda_k2, diff_out_norm,
              c_w_out, final_norm):
    seq_len = x.shape[1]
    cos_mla, sin_mla = rope_tables(seq_len, MLA_ROPE_DIM)
    cos_diff, sin_diff = rope_tables(seq_len, DIFF_HEAD_DIM)
    lower_bounds = jnp.cumsum(jax.nn.softmax(hgrn_lower_bound.astype(F32), axis=1), axis=1)

    for layer in range(DEPTH):
        j = layer // 2
        h = rmsnorm(x, norm_attn[layer])
        if layer % 2 == 0:
            mix = hgrn2_mla_mixer(h, ab_w_in[j], lower_bounds[0, j], lower_bounds[1, j],
                                  hgrn_out_norm[j], mla_q_norm[j], mla_w_uq[j], mla_kv_norm[j],
                                  mla_w_ukv[j], ab_w_out[j], cos_mla, sin_mla)
        else:
            lambda_init = 0.8 - 0.6 * math.exp(-0.3 * layer)
            mix = diff_attention_mixer(h, c_w_in[j], diff_lambda_q1[j], diff_lambda_k1[j],
                                       diff_lambda_q2[j], diff_lambda_k2[j], diff_out_norm[j],
                                       c_w_out[j], lambda_init, cos_diff, sin_diff)
        x = x + mix
        x = x + swiglu(rmsnorm(x, norm_ffn[layer]), ffn_w_gate[layer], ffn_w_up[layer], ffn_w_down[layer])
    return rmsnorm(x, final_norm)
```

```python
import math
import os
import numpy as np
from contextlib import ExitStack
import concourse.bass as bass
import concourse.mybir as mybir
from concourse.bass_utils import run_bass_kernel_spmd

F32 = mybir.dt.float32
BF16 = mybir.dt.bfloat16
ALU = mybir.AluOpType
AF = mybir.ActivationFunctionType
AX = mybir.AxisListType

D = 1024
FH = 2816
NHC = FH // 128
EPS = 1e-6
MLA_SCALE = 192 ** -0.5
DIFF_SCALE = 64 ** -0.5
LAMBDA_INIT = 0.8 - 0.6 * math.exp(-0.3 * 1)


class Buf:
    __slots__ = ("name", "last_w", "readers", "sem", "dma_total", "ent", "kind")

    def __init__(self, name):
        self.name = name
        self.last_w = None
        self.readers = []
        self.sem = None
        self.dma_total = 0


class Op:
    __slots__ = ("eng", "fn", "is_dma", "signal", "sigval", "waits", "dwaits", "dbuf", "inc")

    def __init__(self, eng, fn, is_dma):
        self.eng = eng
        self.fn = fn
        self.is_dma = is_dma
        self.signal = False
        self.sigval = 0
        self.waits = []
        self.dwaits = {}
        self.dbuf = None
        self.inc = 16


ENGS = ("pe", "act", "dve", "pool", "sp")
BLOCKNAME = {"pe": "tensor", "act": "scalar", "dve": "vector", "pool": "gpsimd", "sp": "sync"}


_SEMPOOL = [None]


class SemPool:
    def __init__(self, nc, es, n_dma=64):
        self.eng = {e: [es.enter_context(nc.semaphore(f"s_{e}")), 0] for e in ("pe", "act", "dve", "pool")}
        self.dma = {"sp": [[es.enter_context(nc.semaphore(f"dh{i}")), 0] for i in range(44)],
                    "pool": [[es.enter_context(nc.semaphore(f"ds{i}")), 0] for i in range(40)],
                    "cc": [[es.enter_context(nc.semaphore(f"dc{i}")), 0] for i in range(4)]}


class Phase:
    def __init__(self, nc, name, n_dma_sems=60):
        self.nc = nc
        self.name = name
        self.ops = {e: [] for e in ENGS}
        self.es = ExitStack()
        self.pool = _SEMPOOL[0]
        self.esem = {e: self.pool.eng[e][0] for e in ("pe", "act", "dve", "pool")}
        self.free_dma = {k: list(v) for k, v in self.pool.dma.items()}
        self.bufs = []
        self.nbuf = 0

    def sb(self, name, shape, dtype):
        return self.es.enter_context(self.nc.sbuf_tensor(f"{self.name}_{name}", list(shape), dtype))

    def ps(self, name, shape, dtype=F32):
        return self.es.enter_context(self.nc.psum_tensor(f"{self.name}_{name}", list(shape), dtype))

    def buf(self, name=None):
        self.nbuf += 1
        b = Buf(name or f"b{self.nbuf}")
        self.bufs.append(b)
        return b

    def _dep(self, o, w, kind):
        if w is o:
            return
        if w.is_dma:
            b = w.dbuf
            o.dwaits[b] = max(o.dwaits.get(b, 0), b.dma_total)
            return
        if w.eng == o.eng and not o.is_dma:
            if o.eng == "pe":
                return
            if kind != "raw":
                return
        w.signal = True
        o.waits.append(w)

    def op(self, eng, fn, reads=(), writes=(), dma=False, inc=16):
        o = Op(eng, fn, dma)
        o.inc = inc
        for b in reads:
            if b.last_w is not None:
                self._dep(o, b.last_w, "raw")
        for b in writes:
            if b.last_w is not None:
                self._dep(o, b.last_w, "waw")
            for r in b.readers:
                self._dep(o, r, "war")
        for b in reads:
            b.readers.append(o)
        for b in writes:
            b.last_w = o
            b.readers = []
        if dma:
            d = writes[0]
            kind = "cc" if inc != 16 else eng
            if d.sem is None:
                ent = self.free_dma[kind].pop()
                d.sem = ent[0]
                d.dma_total = ent[1]
                d.ent = ent
                d.kind = kind
            assert d.kind == kind, (d.name, d.kind, kind)
            d.dma_total += inc
            d.ent[1] = d.dma_total
            o.dbuf = d
        self.ops[eng].append(o)
        return o

    def dma(self, out, in_, reads=(), writes=(), eng="sp", **kw):
        return self.op(eng, lambda e: e.dma_start(out=out, in_=in_, **kw), reads=reads, writes=writes, dma=True)

    def emit(self):
        nc = self.nc
        fin_d = {}
        for b in self.bufs:
            if b.sem is not None:
                fin_d[b] = b.dma_total
        lasts = []
        for e in ("pe", "act", "dve", "pool"):
            cands = [o for o in self.ops[e] if not o.is_dma]
            if cands:
                cands[-1].signal = True
                lasts.append(cands[-1])
        for e in ENGS:
            fin = Op(e, None, False)
            fin.dwaits = dict(fin_d)
            fin.waits = list(lasts)
            self.ops[e].append(fin)
        for e in ("pe", "act", "dve", "pool"):
            n = self.pool.eng[e][1]
            for o in self.ops[e]:
                if o.signal and not o.is_dma:
                    n += 1
                    o.sigval = n
            self.pool.eng[e][1] = n
        with nc.Block() as block:
            for e in ENGS:

                def body(eng, e=e):
                    seen = {}
                    for o in self.ops[e]:
                        req = {}
                        for w in o.waits:
                            s = self.esem[w.eng]
                            if req.get(s, 0) < w.sigval:
                                req[s] = w.sigval
                        for b, tot in o.dwaits.items():
                            if req.get(b.sem, 0) < tot:
                                req[b.sem] = tot
                        for s, v in req.items():
                            if seen.get(s, 0) < v:
                                eng.wait_ge(s, v)
                                seen[s] = v
                        if o.fn is None:
                            continue
                        ins = o.fn(eng)
                        if o.is_dma:
                            ins.then_inc(o.dbuf.sem, o.inc)
                        elif o.signal:
                            ins.then_inc(self.esem[e], 1)

                getattr(block, BLOCKNAME[e])(body)
        self.es.close()


class Rot:
    def __init__(self, P, name, n, shape, dtype, psum=False):
        self.items = []
        for i in range(n):
            t = P.ps(f"{name}{i}", shape, dtype) if psum else P.sb(f"{name}{i}", shape, dtype)
            self.items.append((t, P.buf(f"{name}{i}")))
        self.i = 0

    def next(self):
        it = self.items[self.i % len(self.items)]
        self.i += 1
        return it


def bcast_rows(handle, n, offset=0, parts=128):
    return bass.AP(handle, offset, [[0, parts], [1, n]])


def make_ident(P, name="ident"):
    ident = P.sb(name, [128, 128], BF16)
    b = P.buf(name)

    P.op("pool", lambda e: e.memset(ident[:], 1.0), writes=[b])
    P.op("pool", lambda e: e.affine_select(out=ident[:], in_=ident[:], pattern=[[-1, 128]], compare_op=ALU.is_ge, fill=0.0,
                                           base=0, channel_multiplier=1), reads=[b], writes=[b])
    P.op("pool", lambda e: e.affine_select(out=ident[:], in_=ident[:], pattern=[[1, 128]], compare_op=ALU.is_ge, fill=0.0,
                                           base=0, channel_multiplier=-1), reads=[b], writes=[b])
    return ident, b


def eps_tile(P):
    t = P.sb("epsc", [128, 1], F32)
    b = P.buf("epsc")
    P.op("pool", lambda e: e.memset(t[:], EPS), writes=[b])
    return t, b


class NormT:
    def __init__(self, P, ident, ident_b, gains, nbuf=2):
        self.P = P
        self.ident, self.ident_b = ident, ident_b
        self.g = {}
        for k, (h, off) in gains.items():
            t = P.sb(f"gain_{k}", [128, D], F32)
            b = P.buf(f"gain_{k}")
            P.dma(t[:], bcast_rows(h, D, off), writes=[b])
            self.g[k] = (t, b)
        self.sq = Rot(P, "nt_sq", nbuf, [128, D], BF16)
        self.st = Rot(P, "nt_st", 4, [128, 4], F32)
        self.hb = Rot(P, "nt_hb", nbuf, [128, D], BF16)
        self.pt = Rot(P, "nt_pt", 1, [128, 8, 128], BF16, psum=True)
        self.eps, self.epsb = eps_tile(P)

    def rstd(self, xt, xb, width=D):
        P = self.P
        sq, sqb = self.sq.next()
        st, stb = self.st.next()
        P.op("act", lambda e: e.activation(out=sq[:, 0:width], in_=xt, func=AF.Square, accum_out=st[:, 0:1]),
             reads=[xb], writes=[sqb, stb])
        P.op("act", lambda e: e.activation(out=st[:, 1:2], in_=st[:, 0:1], func=AF.Sqrt, bias=self.eps[:, 0:1], scale=1.0 / width),
             reads=[stb, self.epsb], writes=[stb])
        P.op("dve", lambda e: e.reciprocal(out=st[:, 2:3], in_=st[:, 1:2]), reads=[stb], writes=[stb])
        return st, stb

    def run(self, xt, xb, gain, hT, hTb, col0):
        P = self.P
        gt, gb = self.g[gain]
        st, stb = self.rstd(xt, xb)
        hb, hbb = self.hb.next()
        P.op("dve", lambda e: e.scalar_tensor_tensor(out=hb[:], in0=xt, scalar=st[:, 2:3], in1=gt[:],
                                                     op0=ALU.mult, op1=ALU.mult), reads=[xb, stb, gb], writes=[hbb])
        pt, ptb = self.pt.next()

        def tr(e):
            ins = None
            for c in range(8):
                ins = e.transpose(out=pt[:, c, :], in_=hb[:, c * 128:(c + 1) * 128], identity=self.ident[:])
            return ins

        P.op("pe", tr, reads=[hbb, self.ident_b], writes=[ptb])
        P.op("act", lambda e: e.copy(out=hT[:, :, col0:col0 + 128], in_=pt[:]), reads=[ptb], writes=[hTb])


WCOLS = 1024


def load_w(P, name, handle, kchunks, ncols):
    t = P.sb(name, [128, kchunks, ncols], BF16)
    b = P.buf(name)
    src = handle.ap().rearrange("(c p) n -> p c n", p=128)
    for c0 in range(kchunks):
        for n0 in range(0, ncols, WCOLS):
            n1 = min(ncols, n0 + WCOLS)
            P.dma(t[:, c0, n0:n1], src[:, c0, n0:n1], writes=[b], eng="pool")
    return t, b


def mm_group(P, ps, psb, w, wb, kch, wc0, m, rhs, rhsb, rhs_slice, n):
    def f(e):
        ins = None
        for k in range(kch):
            ins = e.matmul(ps[0:m, 0:n], lhsT=w[:, k, wc0:wc0 + m], rhs=rhs[:, k, rhs_slice], start=(k == 0),
                           stop=(k == kch - 1))
        return ins

    P.op("pe", f, reads=[wb, rhsb], writes=[psb])


def mm_group_tm(P, ps, psb, lhs, lhsb, lhs_slice, kch, w, wb, wc0, n):
    def f(e):
        ins = None
        for k in range(kch):
            ins = e.matmul(ps[:, 0:n], lhsT=lhs[:, k, lhs_slice], rhs=w[:, k, wc0:wc0 + n], start=(k == 0),
                           stop=(k == kch - 1))
        return ins

    P.op("pe", f, reads=[lhsb, wb], writes=[psb])


LOOK = 2


def run_pipelined(iters, look):
    n = len(iters)
    for i in range(n + look):
        if i < n:
            iters[i][0]()
        if i - look >= 0:
            it = iters[i - look]
            it[1]()
            if it[2] is not None:
                it[2]()


def attn_fm(nc, name, S, SH, heads, ncomp, scale, look, n_s, n_p, setup, finish, den_pe=()):
    P = Phase(nc, name)
    NT = S // 128
    nq = 512
    NQB = SH // nq
    C = setup(P)
    ps_s = Rot(P, "ps_s", n_s, [128, 512], F32, psum=True)
    ps_o = [Rot(P, f"ps_o{c}", (2 if ncomp == 1 else 1), [128, 512], F32, psum=True) for c in range(ncomp)]
    pT = Rot(P, "pT", n_p, [128, 512], BF16)
    nacc = 2
    accr = [Rot(P, f"acc{i}", 2, [128, 512], F32) for i in range(nacc)]
    pdr = {c: Rot(P, f"pd{c}", 1, [128, 512], F32, psum=True) for c in den_pe}
    for hi, head in enumerate(heads):
        ctx = head["load"](P, hi % 2)
        iters = []
        for qb in range(NQB):
            po = [ps_o[c].next() for c in range(ncomp)]
            acc = [accr[i].next() for i in range(nacc)]
            pd = {c: pdr[c].next() for c in den_pe}
            for kt in range(NT):
                sl = [ps_s.next() for c in range(ncomp)]
                pl = [pT.next() for c in range(ncomp)]

                def rec_qk(sl=sl, pl=pl, kt=kt, qb=qb, ctx=ctx, head=head):
                    for c in range(ncomp):
                        s_, sb_ = sl[c]
                        P.op("pe", lambda e, s_=s_, c=c: head["qk"](e, ctx, s_, c, kt, qb), reads=ctx["kq_bufs"], writes=[sb_])
                    for c in range(ncomp):
                        s_, sb_ = sl[c]
                        p_, pb_ = pl[c]
                        P.op("act", lambda e, s_=s_, p_=p_: e.activation(out=p_[:], in_=s_[:], func=AF.Exp, scale=scale),
                             reads=[sb_], writes=[pb_])

                def rec_pv(pl=pl, kt=kt, po=po, acc=acc, ctx=ctx, pd=pd):
                    vt, vb = ctx["v"]
                    for c in range(ncomp):
                        p_, pb_ = pl[c]
                        o_, ob_ = po[c]
                        P.op("pe", lambda e, p_=p_, o_=o_: e.matmul(o_[:], lhsT=vt[:, kt, :], rhs=p_[:], start=(kt == 0),
                                                                     stop=(kt == NT - 1)), reads=[pb_, vb], writes=[ob_])
                    for c in den_pe:
                        p_, pb_ = pl[c]
                        d_, db_ = pd[c]
                        P.op("pe", lambda e, p_=p_, d_=d_: e.matmul(d_[:], lhsT=C["ones16"][:], rhs=p_[:], start=(kt == 0),
                                                                     stop=(kt == NT - 1)), reads=[pb_, C["ones16b"]], writes=[db_])
                    for c in range(ncomp):
                        if c in den_pe:
                            continue
                        p_, pb_ = pl[c]
                        if ncomp == 2:
                            ai, first = c, (kt == 0)
                        else:
                            ai, first = kt % 2, (kt < 2)
                        a_, ab_ = acc[ai]
                        eng = "dve" if ai == 0 else "pool"
                        if first:
                            P.op(eng, lambda e, a_=a_, p_=p_: e.tensor_copy(out=a_[:], in_=p_[:]), reads=[pb_], writes=[ab_])
                        else:
                            P.op(eng, lambda e, a_=a_, p_=p_: e.tensor_tensor(out=a_[:], in0=a_[:], in1=p_[:], op=ALU.add),
                                 reads=[pb_, ab_], writes=[ab_])

                post = None
                if kt == NT - 1:
                    post = (lambda hi=hi, qb=qb, po=po, acc=acc, pd=pd: finish(P, C, hi, qb, po, acc, pd))
                iters.append((rec_qk, rec_pv, post))
        run_pipelined(iters, look)
    P.emit()


def build(S, npair_groups, debug=False, upto=None):
    SH = S // 2
    NBA = S // 512
    NBO = SH // 512
    NT = S // 128
    NCH = S // 64
    nc = bass.Bass("TRN2", target_bir_lowering=False)
    _es = ExitStack()
    _SEMPOOL[0] = SemPool(nc, _es)

    def din(name, shape):
        return nc.dram_tensor(name, list(shape), F32, kind="ExternalInput")

    def scr(name, shape, dt, dbg=False):
        if dbg and debug:
            return nc.dram_tensor(name, list(shape), dt, kind="ExternalOutput")
        return nc.dram_tensor(name, list(shape), dt)

    x_in = din("x", [S, D])
    w_a = din("w_a", [D, 23 * 128 + 512])
    w_uq = din("w_uq", [384, 1024])
    w_ukv = din("w_ukv", [256, 1024])
    w_o0 = din("w_o0", [D, D])
    w_c = din("w_c", [D, 16 * 128 + 16 * 128 + D])
    w_o1 = din("w_o1", [D, D])
    ffn_g = [din(f"ffn_g{l}", [D, FH]) for l in range(2)]
    ffn_u = [din(f"ffn_u{l}", [D, FH]) for l in range(2)]
    ffn_d = [din(f"ffn_d{l}", [FH, D]) for l in range(2)]
    cols_in = din("cols", [128, 32])
    rows_in = din("rows", [8, D])
    out = nc.dram_tensor("out", [SH, D], F32, kind="ExternalOutput")

    ROPEC = din("ropec_t", [128, S])
    ROPES = din("ropes_t", [128, S])
    QT0 = scr("QT0", [2, 4, 128, SH], BF16)
    KT0 = scr("KT0", [2, 4, 128, SH], BF16)
    KH0 = scr("KH0", [2, 4, S, 128], BF16)
    DEC0 = scr("DEC0", [2, 4, 128, NCH], F32)
    V0 = scr("V0", [S, 512], BF16)
    GATE0 = scr("GATE0", [4, 128, SH], F32)
    OFW = scr("OFW", [4, 128, SH], F32)
    QN = scr("QN", [4, 128, SH], BF16)
    QR = scr("QR", [2, 128, SH], BF16)
    KN = scr("KN", [4, 128, S], BF16)
    KR = scr("KR", [128, S], BF16)
    VBm = scr("VBm", [4, S, 128], BF16)
    MIXT = scr("MIXT", [8, 128, SH], BF16, dbg=True)
    X1A = scr("X1A", [SH, D], F32)
    X1 = scr("X1", [SH, D], F32, dbg=True)
    Q1T = scr("Q1T", [8, 128, SH], BF16)
    VR = min(512, SH)
    NVC = SH // VR
    K1S = [scr(f"K1S{h}", [128, SH], BF16) for h in range(8)]
    V1S = [scr(f"V1S{c}", [VR, D], BF16) for c in range(NVC)]
    K1G = [scr(f"K1G{h}", [2 * 128, SH], BF16) for h in range(8)]
    V1G = [scr(f"V1G{c}", [2 * VR, D], BF16) for c in range(NVC)]
    MIXT1 = scr("MIXT1", [8, 128, SH], BF16, dbg=True)
    X2A = scr("X2A", [SH, D], F32)

    P = Phase(nc, "pA")
    ident, identb = make_ident(P)
    wa, wab = load_w(P, "wa", w_a, 8, 23 * 128 + 512)
    wuq, wuqb = load_w(P, "wuq", w_uq, 3, 1024)
    wukv, wukvb = load_w(P, "wukv", w_ukv, 2, 1024)
    cols = P.sb("cols", [128, 32], F32); colsb = P.buf("cols")
    P.dma(cols[:], cols_in.ap(), writes=[colsb])
    lbt = P.sb("lbt", [128, 32], F32); lbb = P.buf("lbt")
    P.op("dve", lambda e: e.tensor_tensor(out=lbt[:, 0:8], in0=cols[:, 0:8], in1=cols[:, 8:16], op=ALU.subtract),
         reads=[colsb], writes=[lbb])
    P.op("act", lambda e: e.activation(out=lbt[:, 8:16], in_=lbt[:, 0:8], func=AF.Sigmoid), reads=[lbb], writes=[lbb])
    P.op("dve", lambda e: e.tensor_scalar(out=lbt[:, 16:24], in0=lbt[:, 8:16], scalar1=-1.0, scalar2=1.0, op0=ALU.mult,
                                          op1=ALU.add), reads=[lbb], writes=[lbb])
    P.op("dve", lambda e: e.tensor_scalar(out=lbt[:, 24:32], in0=lbt[:, 16:24], scalar1=-1.0, scalar2=None, op0=ALU.mult),
         reads=[lbb], writes=[lbb])
    ones = P.sb("ones", [128, 128], BF16); onesb = P.buf("ones")
    P.op("pool", lambda e: e.memset(ones[:], 1.0), writes=[onesb])
    rmask = P.sb("rmask", [128, 8, 64], F32); rmb = P.buf("rmask")

    P.op("pool", lambda e: e.memset(rmask[:], 1.0), writes=[rmb])
    P.op("pool", lambda e: e.memset(rmask[:, :, 0:1], 0.0), reads=[rmb], writes=[rmb])
    NA = NormT(P, ident, identb, {"attn0": (rows_in, 0)})
    xr = Rot(P, "x", 3, [128, D], F32)
    hTr = Rot(P, "hT", 2, [128, 8, 512], BF16)
    psr = Rot(P, "ps", 4, [128, 512], F32, psum=True)
    ps_ss = Rot(P, "ps_ss", 1, [128, 512], F32, psum=True)
    ps_tm = Rot(P, "ps_tm", 1, [128, 512], F32, psum=True)
    ps_kt = Rot(P, "ps_kt", 1, [128, 4, 128], BF16, psum=True)
    f32r = Rot(P, "f32", 12, [128, 512], F32)
    b16r = Rot(P, "b16", 8, [128, 512], BF16)
    sgr = Rot(P, "sgr", 8, [128, 512], F32)
    qsr = Rot(P, "qs", 5, [128, 512], F32)
    rtab = Rot(P, "rtab", 4, [128, 512], F32)
    lat = Rot(P, "lat", 2, [128, 3, 512], F32)
    latn = Rot(P, "latn", 2, [128, 3, 512], BF16)
    kh16 = Rot(P, "kh16", 3, [128, 4, 128], BF16)
    v16 = Rot(P, "v16", 3, [128, 512], BF16)
    decr = Rot(P, "dec", 4, [128, 8], F32)
    D_ = {k: P.buf(k) for k in "QT0 KT0 KH0 DEC0 V0 GATE0 QN QR KN KR VBm".split()}

    for t in range(NBA):
        own = t < NBO
        c0 = t * 512
        sl = slice(c0, c0 + 512)
        hT, hTb = hTr.next()
        for j in range(4):
            xt, xb = xr.next()
            P.dma(xt[:], x_in.ap()[c0 + j * 128:c0 + (j + 1) * 128, :], writes=[xb])
            NA.run(xt[:], xb, "attn0", hT, hTb, j * 128)
        ct, ctb = rtab.next(); st_, stb_ = rtab.next()
        P.dma(ct[:], ROPEC.ap()[:, sl], writes=[ctb])
        P.dma(st_[:], ROPES.ap()[:, sl], writes=[stb_])

        def fm(group, ps, psb):
            mm_group(P, ps, psb, wa, wab, 8, group * 128, 128, hT, hTb, slice(0, 512), 512)

        for j in range(4):
            ps, psb = ps_tm.next()
            mm_group_tm(P, ps, psb, hT, hTb, slice(j * 128, (j + 1) * 128), 8, wa, wab, 23 * 128, 512)
            v, vb = v16.next()
            P.op("act", lambda e, v=v, ps=ps: e.copy(out=v[:], in_=ps[:]), reads=[psb], writes=[vb])
            P.dma(V0.ap()[c0 + j * 128:c0 + (j + 1) * 128, :], v[:], reads=[vb], writes=[D_["V0"]], eng="pool")
        qs = []
        if own:
            for h in range(4):
                ps, psb = psr.next()
                fm(h, ps, psb)
                q, qb_ = qsr.next()
                P.op("act", lambda e, q=q, ps=ps: e.activation(out=q[:], in_=ps[:], func=AF.Silu), reads=[psb], writes=[qb_])
                qs.append((q, qb_))
            for h in range(4):
                ps, psb = psr.next()
                fm(12 + h, ps, psb)
                g, gb = f32r.next()
                P.op("act", lambda e, g=g, ps=ps: e.activation(out=g[:], in_=ps[:], func=AF.Silu), reads=[psb], writes=[gb])
                P.dma(GATE0.ap()[h, :, sl], g[:], reads=[gb], writes=[D_["GATE0"]], eng="pool")
        sig_l = []
        for d in ((0, 1) if own else (1,)):
            for h in range(4):
                ps, psb = psr.next()
                fm(4 + 4 * d + h, ps, psb)
                sg, sgb = sgr.next()
                P.op("act", lambda e, sg=sg, ps=ps: e.activation(out=sg[:], in_=ps[:], func=AF.Sigmoid), reads=[psb], writes=[sgb])
                sig_l.append((d, h, sg, sgb))
        for (d, h, sg, sgb) in sig_l:
            ci = d * 4 + h
            g, gb = f32r.next()
            P.op("act", lambda e, g=g, sg=sg, ci=ci: e.activation(out=g[:], in_=sg[:], func=AF.Ln, bias=lbt[:, 8 + ci:9 + ci],
                                                                 scale=lbt[:, 16 + ci:17 + ci]), reads=[sgb, lbb], writes=[gb])
            k, kb = f32r.next()
            P.op("pool", lambda e, k=k, sg=sg, ci=ci: e.tensor_scalar(out=k[:], in0=sg[:], scalar1=lbt[:, 24 + ci:25 + ci],
                                                                     scalar2=lbt[:, 16 + ci:17 + ci], op0=ALU.mult, op1=ALU.add),
                 reads=[sgb, lbb], writes=[kb])
            b_, bb = f32r.next()
            P.op("dve", lambda e, b_=b_, g=g: e.tensor_tensor_scan(out=b_[:], data0=rmask[:].rearrange("p a b -> p (a b)"),
                                                                  data1=g[:], initial=0.0, op0=ALU.mult, op1=ALU.add),
                 reads=[gb, rmb], writes=[bb])
            b3 = b_[:].rearrange("p (a b) -> p a b", b=64)
            dl, dlb = f32r.next()
            P.op("pool", lambda e, dl=dl, b3=b3: e.tensor_tensor(out=dl[:].rearrange("p (a b) -> p a b", b=64), in0=b3,
                                                               in1=b3[:, :, 63:64].to_broadcast([128, 8, 64]), op=ALU.subtract),
                 reads=[bb], writes=[dlb])
            dec, decb = decr.next()
            P.op("act", lambda e, dec=dec, b3=b3: e.activation(out=dec[:], in_=b3[:, :, 63], func=AF.Exp), reads=[bb], writes=[decb])
            P.dma(DEC0.ap()[d, h, :, t * 8:(t + 1) * 8], dec[:], reads=[decb], writes=[D_["DEC0"]], eng="pool")
            if d == 0:
                bq, bqb = b_, bb
                ex, exb = dl, dlb
                ex_scale = -1.0
            else:
                bq, bqb = f32r.next()
                P.op("pool", lambda e, bq=bq, g=g, dl=dl: e.tensor_tensor(out=bq[:], in0=g[:], in1=dl[:], op=ALU.subtract),
                     reads=[gb, dlb], writes=[bqb])
                ex, exb = f32r.next()
                P.op("pool", lambda e, ex=ex, b_=b_, g=g: e.tensor_tensor(out=ex[:], in0=b_[:], in1=g[:], op=ALU.subtract),
                     reads=[bb, gb], writes=[exb])
                ex_scale = 1.0
            ek, ekb = f32r.next()
            P.op("act", lambda e, ek=ek, ex=ex, ex_scale=ex_scale: e.activation(out=ek[:], in_=ex[:], func=AF.Exp, scale=ex_scale),
                 reads=[exb], writes=[ekb])
            kh, khb = b16r.next()
            P.op("dve", lambda e, kh=kh, k=k, ek=ek: e.tensor_tensor(out=kh[:], in0=k[:], in1=ek[:], op=ALU.mult),
                 reads=[kb, ekb], writes=[khb])
            pk, pkb = ps_kt.next()

            def trk(e, pk=pk, kh=kh):
                ins = None
                for j in range(4):
                    ins = e.transpose(out=pk[:, j, :], in_=kh[:, j * 128:(j + 1) * 128], identity=ident[:])
                return ins

            P.op("pe", trk, reads=[khb, identb], writes=[pkb])
            kt_, ktb = kh16.next()
            P.op("act", lambda e, kt_=kt_, pk=pk: e.copy(out=kt_[:], in_=pk[:]), reads=[pkb], writes=[ktb])
            P.dma(KH0.ap()[d, h, sl, :].rearrange("(j p) k -> p j k", p=128), kt_[:], reads=[ktb], writes=[D_["KH0"]], eng="pool")
            if own:
                eq, eqb = f32r.next()
                P.op("act", lambda e, eq=eq, bq=bq: e.activation(out=eq[:], in_=bq[:], func=AF.Exp), reads=[bqb], writes=[eqb])
                en, enb = f32r.next()
                P.op("act", lambda e, en=en, bq=bq: e.activation(out=en[:], in_=bq[:], func=AF.Exp, scale=-1.0), reads=[bqb], writes=[enb])
                qt, qtb = b16r.next()
                q, qb_ = qs[h]
                P.op("dve", lambda e, qt=qt, q=q, eq=eq: e.scalar_tensor_tensor(out=qt[:], in0=q[:], scalar=128 ** -0.5, in1=eq[:],
                                                                              op0=ALU.mult, op1=ALU.mult), reads=[qb_, eqb], writes=[qtb])
                P.dma(QT0.ap()[d, h, :, sl], qt[:], reads=[qtb], writes=[D_["QT0"]], eng="pool")
                kt2, kt2b = b16r.next()
                P.op("pool", lambda e, kt2=kt2, k=k, en=en: e.tensor_tensor(out=kt2[:], in0=k[:], in1=en[:], op=ALU.mult),
                     reads=[kb, enb], writes=[kt2b])
                P.dma(KT0.ap()[d, h, :, sl], kt2[:], reads=[kt2b], writes=[D_["KT0"]], eng="pool")
        lat_jobs = ([(16, 3, 20)] if own else []) + [(19, 2, 23)]
        for (g0, ng, gcol) in lat_jobs:
            lt, ltb = lat.next()
            for i in range(ng):
                ps, psb = psr.next()
                fm(g0 + i, ps, psb)
                P.op("act", lambda e, lt=lt, ps=ps, i=i: e.copy(out=lt[:, i, :], in_=ps[:]), reads=[psb], writes=[ltb])
            ss, ssb = ps_ss.next()
            sqs = []
            for i in range(ng):
                sq, sqb = b16r.next()
                P.op("pool", lambda e, sq=sq, lt=lt, i=i: e.tensor_tensor(out=sq[:], in0=lt[:, i, :], in1=lt[:, i, :], op=ALU.mult),
                     reads=[ltb], writes=[sqb])
                sqs.append((sq, sqb))

            def ssm(e, ss=ss, sqs=sqs):
                ins = None
                for i, (sq, _) in enumerate(sqs):
                    ins = e.matmul(ss[:], lhsT=ones[:], rhs=sq[:], start=(i == 0), stop=(i == len(sqs) - 1))
                return ins

            P.op("pe", ssm, reads=[onesb] + [b for _, b in sqs], writes=[ssb])
            rs, rsb = f32r.next()
            P.op("act", lambda e, rs=rs, ss=ss, ng=ng: e.activation(out=rs[:], in_=ss[:], func=AF.Sqrt, bias=NA.eps[:, 0:1],
                                                                   scale=1.0 / (ng * 128)), reads=[ssb, NA.epsb], writes=[rsb])
            P.op("dve", lambda e, rs=rs: e.reciprocal(out=rs[:], in_=rs[:]), reads=[rsb], writes=[rsb])
            ln, lnb = latn.next()
            for i in range(ng):
                P.op("dve", lambda e, ln=ln, lt=lt, rs=rs, i=i, gcol=gcol: e.scalar_tensor_tensor(
                    out=ln[:, i, :], in0=lt[:, i, :], scalar=cols[:, gcol + i:gcol + i + 1], in1=rs[:], op0=ALU.mult, op1=ALU.mult),
                    reads=[ltb, rsb, colsb], writes=[lnb])
            if g0 == 16:
                for h in range(4):
                    ps, psb = psr.next()
                    mm_group(P, ps, psb, wuq, wuqb, 3, h * 128, 128, ln, lnb, slice(0, 512), 512)
                    o, ob = b16r.next()
                    P.op("act", lambda e, o=o, ps=ps: e.copy(out=o[:], in_=ps[:]), reads=[psb], writes=[ob])
                    P.dma(QN.ap()[h, :, sl], o[:], reads=[ob], writes=[D_["QN"]], eng="pool")
                for pr in range(2):
                    ps, psb = psr.next()
                    mm_group(P, ps, psb, wuq, wuqb, 3, (4 + pr) * 128, 128, ln, lnb, slice(0, 512), 512)
                    ps2, ps2b = psr.next()
                    mm_group(P, ps2, ps2b, wuq, wuqb, 3, (6 + pr) * 128, 128, ln, lnb, slice(0, 512), 512)
                    t1, t1b = f32r.next()
                    P.op("dve", lambda e, t1=t1, ps=ps, ct=ct: e.tensor_tensor(out=t1[:], in0=ps[:], in1=ct[:], op=ALU.mult),
                         reads=[psb, ctb], writes=[t1b])
                    t2, t2b = f32r.next()
                    P.op("dve", lambda e, t2=t2, ps2=ps2, st_=st_: e.tensor_tensor(out=t2[:], in0=ps2[:], in1=st_[:], op=ALU.mult),
                         reads=[ps2b, stb_], writes=[t2b])
                    o, ob = b16r.next()
                    P.op("pool", lambda e, o=o, t1=t1, t2=t2: e.tensor_tensor(out=o[:], in0=t1[:], in1=t2[:], op=ALU.add),
                         reads=[t1b, t2b], writes=[ob])
                    P.dma(QR.ap()[pr, :, sl], o[:], reads=[ob], writes=[D_["QR"]], eng="pool")
            else:
                for h in range(4):
                    ps, psb = psr.next()
                    mm_group(P, ps, psb, wukv, wukvb, 2, h * 128, 128, ln, lnb, slice(0, 512), 512)
                    o, ob = b16r.next()
                    P.op("act", lambda e, o=o, ps=ps: e.copy(out=o[:], in_=ps[:]), reads=[psb], writes=[ob])
                    P.dma(KN.ap()[h, :, sl], o[:], reads=[ob], writes=[D_["KN"]], eng="pool")
                for j in range(4):
                    ps, psb = ps_tm.next()
                    mm_group_tm(P, ps, psb, ln, lnb, slice(j * 128, (j + 1) * 128), 2, wukv, wukvb, 512, 512)
                    v, vb = v16.next()
                    P.op("act", lambda e, v=v, ps=ps: e.copy(out=v[:], in_=ps[:]), reads=[psb], writes=[vb])
                    P.dma(VBm.ap()[:, c0 + j * 128:c0 + (j + 1) * 128, :].rearrange("h p d -> p h d"),
                          v[:].rearrange("p (h d) -> p h d", d=128), reads=[vb], writes=[D_["VBm"]], eng="pool")
        ps, psb = psr.next()
        fm(21, ps, psb)
        ps2, ps2b = psr.next()
        fm(22, ps2, ps2b)
        t1, t1b = f32r.next()
        P.op("dve", lambda e, t1=t1, ps=ps, ct=ct: e.tensor_tensor(out=t1[:], in0=ps[:], in1=ct[:], op=ALU.mult), reads=[psb, ctb], writes=[t1b])
        t2, t2b = f32r.next()
        P.op("dve", lambda e, t2=t2, ps2=ps2, st_=st_: e.tensor_tensor(out=t2[:], in0=ps2[:], in1=st_[:], op=ALU.mult), reads=[ps2b, stb_], writes=[t2b])
        o, ob = b16r.next()
        P.op("pool", lambda e, o=o, t1=t1, t2=t2: e.tensor_tensor(out=o[:], in0=t1[:], in1=t2[:], op=ALU.add), reads=[t1b, t2b], writes=[ob])
        P.dma(KR.ap()[:, sl], o[:], reads=[ob], writes=[D_["KR"]], eng="pool")
    P.emit()
    if upto == "A":
        return nc

    P = Phase(nc, "pB")
    cols = P.sb("cols", [128, 32], F32); colsb = P.buf("cols")
    P.dma(cols[:], cols_in.ap(), writes=[colsb])
    ones = P.sb("ones", [128, 128], BF16); onesb = P.buf("ones")
    P.op("pool", lambda e: e.memset(ones[:], 1.0), writes=[onesb])
    epsB, epsBb = eps_tile(P)
    masks = []
    for d in range(2):
        m = P.sb(f"mask{d}", [128, 128], F32); mb = P.buf(f"mask{d}")

        P.op("pool", lambda e, m=m: e.memset(m[:], 1.0), writes=[mb])
        if d == 0:
            P.op("pool", lambda e, m=m: e.affine_select(out=m[:], in_=m[:], pattern=[[1, 128]], compare_op=ALU.is_ge, fill=0.0,
                                                       base=0, channel_multiplier=-1), reads=[mb], writes=[mb])
            P.op("pool", lambda e, m=m: e.memset(m[0:64, 64:128], 0.0), reads=[mb], writes=[mb])
        else:
            P.op("pool", lambda e, m=m: e.affine_select(out=m[:], in_=m[:], pattern=[[-1, 128]], compare_op=ALU.is_ge, fill=0.0,
                                                       base=0, channel_multiplier=1), reads=[mb], writes=[mb])
            P.op("pool", lambda e, m=m: e.memset(m[64:128, 0:64], 0.0), reads=[mb], writes=[mb])
        m2 = P.sb(f"maskf{d}", [128, 128], F32); m2b = P.buf(f"maskf{d}")
        P.op("dve", lambda e, m=m, m2=m2: e.tensor_copy(out=m2[:], in_=m[:]), reads=[mb], writes=[m2b])
        masks.append((m2, m2b))
    St = P.sb("St", [128, 4, 128], F32); Sb = [P.buf(f"S{h}") for h in range(4)]
    S16 = P.sb("S16", [128, 4, 128], BF16); S16b = P.buf("S16")
    qtr = Rot(P, "qt", 2, [128, 4, 512], BF16)
    ktr = Rot(P, "kt", 2, [128, 4, 512], BF16)
    khr = Rot(P, "kh", 2, [128, 4, 4, 128], BF16)
    vr = Rot(P, "v", 2, [128, 4, 512], BF16)
    dcr = Rot(P, "dc", 2, [128, 4, 8], F32)
    ofr = Rot(P, "of", 2, [128, 4, 512], F32)
    gtr = Rot(P, "gt", 2, [128, 4, 512], F32)
    psO = [Rot(P, f"psO{h}", 1, [128, 512], F32, psum=True) for h in range(4)]
    psXr = Rot(P, "psX", 2, [128, 4, 128], F32, psum=True)
    psSr = Rot(P, "psS", 1, [128, 4, 128], F32, psum=True)
    ps_ss = Rot(P, "ps_ss", 1, [128, 512], F32, psum=True)
    atr = Rot(P, "at", 2, [128, 4, 128], BF16)
    sc32r = Rot(P, "sc32", 2, [128, 4, 128], F32)
    ev = Rot(P, "ev", 6, [128, 512], F32)
    e16 = Rot(P, "e16", 4, [128, 512], BF16)
    DOF = P.buf("OFW"); DMX = P.buf("MIXT")

    def reset_state():
        for h in range(4):
            P.op("pool", lambda e, h=h: e.memset(St[:, h, :], 0.0), writes=[Sb[h]])
        P.op("pool", lambda e: e.memset(S16[:], 0.0), writes=[S16b])

    def hgrn_block(d, t, full, final):
        c0 = t * 512
        sl = slice(c0, c0 + 512)
        kh, khb = khr.next()
        for h in range(4):
            P.dma(kh[:, h, :, :], KH0.ap()[d, h, sl, :].rearrange("(j p) k -> p j k", p=128), writes=[khb])
        v, vb = vr.next()
        P.dma(v[:], V0.ap()[sl, :].rearrange("(j p) c -> p j c", p=128), writes=[vb])
        dc, dcb = dcr.next()
        P.dma(dc[:], DEC0.ap()[d, :, :, t * 8:(t + 1) * 8].rearrange("h p c -> p h c"), writes=[dcb])
        if full:
            qt, qtb = qtr.next()
            P.dma(qt[:], QT0.ap()[d, :, :, sl].rearrange("h p s -> p h s"), writes=[qtb])
            kt, ktb = ktr.next()
            P.dma(kt[:], KT0.ap()[d, :, :, sl].rearrange("h p s -> p h s"), writes=[ktb])
            pso = [psO[h].next() for h in range(4)]
        if final:
            of, ofb = ofr.next()
            P.dma(of[:], OFW.ap()[:, :, sl].rearrange("h p s -> p h s"), reads=[DOF], writes=[ofb])
            gt, gtb = gtr.next()
            P.dma(gt[:], GATE0.ap()[:, :, sl].rearrange("h p s -> p h s"), writes=[gtb])
        m, mb = masks[d]
        tiles = range(4) if d == 0 else range(3, -1, -1)
        for j in tiles:
            chunks = (0, 1) if d == 0 else (1, 0)
            if full:
                pS, pSb = psSr.next()

                def sc(e, pS=pS, j=j):
                    ins = None
                    for h in range(4):
                        ins = e.matmul(pS[:, h, :], lhsT=kt[:, h, j * 128:(j + 1) * 128], rhs=qt[:, h, j * 128:(j + 1) * 128],
                                       start=True, stop=True)
                    return ins

                P.op("pe", sc, reads=[ktb, qtb], writes=[pSb])
                sc32, sc32b = sc32r.next()
                P.op("act", lambda e, sc32=sc32, pS=pS: e.copy(out=sc32[:], in_=pS[:]), reads=[pSb], writes=[sc32b])
                at, atb = atr.next()
                P.op("pool", lambda e, at=at, sc32=sc32: e.tensor_tensor(out=at[:], in0=sc32[:],
                                                                       in1=m[:].unsqueeze(1).to_broadcast([128, 4, 128]), op=ALU.mult),
                     reads=[sc32b, mb], writes=[atb])
            for c in chunks:
                pr = slice(64 * c, 64 * c + 64)
                co = j * 128 + c * 64
                cidx = j * 2 + c
                if full:
                    for h in range(4):
                        po, pob = pso[h]

                        def om(e, at=at, po=po, h=h, co=co, c=c, j=j):
                            e.matmul(po[:, co:co + 64], lhsT=v[:, j, h * 128:(h + 1) * 128], rhs=at[:, h, c * 64:(c + 1) * 64],
                                     start=True, stop=False)
                            return e.matmul(po[:, co:co + 64], lhsT=S16[:, h, :], rhs=qt[:, h, co:co + 64], start=False, stop=True)

                        P.op("pe", om, reads=[vb, atb, S16b, qtb], writes=[pob])
                pX, pXb = psXr.next()

                def su(e, pX=pX, pr=pr, j=j):
                    ins = None
                    for h in range(4):
                        ins = e.matmul(pX[:, h, :], lhsT=kh[pr, h, j, :], rhs=v[pr, j, h * 128:(h + 1) * 128], start=True, stop=True)
                    return ins

                P.op("pe", su, reads=[khb, vb], writes=[pXb])
                for h in range(4):
                    P.op("dve", lambda e, pX=pX, h=h, cidx=cidx: e.scalar_tensor_tensor(
                        out=St[:, h, :], in0=St[:, h, :], scalar=dc[:, h, cidx:cidx + 1], in1=pX[:, h, :], op0=ALU.mult, op1=ALU.add),
                        reads=[Sb[h], dcb, pXb], writes=[Sb[h]])
                P.op("act", lambda e: e.copy(out=S16[:], in_=St[:]), reads=Sb, writes=[S16b])
        if full:
            for h in range(4):
                po, pob = pso[h]
                if not final:
                    o, ob = ev.next()
                    P.op("act", lambda e, o=o, po=po: e.copy(out=o[:], in_=po[:]), reads=[pob], writes=[ob])
                    P.dma(OFW.ap()[h, :, sl], o[:], reads=[ob], writes=[DOF], eng="pool")
                else:
                    tot, totb = ev.next()
                    P.op("dve", lambda e, tot=tot, po=po, h=h: e.tensor_tensor(out=tot[:], in0=po[:], in1=of[:, h, :], op=ALU.add),
                         reads=[pob, ofb], writes=[totb])
                    sq, sqb = e16.next()
                    P.op("act", lambda e, sq=sq, tot=tot: e.activation(out=sq[:], in_=tot[:], func=AF.Square), reads=[totb], writes=[sqb])
                    ss, ssb = ps_ss.next()
                    P.op("pe", lambda e, ss=ss, sq=sq: e.matmul(ss[:], lhsT=ones[:], rhs=sq[:], start=True, stop=True),
                         reads=[onesb, sqb], writes=[ssb])
                    rs, rsb = ev.next()
                    P.op("act", lambda e, rs=rs, ss=ss: e.activation(out=rs[:], in_=ss[:], func=AF.Sqrt, bias=epsB[:, 0:1], scale=1.0 / 128),
                         reads=[ssb, epsBb], writes=[rsb])
                    P.op("dve", lambda e, rs=rs: e.reciprocal(out=rs[:], in_=rs[:]), reads=[rsb], writes=[rsb])
                    P.op("pool", lambda e, tot=tot, rs=rs: e.tensor_tensor(out=tot[:], in0=tot[:], in1=rs[:], op=ALU.mult),
                         reads=[totb, rsb], writes=[totb])
                    o, ob = e16.next()
                    P.op("dve", lambda e, o=o, tot=tot, h=h: e.scalar_tensor_tensor(out=o[:], in0=tot[:], scalar=cols[:, 16 + h:17 + h],
                                                                                   in1=gt[:, h, :], op0=ALU.mult, op1=ALU.mult),
                         reads=[totb, colsb, gtb], writes=[ob])
                    P.dma(MIXT.ap()[h, :, sl], o[:], reads=[ob], writes=[DMX], eng="pool")

    reset_state()
    for t in range(NBO):
        hgrn_block(0, t, True, False)
    reset_state()
    for t in range(NBA - 1, NBO - 1, -1):
        hgrn_block(1, t, False, False)
    for t in range(NBO - 1, -1, -1):
        hgrn_block(1, t, True, True)
    P.emit()
    if upto == "B":
        return nc

    def attn_setup_common(P):
        C = dict(mx=P.buf("MIXOUT"))
        ones32 = P.sb("ones32", [128, 128], F32); o32b = P.buf("ones32")
        P.op("pool", lambda e: e.memset(ones32[:], 1.0), writes=[o32b])
        C.update(ones32=ones32, o32b=o32b, ss=Rot(P, "ss", 1, [128, 512], F32, psum=True),
                 r32=Rot(P, "r32", 4, [128, 512], F32), o16=Rot(P, "o16", 2, [128, 512], BF16))
        return C

    def mla_setup(P):
        C = attn_setup_common(P)
        C["KT"] = [(P.sb(f"K{i}", [128, S], BF16), P.buf(f"K{i}")) for i in range(2)]
        C["QT"] = [(P.sb(f"Q{i}", [128, SH], BF16), P.buf(f"Q{i}")) for i in range(2)]
        C["VT"] = (P.sb("V", [128, NT, 128], BF16), P.buf("V"))
        P.op("pool", lambda e: e.memset(C["KT"][1][0][:], 0.0), writes=[C["KT"][1][1]])
        P.op("pool", lambda e: e.memset(C["QT"][1][0][:], 0.0), writes=[C["QT"][1][1]])
        return C

    def den_matmul(P, C, accs):
        ss, ssb = C["ss"].next()

        def f(e):
            ins = None
            for i, (a_, _) in enumerate(accs):
                ins = e.matmul(ss[:], lhsT=C["ones32"][:], rhs=a_[:], start=(i == 0), stop=(i == len(accs) - 1))
            return ins

        P.op("pe", f, reads=[C["o32b"]] + [ab_ for _, ab_ in accs], writes=[ssb])
        return ss, ssb

    def mla_finish(P, C, hi, qb, po, acc, pd):
        ss, ssb = den_matmul(P, C, acc)
        r, rb = C["r32"].next()
        P.op("dve", lambda e: e.reciprocal(out=r[:], in_=ss[:]), reads=[ssb], writes=[rb])
        o_, ob_ = po[0]
        o16, o16b = C["o16"].next()
        P.op("dve", lambda e: e.tensor_tensor(out=o16[:], in0=o_[:], in1=r[:], op=ALU.mult), reads=[ob_, rb], writes=[o16b])
        P.dma(MIXT.ap()[4 + hi, :, qb * 512:(qb + 1) * 512], o16[:], reads=[o16b], writes=[C["mx"]], eng="pool")

    def mla_head(h):
        pb = 64 * (h % 2)

        def load(P, slot):
            C = mla_C[0]
            (k0, k0b), (k1, k1b) = C["KT"]
            (q0, q0b), (q1, q1b) = C["QT"]
            vt, vb = C["VT"]
            half = S // 2
            for lo, hi_ in ((0, half), (half, S)):
                P.dma(k0[:, lo:hi_], KN.ap()[h][:, lo:hi_], writes=[k0b])
                P.dma(k1[0:64, lo:hi_], KR.ap()[0:64, lo:hi_], writes=[k1b])
            P.dma(q0[:], QN.ap()[h], writes=[q0b])
            P.dma(q1[0:64, :], QR.ap()[h // 2, pb:pb + 64, :], writes=[q1b])
            vsrc = VBm.ap()[h].rearrange("(n p) d -> p n d", p=128)
            for n0 in range(0, NT, 8):
                n1 = min(NT, n0 + 8)
                P.dma(vt[:, n0:n1, :], vsrc[:, n0:n1, :], writes=[vb])
            return dict(kq_bufs=[k0b, k1b, q0b, q1b], v=(vt, vb), k0=k0, k1=k1, q0=q0, q1=q1, pb=pb)

        def qk(e, ctx, s_, c, kt, qb):
            pb_ = ctx["pb"]
            e.matmul(s_[:], lhsT=ctx["k0"][:, kt * 128:(kt + 1) * 128], rhs=ctx["q0"][:, qb * 512:(qb + 1) * 512], start=True, stop=False)
            return e.matmul(s_[:], lhsT=ctx["k1"][:, kt * 128:(kt + 1) * 128],
                            rhs=ctx["q1"][:, qb * 512:(qb + 1) * 512], start=False, stop=True)

        return dict(load=load, qk=qk)

    mla_C = [None]

    def mla_setup_wrap(P):
        mla_C[0] = mla_setup(P)
        return mla_C[0]

    attn_fm(nc, "pC", S, SH, [mla_head(h) for h in range(4)], 1, MLA_SCALE, 2, 3, 4, mla_setup_wrap, mla_finish)
    if upto == "C":
        return nc

    def outproj_phase(name, MIX, w_o, XIN, xin_is_input, XOUT):
        P = Phase(nc, name)
        wo, wob = load_w(P, "wo", w_o, 8, D)
        mxr = Rot(P, "mx", 2, [128, 8, 512], BF16)
        xr = Rot(P, "x", 3, [128, D], F32)
        psr = Rot(P, "ps", 4, [128, 512], F32, psum=True)
        XO = P.buf("XO")
        for t in range(NBO):
            sl = slice(t * 512, (t + 1) * 512)
            mx, mxb = mxr.next()
            P.dma(mx[:], MIX.ap()[:, :, sl].rearrange("c p s -> p c s"), writes=[mxb])
            for j in range(4):
                r0 = t * 512 + j * 128
                xt, xb = xr.next()
                P.dma(xt[:], XIN.ap()[r0:r0 + 128, :], writes=[xb])
                for n in range(2):
                    ps, psb = psr.next()
                    mm_group_tm(P, ps, psb, mx, mxb, slice(j * 128, (j + 1) * 128), 8, wo, wob, n * 512, 512)
                    P.op("dve", lambda e, xt=xt, ps=ps, n=n: e.tensor_tensor(out=xt[:, n * 512:(n + 1) * 512], in0=xt[:, n * 512:(n + 1) * 512],
                                                                           in1=ps[:], op=ALU.add), reads=[psb, xb], writes=[xb])
                P.dma(XOUT.ap()[r0:r0 + 128, :], xt[:], reads=[xb], writes=[XO], eng="pool")
        P.emit()

    def ffn_phase(name, l, XIN, XOUT, final):
        P = Phase(nc, name)
        ident, identb = make_ident(P)
        wg, wgb = load_w(P, "wg", ffn_g[l], 8, FH)
        wu, wub = load_w(P, "wu", ffn_u[l], 8, FH)
        wd, wdb = load_w(P, "wd", ffn_d[l], NHC, D)
        gains = {"ffn": (rows_in, (2 + l) * D)}
        NA = NormT(P, ident, identb, gains, nbuf=1)
        if final:
            gf = P.sb("gfin", [128, D], F32); gfb = P.buf("gfin")
            P.dma(gf[:], bcast_rows(rows_in, D, 4 * D), writes=[gfb])
        xs = [(P.sb(f"x{i}", [128, D], F32), P.buf(f"x{i}")) for i in range(5)]
        hTr = Rot(P, "hT", 1, [128, 8, 512], BF16)
        actr = Rot(P, "act", 1, [128, NHC, 512], BF16)
        psg = Rot(P, "psg", 3, [128, 512], F32, psum=True)
        psu = Rot(P, "psu", 2, [128, 512], F32, psum=True)
        psd = Rot(P, "psd", 2, [128, 512], F32, psum=True)
        sg = Rot(P, "sg", 2, [128, 512], F32)
        XO = P.buf("XO")
        for t in range(NBO):
            hT, hTb = hTr.next()
            xt_l = []
            for j in range(4):
                xt, xb = xs[(t * 4 + j) % 5]
                r0 = t * 512 + j * 128
                P.dma(xt[:], XIN.ap()[r0:r0 + 128, :], writes=[xb])
                NA.run(xt[:], xb, "ffn", hT, hTb, j * 128)
                xt_l.append((xt, xb))
            act, actb = actr.next()
            for hc in range(NHC):
                pg, pgb = psg.next()
                mm_group(P, pg, pgb, wg, wgb, 8, hc * 128, 128, hT, hTb, slice(0, 512), 512)
                pu, pub = psu.next()
                mm_group(P, pu, pub, wu, wub, 8, hc * 128, 128, hT, hTb, slice(0, 512), 512)
                s_, sb_ = sg.next()
                P.op("act", lambda e, s_=s_, pg=pg: e.activation(out=s_[:], in_=pg[:], func=AF.Silu), reads=[pgb], writes=[sb_])
                P.op("dve", lambda e, s_=s_, pu=pu, hc=hc, act=act: e.tensor_tensor(out=act[:, hc, :], in0=s_[:], in1=pu[:], op=ALU.mult),
                     reads=[sb_, pub], writes=[actb])
            for j in range(4):
                xt, xb = xt_l[j]
                r0 = t * 512 + j * 128
                for n in range(2):
                    pd, pdb = psd.next()
                    mm_group_tm(P, pd, pdb, act, actb, slice(j * 128, (j + 1) * 128), NHC, wd, wdb, n * 512, 512)
                    P.op("dve", lambda e, xt=xt, pd=pd, n=n: e.tensor_tensor(out=xt[:, n * 512:(n + 1) * 512], in0=xt[:, n * 512:(n + 1) * 512],
                                                                           in1=pd[:], op=ALU.add), reads=[pdb, xb], writes=[xb])
                if final:
                    st, stb = NA.rstd(xt[:], xb)
                    P.op("dve", lambda e, xt=xt, st=st: e.scalar_tensor_tensor(out=xt[:], in0=xt[:], scalar=st[:, 2:3], in1=gf[:],
                                                                             op0=ALU.mult, op1=ALU.mult), reads=[xb, stb, gfb], writes=[xb])
                P.dma(XOUT.ap()[r0:r0 + 128, :], xt[:], reads=[xb], writes=[XO], eng="pool")
        P.emit()

    outproj_phase("pD", MIXT, w_o0, x_in, True, X1A)
    if upto == "D":
        return nc
    ffn_phase("pE", 0, X1A, X1, False)
    if upto == "E":
        return nc

    P = Phase(nc, "pF")
    ident, identb = make_ident(P)
    wc, wcb = load_w(P, "wc", w_c, 8, 32 * 128 + D)
    NA = NormT(P, ident, identb, {"attn1": (rows_in, D)})
    xr = Rot(P, "x", 3, [128, D], F32)
    hTr = Rot(P, "hT", 2, [128, 8, 512], BF16)
    psr = Rot(P, "ps", 5, [128, 512], F32, psum=True)
    ps_tm = Rot(P, "ps_tm", 2, [128, 512], F32, psum=True)
    f32r = Rot(P, "f32", 6, [128, 512], F32)
    b16r = Rot(P, "b16", 4, [128, 512], BF16)
    v16 = Rot(P, "v16", 3, [128, D], BF16)
    rtab = Rot(P, "rtab", 4, [128, 512], F32)
    DQ = P.buf("Q1T"); DK = P.buf("K1S"); DV = P.buf("V1S"); DKG = P.buf("K1G"); DVG = P.buf("V1G")
    for t in range(NBO):
        c0 = t * 512
        sl = slice(c0, c0 + 512)
        hT, hTb = hTr.next()
        for j in range(4):
            xt, xb = xr.next()
            P.dma(xt[:], X1.ap()[c0 + j * 128:c0 + (j + 1) * 128, :], writes=[xb])
            NA.run(xt[:], xb, "attn1", hT, hTb, j * 128)
        ct, ctb = rtab.next(); st_, stb_ = rtab.next()
        P.dma(ct[:], ROPEC.ap()[:, sl], writes=[ctb])
        P.dma(st_[:], ROPES.ap()[:, sl], writes=[stb_])
        for j in range(4):
            v, vb = v16.next()
            for n in range(2):
                ps, psb = ps_tm.next()
                mm_group_tm(P, ps, psb, hT, hTb, slice(j * 128, (j + 1) * 128), 8, wc, wcb, 32 * 128 + n * 512, 512)
                P.op("act", lambda e, v=v, ps=ps, n=n: e.copy(out=v[:, n * 512:(n + 1) * 512], in_=ps[:]), reads=[psb], writes=[vb])
            r0 = c0 + j * 128
            P.dma(V1S[r0 // VR].ap()[r0 % VR:r0 % VR + 128, :], v[:], reads=[vb], writes=[DV], eng="pool")
        for which in range(2):
            for h in range(8):
                ps, psb = psr.next()
                mm_group(P, ps, psb, wc, wcb, 8, (which * 16 + h) * 128, 128, hT, hTb, slice(0, 512), 512)
                ps2, ps2b = psr.next()
                mm_group(P, ps2, ps2b, wc, wcb, 8, (which * 16 + 8 + h) * 128, 128, hT, hTb, slice(0, 512), 512)
                t1, t1b = f32r.next()
                P.op("dve", lambda e, t1=t1, ps=ps, ct=ct: e.tensor_tensor(out=t1[:], in0=ps[:], in1=ct[:], op=ALU.mult), reads=[psb, ctb], writes=[t1b])
                t2, t2b = f32r.next()
                P.op("dve", lambda e, t2=t2, ps2=ps2, st_=st_: e.tensor_tensor(out=t2[:], in0=ps2[:], in1=st_[:], op=ALU.mult), reads=[ps2b, stb_], writes=[t2b])
                o, ob = b16r.next()
                P.op("pool", lambda e, o=o, t1=t1, t2=t2: e.tensor_tensor(out=o[:], in0=t1[:], in1=t2[:], op=ALU.add), reads=[t1b, t2b], writes=[ob])
                if which == 0:
                    P.dma(Q1T.ap()[h, :, sl], o[:], reads=[ob], writes=[DQ], eng="pool")
                else:
                    P.dma(K1S[h].ap()[:, sl], o[:], reads=[ob], writes=[DK], eng="pool")
    groups = npair_groups
    for h in range(8):
        P.op("pool", lambda e, h=h: e.collective_compute("AllGather", ALU.bypass, replica_groups=groups, ins=[K1S[h].ap()],
                                                         outs=[K1G[h].ap()]), reads=[DK], writes=[DKG], dma=True, inc=1)
    for c in range(NVC):
        P.op("pool", lambda e, c=c: e.collective_compute("AllGather", ALU.bypass, replica_groups=groups, ins=[V1S[c].ap()],
                                                         outs=[V1G[c].ap()]), reads=[DV], writes=[DVG], dma=True, inc=1)
    P.emit()
    if upto == "F":
        return nc

    def diff_setup(P):
        C = attn_setup_common(P)
        C["KT"] = Rot(P, "K", 2, [128, 2, SH], BF16)
        C["Q0"] = Rot(P, "Qa", 2, [128, SH], BF16)
        C["Q1"] = Rot(P, "Qb", 2, [128, SH], BF16)
        for rot in (C["Q0"], C["Q1"]):
            for (t_, b_) in rot.items:
                P.op("pool", lambda e, t_=t_: e.memset(t_[:], 0.0), writes=[b_])
        C["VT"] = Rot(P, "V", 2, [128, NT, 128], BF16)
        cols = P.sb("cols", [128, 32], F32); colsb = P.buf("cols")
        P.dma(cols[:], cols_in.ap(), writes=[colsb])
        ones16 = P.sb("ones16", [128, 128], BF16); o16b_ = P.buf("ones16")
        P.op("pool", lambda e: e.memset(ones16[:], 1.0), writes=[o16b_])
        lv = P.sb("lv", [128, 256], F32); lvb = P.buf("lv")
        P.dma(lv[:], bcast_rows(rows_in, 256, 6 * D), writes=[lvb])
        lam = P.sb("lam", [128, 8], F32); lamb = P.buf("lam")
        pr = P.sb("pr", [128, 128], F32); prb = P.buf("pr")
        P.op("dve", lambda e: e.tensor_tensor(out=pr[:, 0:64], in0=lv[:, 0:64], in1=lv[:, 64:128], op=ALU.mult), reads=[lvb], writes=[prb])
        P.op("dve", lambda e: e.tensor_tensor(out=pr[:, 64:128], in0=lv[:, 128:192], in1=lv[:, 192:256], op=ALU.mult), reads=[lvb], writes=[prb])
        P.op("dve", lambda e: e.reduce_sum(out=lam[:, 0:2], in_=pr[:].rearrange("p (a b) -> p a b", b=64), axis=AX.X), reads=[prb], writes=[lamb])
        P.op("act", lambda e: e.activation(out=lam[:, 2:4], in_=lam[:, 0:2], func=AF.Exp), reads=[lamb], writes=[lamb])
        P.op("dve", lambda e: e.tensor_tensor(out=lam[:, 4:5], in0=lam[:, 2:3], in1=lam[:, 3:4], op=ALU.subtract), reads=[lamb], writes=[lamb])
        P.op("dve", lambda e: e.tensor_scalar(out=lam[:, 5:6], in0=lam[:, 4:5], scalar1=LAMBDA_INIT, scalar2=-1.0, op0=ALU.add, op1=ALU.mult),
             reads=[lamb], writes=[lamb])
        epsd = P.sb("epsd", [128, 1], F32); epsdb = P.buf("epsd")
        P.op("pool", lambda e: e.memset(epsd[:], EPS / (1.0 - LAMBDA_INIT) ** 2), writes=[epsdb])
        C.update(lam=lam, lamb=lamb, eps=epsd, epsb=epsdb, cols=cols, colsb=colsb, ones16=ones16, ones16b=o16b_,
                 sq16=Rot(P, "sq16", 2, [128, 512], BF16))
        return C

    def diff_finish(P, C, hi, qb, po, acc, pd):
        lam, lamb = C["lam"], C["lamb"]
        d0, d0b = den_matmul(P, C, [acc[0]])
        d1, d1b = pd[1]
        r0, r0b = C["r32"].next()
        P.op("dve", lambda e: e.reciprocal(out=r0[:], in_=d0[:]), reads=[d0b], writes=[r0b])
        r1, r1b = C["r32"].next()
        P.op("dve", lambda e: e.reciprocal(out=r1[:], in_=d1[:]), reads=[d1b], writes=[r1b])
        (o0, o0b), (o1, o1b) = po
        a, ab = C["r32"].next()
        P.op("dve", lambda e: e.tensor_tensor(out=a[:], in0=o0[:], in1=r0[:], op=ALU.mult), reads=[o0b, r0b], writes=[ab])
        t, tb = C["r32"].next()
        P.op("dve", lambda e: e.tensor_tensor(out=t[:], in0=o1[:], in1=r1[:], op=ALU.mult), reads=[o1b, r1b], writes=[tb])
        P.op("dve", lambda e: e.scalar_tensor_tensor(out=a[:], in0=t[:], scalar=lam[:, 5:6], in1=a[:], op0=ALU.mult, op1=ALU.add),
             reads=[ab, tb, lamb], writes=[ab])
        sq, sqb = C["sq16"].next()
        P.op("act", lambda e: e.activation(out=sq[:], in_=a[:], func=AF.Square), reads=[ab], writes=[sqb])
        ss, ssb = C["ss"].next()
        P.op("pe", lambda e: e.matmul(ss[:], lhsT=C["ones16"][:], rhs=sq[:], start=True, stop=True), reads=[C["ones16b"], sqb], writes=[ssb])
        rs, rsb = r0, r0b
        P.op("act", lambda e: e.activation(out=rs[:], in_=ss[:], func=AF.Sqrt, bias=C["eps"][:, 0:1],
                                           scale=1.0 / (128 * (1.0 - LAMBDA_INIT) ** 2)), reads=[ssb, C["epsb"]], writes=[rsb])
        P.op("dve", lambda e: e.reciprocal(out=rs[:], in_=rs[:]), reads=[rsb], writes=[rsb])
        o16, o16b = C["o16"].next()
        P.op("dve", lambda e: e.scalar_tensor_tensor(out=o16[:], in0=a[:], scalar=C["cols"][:, 25:26], in1=rs[:], op0=ALU.mult, op1=ALU.mult),
             reads=[ab, rsb, C["colsb"]], writes=[o16b])
        P.dma(MIXT1.ap()[hi, :, qb * 512:(qb + 1) * 512], o16[:], reads=[o16b], writes=[C["mx"]], eng="pool")

    def diff_head(h):
        def load(P, slot):
            C = diff_C[0]
            kt, ktb = C["KT"].next()
            kap = K1G[h].ap().rearrange("(r p) s -> p r s", r=2)
            for r in range(2):
                P.dma(kt[:, r, :], kap[:, r, :], writes=[ktb])
            q0, q0b = C["Q0"].next()
            q1, q1b = C["Q1"].next()
            P.dma(q0[0:64, :], Q1T.ap()[h][0:64, :], writes=[q0b])
            P.dma(q1[64:128, :], Q1T.ap()[h][64:128, :], writes=[q1b])
            vt, vb = C["VT"].next()
            for r in range(2):
                for c in range(NVC):
                    n0 = r * (SH // 128) + c * (VR // 128)
                    vap = V1G[c].ap()[r * VR:(r + 1) * VR, h * 128:(h + 1) * 128]
                    P.dma(vt[:, n0:n0 + VR // 128, :], vap.rearrange("(n p) d -> p n d", p=128), writes=[vb])
            return dict(kq_bufs=[ktb, q0b, q1b], v=(vt, vb), ktf=kt[:].rearrange("p r s -> p (r s)"), q=(q0, q1))

        def qk(e, ctx, s_, c, kt, qb):
            return e.matmul(s_[:], lhsT=ctx["ktf"][:, kt * 128:(kt + 1) * 128], rhs=ctx["q"][c][:, qb * 512:(qb + 1) * 512],
                            start=True, stop=True)

        return dict(load=load, qk=qk)

    diff_C = [None]

    def diff_setup_wrap(P):
        diff_C[0] = diff_setup(P)
        return diff_C[0]

    attn_fm(nc, "pG", S, SH, [diff_head(h) for h in range(8)], 2, DIFF_SCALE, 1, 4, 6, diff_setup_wrap, diff_finish, den_pe=(1,))
    if upto == "G":
        return nc

    outproj_phase("pH", MIXT1, w_o1, X1, False, X2A)
    ffn_phase("pI", 1, X2A, out, True)
    return nc


def _swap_halves(w, width):
    n = w.shape[1] // width
    w4 = w.reshape(w.shape[0], n, 2, width // 2)
    return np.ascontiguousarray(w4[:, :, ::-1, :]).reshape(w.shape[0], n * width)


def prep_inputs(S, ncores, x, norm_attn, norm_ffn, ffn_w_gate, ffn_w_up, ffn_w_down, ab_w_in, hgrn_lower_bound,
                hgrn_out_norm, mla_q_norm, mla_w_uq, mla_kv_norm, mla_w_ukv, ab_w_out, c_w_in,
                diff_lambda_q1, diff_lambda_k1, diff_lambda_q2, diff_lambda_k2, diff_out_norm, c_w_out, final_norm):
    f32 = np.float32
    w_in = np.asarray(ab_w_in[0], f32)
    sp = np.cumsum([0, 512, 512, 512, 512, 512, 384, 256, 64])
    Wq, Wffw, Wfbw, Wi, Wg, Wcq, Wckv, Wkr = [w_in[:, sp[i]:sp[i + 1]] for i in range(8)]
    Wkr_sw = _swap_halves(Wkr, 64)
    uq = np.asarray(mla_w_uq[0], f32).reshape(384, 4, 192)
    uq_n = uq[:, :, :128].reshape(384, 512)
    uq_r = uq[:, :, 128:].reshape(384, 256)
    w_uq = np.concatenate([uq_n, uq_r, _swap_halves(uq_r, 64)], 1)
    ukv = np.asarray(mla_w_ukv[0], f32).reshape(256, 4, 256)
    w_ukv = np.concatenate([ukv[:, :, :128].reshape(256, 512), ukv[:, :, 128:].reshape(256, 512)], 1)
    cw = np.asarray(c_w_in[0], f32)
    cq, ck, cv = cw[:, :1024], cw[:, 1024:2048], cw[:, 2048:]
    w_c = np.concatenate([cq, _swap_halves(cq, 64), ck, _swap_halves(ck, 64), cv], 1)
    inv = (1.0 / (10000.0 ** (np.arange(0, 64, 2, dtype=f32) / f32(64)))).astype(f32)
    p = np.arange(128)
    sign = np.where((p % 64) < 32, -1.0, 1.0).astype(f32)
    rows = np.zeros((8, D), f32)
    rows[0] = norm_attn[0]; rows[1] = norm_attn[1]; rows[2] = norm_ffn[0]; rows[3] = norm_ffn[1]; rows[4] = final_norm
    rows[5, :128] = diff_out_norm[0]
    rows[6, 0:64] = diff_lambda_q1[0]; rows[6, 64:128] = diff_lambda_k1[0]
    rows[6, 128:192] = diff_lambda_q2[0]; rows[6, 192:256] = diff_lambda_k2[0]
    lbraw = np.asarray(hgrn_lower_bound, f32)
    shared = dict(w_uq=w_uq, w_ukv=w_ukv, w_o0=np.asarray(ab_w_out[0], f32), w_c=w_c, w_o1=np.asarray(c_w_out[0], f32),
                  rows=rows)
    for l in range(2):
        shared[f"ffn_g{l}"] = np.asarray(ffn_w_gate[l], f32)
        shared[f"ffn_u{l}"] = np.asarray(ffn_w_up[l], f32)
        shared[f"ffn_d{l}"] = np.asarray(ffn_w_down[l], f32)
    per_r = []
    for r in range(2):
        fd = (Wffw, Wfbw) if r == 0 else (Wfbw, Wffw)
        w_a = np.concatenate([Wq, fd[0], fd[1], Wg, Wcq, Wckv, Wkr, Wkr, Wkr_sw, Wkr_sw, Wi], 1)
        cols = np.zeros((128, 32), f32)
        for d in range(2):
            od = d if r == 0 else 1 - d
            cols[:, d * 4:(d + 1) * 4] = lbraw[od, 0].reshape(4, 128).T
            cols[:, 8 + d * 4:8 + (d + 1) * 4] = lbraw[od, 1].reshape(4, 128).T
        cols[:, 16:20] = np.asarray(hgrn_out_norm[0], f32).reshape(4, 128).T
        cols[:, 20:23] = np.asarray(mla_q_norm[0], f32).reshape(3, 128).T
        cols[:, 23:25] = np.asarray(mla_kv_norm[0], f32).reshape(2, 128).T
        cols[:, 25] = np.asarray(diff_out_norm[0], f32)
        posv = np.arange(S, dtype=f32) if r == 0 else (S - 1 - np.arange(S)).astype(f32)
        ang = (posv[None, :] * inv[p % 32][:, None]).astype(f32)
        per_r.append(dict(w_a=np.ascontiguousarray(w_a), cols=cols, ropec_t=np.cos(ang).astype(f32),
                          ropes_t=(np.sin(ang) * sign[:, None]).astype(f32)))
    in_maps = []
    for c in range(ncores):
        b, r = c // 2, c % 2
        xb = np.asarray(x[b], f32)
        if r == 1:
            xb = xb[::-1]
        m = dict(shared)
        m.update(per_r[r])
        m["x"] = np.ascontiguousarray(xb)
        in_maps.append(m)
    return in_maps


_CACHE = {}


def run(S, ncores, inputs, debug=False, upto=None):
    key = (S, ncores, debug, upto)
    if key not in _CACHE:
        groups = [[2 * i, 2 * i + 1] for i in range(ncores // 2)]
        _CACHE[key] = build(S, groups, debug, upto)
    nc = _CACHE[key]
    in_maps = prep_inputs(S, ncores, **inputs)
    res = run_bass_kernel_spmd(nc, in_maps, core_ids=list(range(ncores)))
    B = ncores // 2
    SH = S // 2
    out = np.zeros((B, S, D), np.float32)
    for c in range(ncores):
        b, r = c // 2, c % 2
        o = res.results[c]["out"]
        if r == 0:
            out[b, :SH] = o
        else:
            out[b, SH:] = o[::-1]
    return out, res


def kernel(**inputs):
    out, _ = run(8192, 8, inputs)
    return out
```

```python
import math
import os
import numpy as np
from contextlib import ExitStack
import concourse.bass as bass
import concourse.mybir as mybir
from concourse.bass_utils import run_bass_kernel_spmd

F32 = mybir.dt.float32
BF16 = mybir.dt.bfloat16
ALU = mybir.AluOpType
AF = mybir.ActivationFunctionType
AX = mybir.AxisListType

D = 1024
FH = 2816
NHC = FH // 128
EPS = 1e-6
MLA_SCALE = 192 ** -0.5
DIFF_SCALE = 64 ** -0.5
LAMBDA_INIT = 0.8 - 0.6 * math.exp(-0.3 * 1)


class Buf:
    __slots__ = ("name", "last_w", "readers", "sem", "dma_total", "ent", "kind")

    def __init__(self, name):
        self.name = name
        self.last_w = None
        self.readers = []
        self.sem = None
        self.dma_total = 0


class Op:
    __slots__ = ("eng", "fn", "is_dma", "signal", "sigval", "waits", "dwaits", "dbuf", "inc")

    def __init__(self, eng, fn, is_dma):
        self.eng = eng
        self.fn = fn
        self.is_dma = is_dma
        self.signal = False
        self.sigval = 0
        self.waits = []
        self.dwaits = {}
        self.dbuf = None
        self.inc = 16


ENGS = ("pe", "act", "dve", "pool", "sp")
BLOCKNAME = {"pe": "tensor", "act": "scalar", "dve": "vector", "pool": "gpsimd", "sp": "sync"}


_SEMPOOL = [None]


class SemPool:
    def __init__(self, nc, es, n_dma=64):
        self.eng = {e: [es.enter_context(nc.semaphore(f"s_{e}")), 0] for e in ("pe", "act", "dve", "pool")}
        self.dma = {"sp": [[es.enter_context(nc.semaphore(f"dh{i}")), 0] for i in range(44)],
                    "pool": [[es.enter_context(nc.semaphore(f"ds{i}")), 0] for i in range(40)],
                    "cc": [[es.enter_context(nc.semaphore(f"dc{i}")), 0] for i in range(4)]}


class Phase:
    def __init__(self, nc, name, n_dma_sems=60):
        self.nc = nc
        self.name = name
        self.ops = {e: [] for e in ENGS}
        self.es = ExitStack()
        self.pool = _SEMPOOL[0]
        self.esem = {e: self.pool.eng[e][0] for e in ("pe", "act", "dve", "pool")}
        self.free_dma = {k: list(v) for k, v in self.pool.dma.items()}
        self.bufs = []
        self.nbuf = 0

    def sb(self, name, shape, dtype):
        return self.es.enter_context(self.nc.sbuf_tensor(f"{self.name}_{name}", list(shape), dtype))

    def ps(self, name, shape, dtype=F32):
        return self.es.enter_context(self.nc.psum_tensor(f"{self.name}_{name}", list(shape), dtype))

    def buf(self, name=None):
        self.nbuf += 1
        b = Buf(name or f"b{self.nbuf}")
        self.bufs.append(b)
        return b

    def _dep(self, o, w, kind):
        if w is o:
            return
        if w.is_dma:
            b = w.dbuf
            o.dwaits[b] = max(o.dwaits.get(b, 0), b.dma_total)
            return
        if w.eng == o.eng and not o.is_dma:
            if o.eng == "pe":
                return
            if kind != "raw":
                return
        w.signal = True
        o.waits.append(w)

    def op(self, eng, fn, reads=(), writes=(), dma=False, inc=16):
        o = Op(eng, fn, dma)
        o.inc = inc
        for b in reads:
            if b.last_w is not None:
                self._dep(o, b.last_w, "raw")
        for b in writes:
            if b.last_w is not None:
                self._dep(o, b.last_w, "waw")
            for r in b.readers:
                self._dep(o, r, "war")
        for b in reads:
            b.readers.append(o)
        for b in writes:
            b.last_w = o
            b.readers = []
        if dma:
            d = writes[0]
            kind = "cc" if inc != 16 else eng
            if d.sem is None:
                ent = self.free_dma[kind].pop()
                d.sem = ent[0]
                d.dma_total = ent[1]
                d.ent = ent
                d.kind = kind
            assert d.kind == kind, (d.name, d.kind, kind)
            d.dma_total += inc
            d.ent[1] = d.dma_total
            o.dbuf = d
        self.ops[eng].append(o)
        return o

    def dma(self, out, in_, reads=(), writes=(), eng="sp", **kw):
        return self.op(eng, lambda e: e.dma_start(out=out, in_=in_, **kw), reads=reads, writes=writes, dma=True)

    def emit(self):
        nc = self.nc
        fin_d = {}
        for b in self.bufs:
            if b.sem is not None:
                fin_d[b] = b.dma_total
        lasts = []
        for e in ("pe", "act", "dve", "pool"):
            cands = [o for o in self.ops[e] if not o.is_dma]
            if cands:
                cands[-1].signal = True
                lasts.append(cands[-1])
        for e in ENGS:
            fin = Op(e, None, False)
            fin.dwaits = dict(fin_d)
            fin.waits = list(lasts)
            self.ops[e].append(fin)
        for e in ("pe", "act", "dve", "pool"):
            n = self.pool.eng[e][1]
            for o in self.ops[e]:
                if o.signal and not o.is_dma:
                    n += 1
                    o.sigval = n
            self.pool.eng[e][1] = n
        with nc.Block() as block:
            for e in ENGS:

                def body(eng, e=e):
                    seen = {}
                    for o in self.ops[e]:
                        req = {}
                        for w in o.waits:
                            s = self.esem[w.eng]
                            if req.get(s, 0) < w.sigval:
                                req[s] = w.sigval
                        for b, tot in o.dwaits.items():
                            if req.get(b.sem, 0) < tot:
                                req[b.sem] = tot
                        for s, v in req.items():
                            if seen.get(s, 0) < v:
                                eng.wait_ge(s, v)
                                seen[s] = v
                        if o.fn is None:
                            continue
                        ins = o.fn(eng)
                        if o.is_dma:
                            ins.then_inc(o.dbuf.sem, o.inc)
                        elif o.signal:
                            ins.then_inc(self.esem[e], 1)

                getattr(block, BLOCKNAME[e])(body)
        self.es.close()


class Rot:
    def __init__(self, P, name, n, shape, dtype, psum=False):
        self.items = []
        for i in range(n):
            t = P.ps(f"{name}{i}", shape, dtype) if psum else P.sb(f"{name}{i}", shape, dtype)
            self.items.append((t, P.buf(f"{name}{i}")))
        self.i = 0

    def next(self):
        it = self.items[self.i % len(self.items)]
        self.i += 1
        return it


def bcast_rows(handle, n, offset=0, parts=128):
    return bass.AP(handle, offset, [[0, parts], [1, n]])


def make_ident(P, name="ident"):
    ident = P.sb(name, [128, 128], BF16)
    b = P.buf(name)

    P.op("pool", lambda e: e.memset(ident[:], 1.0), writes=[b])
    P.op("pool", lambda e: e.affine_select(out=ident[:], in_=ident[:], pattern=[[-1, 128]], compare_op=ALU.is_ge, fill=0.0,
                                           base=0, channel_multiplier=1), reads=[b], writes=[b])
    P.op("pool", lambda e: e.affine_select(out=ident[:], in_=ident[:], pattern=[[1, 128]], compare_op=ALU.is_ge, fill=0.0,
                                           base=0, channel_multiplier=-1), reads=[b], writes=[b])
    return ident, b


def eps_tile(P):
    t = P.sb("epsc", [128, 1], F32)
    b = P.buf("epsc")
    P.op("pool", lambda e: e.memset(t[:], EPS), writes=[b])
    return t, b


class NormT:
    def __init__(self, P, ident, ident_b, gains, nbuf=2):
        self.P = P
        self.ident, self.ident_b = ident, ident_b
        self.g = {}
        for k, (h, off) in gains.items():
            t = P.sb(f"gain_{k}", [128, D], F32)
            b = P.buf(f"gain_{k}")
            P.dma(t[:], bcast_rows(h, D, off), writes=[b])
            self.g[k] = (t, b)
        self.sq = Rot(P, "nt_sq", nbuf, [128, D], BF16)
        self.st = Rot(P, "nt_st", 4, [128, 4], F32)
        self.hb = Rot(P, "nt_hb", nbuf, [128, D], BF16)
        self.pt = Rot(P, "nt_pt", 1, [128, 8, 128], BF16, psum=True)
        self.eps, self.epsb = eps_tile(P)

    def rstd(self, xt, xb, width=D):
        P = self.P
        sq, sqb = self.sq.next()
        st, stb = self.st.next()
        P.op("act", lambda e: e.activation(out=sq[:, 0:width], in_=xt, func=AF.Square, accum_out=st[:, 0:1]),
             reads=[xb], writes=[sqb, stb])
        P.op("act", lambda e: e.activation(out=st[:, 1:2], in_=st[:, 0:1], func=AF.Sqrt, bias=self.eps[:, 0:1], scale=1.0 / width),
             reads=[stb, self.epsb], writes=[stb])
        P.op("dve", lambda e: e.reciprocal(out=st[:, 2:3], in_=st[:, 1:2]), reads=[stb], writes=[stb])
        return st, stb

    def run(self, xt, xb, gain, hT, hTb, col0):
        P = self.P
        gt, gb = self.g[gain]
        st, stb = self.rstd(xt, xb)
        hb, hbb = self.hb.next()
        P.op("dve", lambda e: e.scalar_tensor_tensor(out=hb[:], in0=xt, scalar=st[:, 2:3], in1=gt[:],
                                                     op0=ALU.mult, op1=ALU.mult), reads=[xb, stb, gb], writes=[hbb])
        pt, ptb = self.pt.next()

        def tr(e):
            ins = None
            for c in range(8):
                ins = e.transpose(out=pt[:, c, :], in_=hb[:, c * 128:(c + 1) * 128], identity=self.ident[:])
            return ins

        P.op("pe", tr, reads=[hbb, self.ident_b], writes=[ptb])
        P.op("act", lambda e: e.copy(out=hT[:, :, col0:col0 + 128], in_=pt[:]), reads=[ptb], writes=[hTb])


WCOLS = 1024


def load_w(P, name, handle, kchunks, ncols):
    t = P.sb(name, [128, kchunks, ncols], BF16)
    b = P.buf(name)
    src = handle.ap().rearrange("(c p) n -> p c n", p=128)
    for c0 in range(kchunks):
        for n0 in range(0, ncols, WCOLS):
            n1 = min(ncols, n0 + WCOLS)
            P.dma(t[:, c0, n0:n1], src[:, c0, n0:n1], writes=[b], eng="pool")
    return t, b


def mm_group(P, ps, psb, w, wb, kch, wc0, m, rhs, rhsb, rhs_slice, n):
    def f(e):
        ins = None
        for k in range(kch):
            ins = e.matmul(ps[0:m, 0:n], lhsT=w[:, k, wc0:wc0 + m], rhs=rhs[:, k, rhs_slice], start=(k == 0),
                           stop=(k == kch - 1))
        return ins

    P.op("pe", f, reads=[wb, rhsb], writes=[psb])


def mm_group_tm(P, ps, psb, lhs, lhsb, lhs_slice, kch, w, wb, wc0, n):
    def f(e):
        ins = None
        for k in range(kch):
            ins = e.matmul(ps[:, 0:n], lhsT=lhs[:, k, lhs_slice], rhs=w[:, k, wc0:wc0 + n], start=(k == 0),
                           stop=(k == kch - 1))
        return ins

    P.op("pe", f, reads=[lhsb, wb], writes=[psb])


LOOK = 2


def run_pipelined(iters, look):
    n = len(iters)
    for i in range(n + look):
        if i < n:
            iters[i][0]()
        if i - look >= 0:
            it = iters[i - look]
            it[1]()
            if it[2] is not None:
                it[2]()


def attn_fm(nc, name, S, SH, heads, ncomp, scale, look, n_s, n_p, setup, finish, den_pe=()):
    P = Phase(nc, name)
    NT = S // 128
    nq = 512
    NQB = SH // nq
    C = setup(P)
    ps_s = Rot(P, "ps_s", n_s, [128, 512], F32, psum=True)
    ps_o = [Rot(P, f"ps_o{c}", (2 if ncomp == 1 else 1), [128, 512], F32, psum=True) for c in range(ncomp)]
    pT = Rot(P, "pT", n_p, [128, 512], BF16)
    nacc = 2
    accr = [Rot(P, f"acc{i}", 2, [128, 512], F32) for i in range(nacc)]
    pdr = {c: Rot(P, f"pd{c}", 1, [128, 512], F32, psum=True) for c in den_pe}
    for hi, head in enumerate(heads):
        ctx = head["load"](P, hi % 2)
        iters = []
        for qb in range(NQB):
            po = [ps_o[c].next() for c in range(ncomp)]
            acc = [accr[i].next() for i in range(nacc)]
            pd = {c: pdr[c].next() for c in den_pe}
            for kt in range(NT):
                sl = [ps_s.next() for c in range(ncomp)]
                pl = [pT.next() for c in range(ncomp)]

                def rec_qk(sl=sl, pl=pl, kt=kt, qb=qb, ctx=ctx, head=head):
                    for c in range(ncomp):
                        s_, sb_ = sl[c]
                        P.op("pe", lambda e, s_=s_, c=c: head["qk"](e, ctx, s_, c, kt, qb), reads=ctx["kq_bufs"], writes=[sb_])
                    for c in range(ncomp):
                        s_, sb_ = sl[c]
                        p_, pb_ = pl[c]
                        P.op("act", lambda e, s_=s_, p_=p_: e.activation(out=p_[:], in_=s_[:], func=AF.Exp, scale=scale),
                             reads=[sb_], writes=[pb_])

                def rec_pv(pl=pl, kt=kt, po=po, acc=acc, ctx=ctx, pd=pd):
                    vt, vb = ctx["v"]
                    for c in range(ncomp):
                        p_, pb_ = pl[c]
                        o_, ob_ = po[c]
                        P.op("pe", lambda e, p_=p_, o_=o_: e.matmul(o_[:], lhsT=vt[:, kt, :], rhs=p_[:], start=(kt == 0),
                                                                     stop=(kt == NT - 1)), reads=[pb_, vb], writes=[ob_])
                    for c in den_pe:
                        p_, pb_ = pl[c]
                        d_, db_ = pd[c]
                        P.op("pe", lambda e, p_=p_, d_=d_: e.matmul(d_[:], lhsT=C["ones16"][:], rhs=p_[:], start=(kt == 0),
                                                                     stop=(kt == NT - 1)), reads=[pb_, C["ones16b"]], writes=[db_])
                    for c in range(ncomp):
                        if c in den_pe:
                            continue
                        p_, pb_ = pl[c]
                        if ncomp == 2:
                            ai, first = c, (kt == 0)
                        else:
                            ai, first = 0, (kt == 0)
                        a_, ab_ = acc[ai]
                        eng = "dve" if ai == 0 else "pool"
                        if first:
                            P.op(eng, lambda e, a_=a_, p_=p_: e.tensor_copy(out=a_[:], in_=p_[:]), reads=[pb_], writes=[ab_])
                        else:
                            P.op(eng, lambda e, a_=a_, p_=p_: e.tensor_tensor(out=a_[:], in0=a_[:], in1=p_[:], op=ALU.add),
                                 reads=[pb_, ab_], writes=[ab_])

                post = None
                if kt == NT - 1:
                    post = (lambda hi=hi, qb=qb, po=po, acc=acc, pd=pd: finish(P, C, hi, qb, po, acc, pd))
                iters.append((rec_qk, rec_pv, post))
        run_pipelined(iters, look)
    P.emit()


def build(S, npair_groups, debug=False, upto=None):
    SH = S // 2
    NBA = S // 512
    NBO = SH // 512
    NT = S // 128
    NCH = S // 64
    nc = bass.Bass("TRN2", target_bir_lowering=False)
    _es = ExitStack()
    _SEMPOOL[0] = SemPool(nc, _es)

    def din(name, shape):
        return nc.dram_tensor(name, list(shape), F32, kind="ExternalInput")

    def scr(name, shape, dt, dbg=False):
        if dbg and debug:
            return nc.dram_tensor(name, list(shape), dt, kind="ExternalOutput")
        return nc.dram_tensor(name, list(shape), dt)

    x_in = din("x", [S, D])
    w_a = din("w_a", [D, 23 * 128 + 512])
    w_uq = din("w_uq", [384, 1024])
    w_ukv = din("w_ukv", [256, 1024])
    w_o0 = din("w_o0", [D, D])
    w_c = din("w_c", [D, 16 * 128 + 16 * 128 + D])
    w_o1 = din("w_o1", [D, D])
    ffn_g = [din(f"ffn_g{l}", [D, FH]) for l in range(2)]
    ffn_u = [din(f"ffn_u{l}", [D, FH]) for l in range(2)]
    ffn_d = [din(f"ffn_d{l}", [FH, D]) for l in range(2)]
    cols_in = din("cols", [128, 32])
    rows_in = din("rows", [8, D])
    out = nc.dram_tensor("out", [SH, D], F32, kind="ExternalOutput")

    ROPEC = din("ropec_t", [128, S])
    ROPES = din("ropes_t", [128, S])
    QT0 = scr("QT0", [2, 4, 128, SH], BF16)
    KT0 = scr("KT0", [2, 4, 128, SH], BF16)
    KH0 = scr("KH0", [2, 4, S, 128], BF16)
    DEC0 = scr("DEC0", [2, 4, 128, NCH], F32)
    V0 = scr("V0", [S, 512], BF16)
    GATE0 = scr("GATE0", [4, 128, SH], F32)
    OFW = scr("OFW", [4, 128, SH], F32)
    QN = scr("QN", [4, 128, SH], BF16)
    QR = scr("QR", [2, 128, SH], BF16)
    KN = scr("KN", [4, 128, S], BF16)
    KR = scr("KR", [128, S], BF16)
    VBm = scr("VBm", [4, S, 128], BF16)
    MIXT = scr("MIXT", [8, 128, SH], BF16, dbg=True)
    X1A = scr("X1A", [SH, D], F32)
    X1 = scr("X1", [SH, D], F32, dbg=True)
    Q1T = scr("Q1T", [8, 128, SH], BF16)
    VR = min(512, SH)
    NVC = SH // VR
    K1S = [scr(f"K1S{h}", [128, SH], BF16) for h in range(8)]
    V1S = [scr(f"V1S{c}", [VR, D], BF16) for c in range(NVC)]
    K1G = [scr(f"K1G{h}", [2 * 128, SH], BF16) for h in range(8)]
    V1G = [scr(f"V1G{c}", [2 * VR, D], BF16) for c in range(NVC)]
    MIXT1 = scr("MIXT1", [8, 128, SH], BF16, dbg=True)
    X2A = scr("X2A", [SH, D], F32)

    P = Phase(nc, "pA")
    ident, identb = make_ident(P)
    wa, wab = load_w(P, "wa", w_a, 8, 23 * 128 + 512)
    wuq, wuqb = load_w(P, "wuq", w_uq, 3, 1024)
    wukv, wukvb = load_w(P, "wukv", w_ukv, 2, 1024)
    cols = P.sb("cols", [128, 32], F32); colsb = P.buf("cols")
    P.dma(cols[:], cols_in.ap(), writes=[colsb])
    lbt = P.sb("lbt", [128, 32], F32); lbb = P.buf("lbt")
    P.op("dve", lambda e: e.tensor_tensor(out=lbt[:, 0:8], in0=cols[:, 0:8], in1=cols[:, 8:16], op=ALU.subtract),
         reads=[colsb], writes=[lbb])
    P.op("act", lambda e: e.activation(out=lbt[:, 8:16], in_=lbt[:, 0:8], func=AF.Sigmoid), reads=[lbb], writes=[lbb])
    P.op("dve", lambda e: e.tensor_scalar(out=lbt[:, 16:24], in0=lbt[:, 8:16], scalar1=-1.0, scalar2=1.0, op0=ALU.mult,
                                          op1=ALU.add), reads=[lbb], writes=[lbb])
    P.op("dve", lambda e: e.tensor_scalar(out=lbt[:, 24:32], in0=lbt[:, 16:24], scalar1=-1.0, scalar2=None, op0=ALU.mult),
         reads=[lbb], writes=[lbb])
    ones = P.sb("ones", [128, 128], BF16); onesb = P.buf("ones")
    P.op("pool", lambda e: e.memset(ones[:], 1.0), writes=[onesb])
    rmask = P.sb("rmask", [128, 8, 64], F32); rmb = P.buf("rmask")

    P.op("pool", lambda e: e.memset(rmask[:], 1.0), writes=[rmb])
    P.op("pool", lambda e: e.memset(rmask[:, :, 0:1], 0.0), reads=[rmb], writes=[rmb])
    NA = NormT(P, ident, identb, {"attn0": (rows_in, 0)})
    xr = Rot(P, "x", 3, [128, D], F32)
    hTr = Rot(P, "hT", 2, [128, 8, 512], BF16)
    psr = Rot(P, "ps", 4, [128, 512], F32, psum=True)
    ps_ss = Rot(P, "ps_ss", 1, [128, 512], F32, psum=True)
    ps_tm = Rot(P, "ps_tm", 1, [128, 512], F32, psum=True)
    ps_kt = Rot(P, "ps_kt", 1, [128, 4, 128], BF16, psum=True)
    f32r = Rot(P, "f32", 12, [128, 512], F32)
    b16r = Rot(P, "b16", 8, [128, 512], BF16)
    sgr = Rot(P, "sgr", 8, [128, 512], F32)
    qsr = Rot(P, "qs", 5, [128, 512], F32)
    rtab = Rot(P, "rtab", 4, [128, 512], F32)
    lat = Rot(P, "lat", 2, [128, 3, 512], F32)
    latn = Rot(P, "latn", 2, [128, 3, 512], BF16)
    kh16 = Rot(P, "kh16", 3, [128, 4, 128], BF16)
    v16 = Rot(P, "v16", 3, [128, 512], BF16)
    decr = Rot(P, "dec", 4, [128, 8], F32)
    D_ = {k: P.buf(k) for k in "QT0 KT0 KH0 DEC0 V0 GATE0 QN QR KN KR VBm".split()}

    for t in range(NBA):
        own = t < NBO
        c0 = t * 512
        sl = slice(c0, c0 + 512)
        hT, hTb = hTr.next()
        for j in range(4):
            xt, xb = xr.next()
            P.dma(xt[:], x_in.ap()[c0 + j * 128:c0 + (j + 1) * 128, :], writes=[xb])
            NA.run(xt[:], xb, "attn0", hT, hTb, j * 128)
        ct, ctb = rtab.next(); st_, stb_ = rtab.next()
        P.dma(ct[:], ROPEC.ap()[:, sl], writes=[ctb])
        P.dma(st_[:], ROPES.ap()[:, sl], writes=[stb_])

        def fm(group, ps, psb):
            mm_group(P, ps, psb, wa, wab, 8, group * 128, 128, hT, hTb, slice(0, 512), 512)

        for j in range(4):
            ps, psb = ps_tm.next()
            mm_group_tm(P, ps, psb, hT, hTb, slice(j * 128, (j + 1) * 128), 8, wa, wab, 23 * 128, 512)
            v, vb = v16.next()
            P.op("act", lambda e, v=v, ps=ps: e.copy(out=v[:], in_=ps[:]), reads=[psb], writes=[vb])
            P.dma(V0.ap()[c0 + j * 128:c0 + (j + 1) * 128, :], v[:], reads=[vb], writes=[D_["V0"]], eng="pool")
        qs = []
        if own:
            for h in range(4):
                ps, psb = psr.next()
                fm(h, ps, psb)
                q, qb_ = qsr.next()
                P.op("act", lambda e, q=q, ps=ps: e.activation(out=q[:], in_=ps[:], func=AF.Silu), reads=[psb], writes=[qb_])
                qs.append((q, qb_))
            for h in range(4):
                ps, psb = psr.next()
                fm(12 + h, ps, psb)
                g, gb = f32r.next()
                P.op("act", lambda e, g=g, ps=ps: e.activation(out=g[:], in_=ps[:], func=AF.Silu), reads=[psb], writes=[gb])
                P.dma(GATE0.ap()[h, :, sl], g[:], reads=[gb], writes=[D_["GATE0"]], eng="pool")
        sig_l = []
        for d in ((0, 1) if own else (1,)):
            for h in range(4):
                ps, psb = psr.next()
                fm(4 + 4 * d + h, ps, psb)
                sg, sgb = sgr.next()
                P.op("act", lambda e, sg=sg, ps=ps: e.activation(out=sg[:], in_=ps[:], func=AF.Sigmoid), reads=[psb], writes=[sgb])
                sig_l.append((d, h, sg, sgb))
        for (d, h, sg, sgb) in sig_l:
            ci = d * 4 + h
            g, gb = f32r.next()
            P.op("act", lambda e, g=g, sg=sg, ci=ci: e.activation(out=g[:], in_=sg[:], func=AF.Ln, bias=lbt[:, 8 + ci:9 + ci],
                                                                 scale=lbt[:, 16 + ci:17 + ci]), reads=[sgb, lbb], writes=[gb])
            k, kb = f32r.next()
            P.op("pool", lambda e, k=k, sg=sg, ci=ci: e.tensor_scalar(out=k[:], in0=sg[:], scalar1=lbt[:, 24 + ci:25 + ci],
                                                                     scalar2=lbt[:, 16 + ci:17 + ci], op0=ALU.mult, op1=ALU.add),
                 reads=[sgb, lbb], writes=[kb])
            b_, bb = f32r.next()
            P.op("dve", lambda e, b_=b_, g=g: e.tensor_tensor_scan(out=b_[:], data0=rmask[:].rearrange("p a b -> p (a b)"),
                                                                  data1=g[:], initial=0.0, op0=ALU.mult, op1=ALU.add),
                 reads=[gb, rmb], writes=[bb])
            b3 = b_[:].rearrange("p (a b) -> p a b", b=64)
            dl, dlb = f32r.next()
            P.op("pool", lambda e, dl=dl, b3=b3: e.tensor_tensor(out=dl[:].rearrange("p (a b) -> p a b", b=64), in0=b3,
                                                               in1=b3[:, :, 63:64].to_broadcast([128, 8, 64]), op=ALU.subtract),
                 reads=[bb], writes=[dlb])
            dec, decb = decr.next()
            P.op("act", lambda e, dec=dec, b3=b3: e.activation(out=dec[:], in_=b3[:, :, 63], func=AF.Exp), reads=[bb], writes=[decb])
            P.dma(DEC0.ap()[d, h, :, t * 8:(t + 1) * 8], dec[:], reads=[decb], writes=[D_["DEC0"]], eng="pool")
            if d == 0:
                bq, bqb = b_, bb
                ex, exb = dl, dlb
                ex_scale = -1.0
            else:
                bq, bqb = f32r.next()
                P.op("pool", lambda e, bq=bq, g=g, dl=dl: e.tensor_tensor(out=bq[:], in0=g[:], in1=dl[:], op=ALU.subtract),
                     reads=[gb, dlb], writes=[bqb])
                ex, exb = f32r.next()
                P.op("pool", lambda e, ex=ex, b_=b_, g=g: e.tensor_tensor(out=ex[:], in0=b_[:], in1=g[:], op=ALU.subtract),
                     reads=[bb, gb], writes=[exb])
                ex_scale = 1.0
            ek, ekb = f32r.next()
            P.op("act", lambda e, ek=ek, ex=ex, ex_scale=ex_scale: e.activation(out=ek[:], in_=ex[:], func=AF.Exp, scale=ex_scale),
                 reads=[exb], writes=[ekb])
            kh, khb = b16r.next()
            P.op("dve", lambda e, kh=kh, k=k, ek=ek: e.tensor_tensor(out=kh[:], in0=k[:], in1=ek[:], op=ALU.mult),
                 reads=[kb, ekb], writes=[khb])
            pk, pkb = ps_kt.next()

            def trk(e, pk=pk, kh=kh):
                ins = None
                for j in range(4):
                    ins = e.transpose(out=pk[:, j, :], in_=kh[:, j * 128:(j + 1) * 128], identity=ident[:])
                return ins

            P.op("pe", trk, reads=[khb, identb], writes=[pkb])
            kt_, ktb = kh16.next()
            P.op("act", lambda e, kt_=kt_, pk=pk: e.copy(out=kt_[:], in_=pk[:]), reads=[pkb], writes=[ktb])
            P.dma(KH0.ap()[d, h, sl, :].rearrange("(j p) k -> p j k", p=128), kt_[:], reads=[ktb], writes=[D_["KH0"]], eng="pool")
            if own:
                eq, eqb = f32r.next()
                P.op("act", lambda e, eq=eq, bq=bq: e.activation(out=eq[:], in_=bq[:], func=AF.Exp), reads=[bqb], writes=[eqb])
                en, enb = f32r.next()
                P.op("act", lambda e, en=en, bq=bq: e.activation(out=en[:], in_=bq[:], func=AF.Exp, scale=-1.0), reads=[bqb], writes=[enb])
                qt, qtb = b16r.next()
                q, qb_ = qs[h]
                P.op("dve", lambda e, qt=qt, q=q, eq=eq: e.scalar_tensor_tensor(out=qt[:], in0=q[:], scalar=128 ** -0.5, in1=eq[:],
                                                                              op0=ALU.mult, op1=ALU.mult), reads=[qb_, eqb], writes=[qtb])
                P.dma(QT0.ap()[d, h, :, sl], qt[:], reads=[qtb], writes=[D_["QT0"]], eng="pool")
                kt2, kt2b = b16r.next()
                P.op("pool", lambda e, kt2=kt2, k=k, en=en: e.tensor_tensor(out=kt2[:], in0=k[:], in1=en[:], op=ALU.mult),
                     reads=[kb, enb], writes=[kt2b])
                P.dma(KT0.ap()[d, h, :, sl], kt2[:], reads=[kt2b], writes=[D_["KT0"]], eng="pool")
        lat_jobs = ([(16, 3, 20)] if own else []) + [(19, 2, 23)]
        for (g0, ng, gcol) in lat_jobs:
            lt, ltb = lat.next()
            for i in range(ng):
                ps, psb = psr.next()
                fm(g0 + i, ps, psb)
                P.op("act", lambda e, lt=lt, ps=ps, i=i: e.copy(out=lt[:, i, :], in_=ps[:]), reads=[psb], writes=[ltb])
            ss, ssb = ps_ss.next()
            sqs = []
            for i in range(ng):
                sq, sqb = b16r.next()
                P.op("pool", lambda e, sq=sq, lt=lt, i=i: e.tensor_tensor(out=sq[:], in0=lt[:, i, :], in1=lt[:, i, :], op=ALU.mult),
                     reads=[ltb], writes=[sqb])
                sqs.append((sq, sqb))

            def ssm(e, ss=ss, sqs=sqs):
                ins = None
                for i, (sq, _) in enumerate(sqs):
                    ins = e.matmul(ss[:], lhsT=ones[:], rhs=sq[:], start=(i == 0), stop=(i == len(sqs) - 1))
                return ins

            P.op("pe", ssm, reads=[onesb] + [b for _, b in sqs], writes=[ssb])
            rs, rsb = f32r.next()
            P.op("act", lambda e, rs=rs, ss=ss, ng=ng: e.activation(out=rs[:], in_=ss[:], func=AF.Sqrt, bias=NA.eps[:, 0:1],
                                                                   scale=1.0 / (ng * 128)), reads=[ssb, NA.epsb], writes=[rsb])
            P.op("dve", lambda e, rs=rs: e.reciprocal(out=rs[:], in_=rs[:]), reads=[rsb], writes=[rsb])
            ln, lnb = latn.next()
            for i in range(ng):
                P.op("dve", lambda e, ln=ln, lt=lt, rs=rs, i=i, gcol=gcol: e.scalar_tensor_tensor(
                    out=ln[:, i, :], in0=lt[:, i, :], scalar=cols[:, gcol + i:gcol + i + 1], in1=rs[:], op0=ALU.mult, op1=ALU.mult),
                    reads=[ltb, rsb, colsb], writes=[lnb])
            if g0 == 16:
                for h in range(4):
                    ps, psb = psr.next()
                    mm_group(P, ps, psb, wuq, wuqb, 3, h * 128, 128, ln, lnb, slice(0, 512), 512)
                    o, ob = b16r.next()
                    P.op("act", lambda e, o=o, ps=ps: e.copy(out=o[:], in_=ps[:]), reads=[psb], writes=[ob])
                    P.dma(QN.ap()[h, :, sl], o[:], reads=[ob], writes=[D_["QN"]], eng="pool")
                for pr in range(2):
                    ps, psb = psr.next()
                    mm_group(P, ps, psb, wuq, wuqb, 3, (4 + pr) * 128, 128, ln, lnb, slice(0, 512), 512)
                    ps2, ps2b = psr.next()
                    mm_group(P, ps2, ps2b, wuq, wuqb, 3, (6 + pr) * 128, 128, ln, lnb, slice(0, 512), 512)
                    t1, t1b = f32r.next()
                    P.op("dve", lambda e, t1=t1, ps=ps, ct=ct: e.tensor_tensor(out=t1[:], in0=ps[:], in1=ct[:], op=ALU.mult),
                         reads=[psb, ctb], writes=[t1b])
                    t2, t2b = f32r.next()
                    P.op("dve", lambda e, t2=t2, ps2=ps2, st_=st_: e.tensor_tensor(out=t2[:], in0=ps2[:], in1=st_[:], op=ALU.mult),
                         reads=[ps2b, stb_], writes=[t2b])
                    o, ob = b16r.next()
                    P.op("pool", lambda e, o=o, t1=t1, t2=t2: e.tensor_tensor(out=o[:], in0=t1[:], in1=t2[:], op=ALU.add),
                         reads=[t1b, t2b], writes=[ob])
                    P.dma(QR.ap()[pr, :, sl], o[:], reads=[ob], writes=[D_["QR"]], eng="pool")
            else:
                for h in range(4):
                    ps, psb = psr.next()
                    mm_group(P, ps, psb, wukv, wukvb, 2, h * 128, 128, ln, lnb, slice(0, 512), 512)
                    o, ob = b16r.next()
                    P.op("act", lambda e, o=o, ps=ps: e.copy(out=o[:], in_=ps[:]), reads=[psb], writes=[ob])
                    P.dma(KN.ap()[h, :, sl], o[:], reads=[ob], writes=[D_["KN"]], eng="pool")
                for j in range(4):
                    ps, psb = ps_tm.next()
                    mm_group_tm(P, ps, psb, ln, lnb, slice(j * 128, (j + 1) * 128), 2, wukv, wukvb, 512, 512)
                    v, vb = v16.next()
                    P.op("act", lambda e, v=v, ps=ps: e.copy(out=v[:], in_=ps[:]), reads=[psb], writes=[vb])
                    P.dma(VBm.ap()[:, c0 + j * 128:c0 + (j + 1) * 128, :].rearrange("h p d -> p h d"),
                          v[:].rearrange("p (h d) -> p h d", d=128), reads=[vb], writes=[D_["VBm"]], eng="pool")
        ps, psb = psr.next()
        fm(21, ps, psb)
        ps2, ps2b = psr.next()
        fm(22, ps2, ps2b)
        t1, t1b = f32r.next()
        P.op("dve", lambda e, t1=t1, ps=ps, ct=ct: e.tensor_tensor(out=t1[:], in0=ps[:], in1=ct[:], op=ALU.mult), reads=[psb, ctb], writes=[t1b])
        t2, t2b = f32r.next()
        P.op("dve", lambda e, t2=t2, ps2=ps2, st_=st_: e.tensor_tensor(out=t2[:], in0=ps2[:], in1=st_[:], op=ALU.mult), reads=[ps2b, stb_], writes=[t2b])
        o, ob = b16r.next()
        P.op("pool", lambda e, o=o, t1=t1, t2=t2: e.tensor_tensor(out=o[:], in0=t1[:], in1=t2[:], op=ALU.add), reads=[t1b, t2b], writes=[ob])
        P.dma(KR.ap()[:, sl], o[:], reads=[ob], writes=[D_["KR"]], eng="pool")
    P.emit()
    if upto == "A":
        return nc

    P = Phase(nc, "pB")
    cols = P.sb("cols", [128, 32], F32); colsb = P.buf("cols")
    P.dma(cols[:], cols_in.ap(), writes=[colsb])
    ones = P.sb("ones", [128, 128], BF16); onesb = P.buf("ones")
    P.op("pool", lambda e: e.memset(ones[:], 1.0), writes=[onesb])
    epsB, epsBb = eps_tile(P)
    masks = []
    for d in range(2):
        m = P.sb(f"mask{d}", [128, 128], F32); mb = P.buf(f"mask{d}")

        P.op("pool", lambda e, m=m: e.memset(m[:], 1.0), writes=[mb])
        if d == 0:
            P.op("pool", lambda e, m=m: e.affine_select(out=m[:], in_=m[:], pattern=[[1, 128]], compare_op=ALU.is_ge, fill=0.0,
                                                       base=0, channel_multiplier=-1), reads=[mb], writes=[mb])
            P.op("pool", lambda e, m=m: e.memset(m[0:64, 64:128], 0.0), reads=[mb], writes=[mb])
        else:
            P.op("pool", lambda e, m=m: e.affine_select(out=m[:], in_=m[:], pattern=[[-1, 128]], compare_op=ALU.is_ge, fill=0.0,
                                                       base=0, channel_multiplier=1), reads=[mb], writes=[mb])
            P.op("pool", lambda e, m=m: e.memset(m[64:128, 0:64], 0.0), reads=[mb], writes=[mb])
        m2 = P.sb(f"maskf{d}", [128, 128], F32); m2b = P.buf(f"maskf{d}")
        P.op("dve", lambda e, m=m, m2=m2: e.tensor_copy(out=m2[:], in_=m[:]), reads=[mb], writes=[m2b])
        masks.append((m2, m2b))
    St = P.sb("St", [128, 4, 128], F32); Sb = [P.buf(f"S{h}") for h in range(4)]
    S16 = P.sb("S16", [128, 4, 128], BF16); S16b = P.buf("S16")
    qtr = Rot(P, "qt", 2, [128, 4, 512], BF16)
    ktr = Rot(P, "kt", 2, [128, 4, 512], BF16)
    khr = Rot(P, "kh", 2, [128, 4, 4, 128], BF16)
    vr = Rot(P, "v", 2, [128, 4, 512], BF16)
    dcr = Rot(P, "dc", 2, [128, 4, 8], F32)
    ofr = Rot(P, "of", 2, [128, 4, 512], F32)
    gtr = Rot(P, "gt", 2, [128, 4, 512], F32)
    psO = [Rot(P, f"psO{h}", 1, [128, 512], F32, psum=True) for h in range(4)]
    psXr = Rot(P, "psX", 2, [128, 4, 128], F32, psum=True)
    psSr = Rot(P, "psS", 1, [128, 4, 128], F32, psum=True)
    ps_ss = Rot(P, "ps_ss", 1, [128, 512], F32, psum=True)
    atr = Rot(P, "at", 2, [128, 4, 128], BF16)
    sc32r = Rot(P, "sc32", 2, [128, 4, 128], F32)
    ev = Rot(P, "ev", 6, [128, 512], F32)
    e16 = Rot(P, "e16", 4, [128, 512], BF16)
    DOF = P.buf("OFW"); DMX = P.buf("MIXT")

    def reset_state():
        for h in range(4):
            P.op("pool", lambda e, h=h: e.memset(St[:, h, :], 0.0), writes=[Sb[h]])
        P.op("pool", lambda e: e.memset(S16[:], 0.0), writes=[S16b])

    def hgrn_block(d, t, full, final):
        c0 = t * 512
        sl = slice(c0, c0 + 512)
        kh, khb = khr.next()
        for h in range(4):
            P.dma(kh[:, h, :, :], KH0.ap()[d, h, sl, :].rearrange("(j p) k -> p j k", p=128), writes=[khb])
        v, vb = vr.next()
        P.dma(v[:], V0.ap()[sl, :].rearrange("(j p) c -> p j c", p=128), writes=[vb])
        dc, dcb = dcr.next()
        P.dma(dc[:], DEC0.ap()[d, :, :, t * 8:(t + 1) * 8].rearrange("h p c -> p h c"), writes=[dcb])
        if full:
            qt, qtb = qtr.next()
            P.dma(qt[:], QT0.ap()[d, :, :, sl].rearrange("h p s -> p h s"), writes=[qtb])
            kt, ktb = ktr.next()
            P.dma(kt[:], KT0.ap()[d, :, :, sl].rearrange("h p s -> p h s"), writes=[ktb])
            pso = [psO[h].next() for h in range(4)]
        if final:
            of, ofb = ofr.next()
            P.dma(of[:], OFW.ap()[:, :, sl].rearrange("h p s -> p h s"), reads=[DOF], writes=[ofb])
            gt, gtb = gtr.next()
            P.dma(gt[:], GATE0.ap()[:, :, sl].rearrange("h p s -> p h s"), writes=[gtb])
        m, mb = masks[d]
        tiles = range(4) if d == 0 else range(3, -1, -1)
        for j in tiles:
            chunks = (0, 1) if d == 0 else (1, 0)
            if full:
                pS, pSb = psSr.next()

                def sc(e, pS=pS, j=j):
                    ins = None
                    for h in range(4):
                        ins = e.matmul(pS[:, h, :], lhsT=kt[:, h, j * 128:(j + 1) * 128], rhs=qt[:, h, j * 128:(j + 1) * 128],
                                       start=True, stop=True)
                    return ins

                P.op("pe", sc, reads=[ktb, qtb], writes=[pSb])
                sc32, sc32b = sc32r.next()
                P.op("act", lambda e, sc32=sc32, pS=pS: e.copy(out=sc32[:], in_=pS[:]), reads=[pSb], writes=[sc32b])
                at, atb = atr.next()
                P.op("pool", lambda e, at=at, sc32=sc32: e.tensor_tensor(out=at[:], in0=sc32[:],
                                                                       in1=m[:].unsqueeze(1).to_broadcast([128, 4, 128]), op=ALU.mult),
                     reads=[sc32b, mb], writes=[atb])
            for c in chunks:
                pr = slice(64 * c, 64 * c + 64)
                co = j * 128 + c * 64
                cidx = j * 2 + c
                if full:
                    for h in range(4):
                        po, pob = pso[h]

                        def om(e, at=at, po=po, h=h, co=co, c=c, j=j):
                            e.matmul(po[:, co:co + 64], lhsT=v[:, j, h * 128:(h + 1) * 128], rhs=at[:, h, c * 64:(c + 1) * 64],
                                     start=True, stop=False)
                            return e.matmul(po[:, co:co + 64], lhsT=S16[:, h, :], rhs=qt[:, h, co:co + 64], start=False, stop=True)

                        P.op("pe", om, reads=[vb, atb, S16b, qtb], writes=[pob])
                pX, pXb = psXr.next()

                def su(e, pX=pX, pr=pr, j=j):
                    ins = None
                    for h in range(4):
                        ins = e.matmul(pX[:, h, :], lhsT=kh[pr, h, j, :], rhs=v[pr, j, h * 128:(h + 1) * 128], start=True, stop=True)
                    return ins

                P.op("pe", su, reads=[khb, vb], writes=[pXb])
                for h in range(4):
                    P.op("dve", lambda e, pX=pX, h=h, cidx=cidx: e.scalar_tensor_tensor(
                        out=St[:, h, :], in0=St[:, h, :], scalar=dc[:, h, cidx:cidx + 1], in1=pX[:, h, :], op0=ALU.mult, op1=ALU.add),
                        reads=[Sb[h], dcb, pXb], writes=[Sb[h]])
                P.op("act", lambda e: e.copy(out=S16[:], in_=St[:]), reads=Sb, writes=[S16b])
        if full:
            for h in range(4):
                po, pob = pso[h]
                if not final:
                    o, ob = ev.next()
                    P.op("act", lambda e, o=o, po=po: e.copy(out=o[:], in_=po[:]), reads=[pob], writes=[ob])
                    P.dma(OFW.ap()[h, :, sl], o[:], reads=[ob], writes=[DOF], eng="pool")
                else:
                    tot, totb = ev.next()
                    P.op("dve", lambda e, tot=tot, po=po, h=h: e.tensor_tensor(out=tot[:], in0=po[:], in1=of[:, h, :], op=ALU.add),
                         reads=[pob, ofb], writes=[totb])
                    sq, sqb = e16.next()
                    P.op("act", lambda e, sq=sq, tot=tot: e.activation(out=sq[:], in_=tot[:], func=AF.Square), reads=[totb], writes=[sqb])
                    ss, ssb = ps_ss.next()
                    P.op("pe", lambda e, ss=ss, sq=sq: e.matmul(ss[:], lhsT=ones[:], rhs=sq[:], start=True, stop=True),
                         reads=[onesb, sqb], writes=[ssb])
                    rs, rsb = ev.next()
                    P.op("act", lambda e, rs=rs, ss=ss: e.activation(out=rs[:], in_=ss[:], func=AF.Sqrt, bias=epsB[:, 0:1], scale=1.0 / 128),
                         reads=[ssb, epsBb], writes=[rsb])
                    P.op("dve", lambda e, rs=rs: e.reciprocal(out=rs[:], in_=rs[:]), reads=[rsb], writes=[rsb])
                    P.op("pool", lambda e, tot=tot, rs=rs: e.tensor_tensor(out=tot[:], in0=tot[:], in1=rs[:], op=ALU.mult),
                         reads=[totb, rsb], writes=[totb])
                    o, ob = e16.next()
                    P.op("dve", lambda e, o=o, tot=tot, h=h: e.scalar_tensor_tensor(out=o[:], in0=tot[:], scalar=cols[:, 16 + h:17 + h],
                                                                                   in1=gt[:, h, :], op0=ALU.mult, op1=ALU.mult),
                         reads=[totb, colsb, gtb], writes=[ob])
                    P.dma(MIXT.ap()[h, :, sl], o[:], reads=[ob], writes=[DMX], eng="pool")

    reset_state()
    for t in range(NBO):
        hgrn_block(0, t, True, False)
    reset_state()
    for t in range(NBA - 1, NBO - 1, -1):
        hgrn_block(1, t, False, False)
    for t in range(NBO - 1, -1, -1):
        hgrn_block(1, t, True, True)
    P.emit()
    if upto == "B":
        return nc

    def attn_setup_common(P):
        C = dict(mx=P.buf("MIXOUT"))
        ones32 = P.sb("ones32", [128, 128], F32); o32b = P.buf("ones32")
        P.op("pool", lambda e: e.memset(ones32[:], 1.0), writes=[o32b])
        C.update(ones32=ones32, o32b=o32b, ss=Rot(P, "ss", 1, [128, 512], F32, psum=True),
                 r32=Rot(P, "r32", 4, [128, 512], F32), o16=Rot(P, "o16", 2, [128, 512], BF16))
        return C

    def mla_setup(P):
        C = attn_setup_common(P)
        C["KT"] = [(P.sb(f"K{i}", [128, S], BF16), P.buf(f"K{i}")) for i in range(2)]
        C["QT"] = [(P.sb(f"Q{i}", [128, SH], BF16), P.buf(f"Q{i}")) for i in range(2)]
        C["VT"] = (P.sb("V", [128, NT, 128], BF16), P.buf("V"))
        P.op("pool", lambda e: e.memset(C["KT"][1][0][:], 0.0), writes=[C["KT"][1][1]])
        P.op("pool", lambda e: e.memset(C["QT"][1][0][:], 0.0), writes=[C["QT"][1][1]])
        return C

    def den_matmul(P, C, accs):
        ss, ssb = C["ss"].next()

        def f(e):
            ins = None
            for i, (a_, _) in enumerate(accs):
                ins = e.matmul(ss[:], lhsT=C["ones32"][:], rhs=a_[:], start=(i == 0), stop=(i == len(accs) - 1))
            return ins

        P.op("pe", f, reads=[C["o32b"]] + [ab_ for _, ab_ in accs], writes=[ssb])
        return ss, ssb

    def mla_finish(P, C, hi, qb, po, acc, pd):
        ss, ssb = den_matmul(P, C, [acc[0]])
        r, rb = C["r32"].next()
        P.op("dve", lambda e: e.reciprocal(out=r[:], in_=ss[:]), reads=[ssb], writes=[rb])
        o_, ob_ = po[0]
        o16, o16b = C["o16"].next()
        P.op("dve", lambda e: e.tensor_tensor(out=o16[:], in0=o_[:], in1=r[:], op=ALU.mult), reads=[ob_, rb], writes=[o16b])
        P.dma(MIXT.ap()[4 + hi, :, qb * 512:(qb + 1) * 512], o16[:], reads=[o16b], writes=[C["mx"]], eng="pool")

    def mla_head(h):
        pb = 64 * (h % 2)

        def load(P, slot):
            C = mla_C[0]
            (k0, k0b), (k1, k1b) = C["KT"]
            (q0, q0b), (q1, q1b) = C["QT"]
            vt, vb = C["VT"]
            half = S // 2
            for lo, hi_ in ((0, half), (half, S)):
                P.dma(k0[:, lo:hi_], KN.ap()[h][:, lo:hi_], writes=[k0b])
                P.dma(k1[0:64, lo:hi_], KR.ap()[0:64, lo:hi_], writes=[k1b])
            P.dma(q0[:], QN.ap()[h], writes=[q0b])
            P.dma(q1[0:64, :], QR.ap()[h // 2, pb:pb + 64, :], writes=[q1b])
            vsrc = VBm.ap()[h].rearrange("(n p) d -> p n d", p=128)
            for n0 in range(0, NT, 8):
                n1 = min(NT, n0 + 8)
                P.dma(vt[:, n0:n1, :], vsrc[:, n0:n1, :], writes=[vb])
            return dict(kq_bufs=[k0b, k1b, q0b, q1b], v=(vt, vb), k0=k0, k1=k1, q0=q0, q1=q1, pb=pb)

        def qk(e, ctx, s_, c, kt, qb):
            pb_ = ctx["pb"]
            e.matmul(s_[:], lhsT=ctx["k0"][:, kt * 128:(kt + 1) * 128], rhs=ctx["q0"][:, qb * 512:(qb + 1) * 512], start=True, stop=False)
            return e.matmul(s_[:], lhsT=ctx["k1"][:, kt * 128:(kt + 1) * 128],
                            rhs=ctx["q1"][:, qb * 512:(qb + 1) * 512], start=False, stop=True)

        return dict(load=load, qk=qk)

    mla_C = [None]

    def mla_setup_wrap(P):
        mla_C[0] = mla_setup(P)
        return mla_C[0]

    attn_fm(nc, "pC", S, SH, [mla_head(h) for h in range(4)], 1, MLA_SCALE, 2, 3, 4, mla_setup_wrap, mla_finish)
    if upto == "C":
        return nc

    def outproj_phase(name, MIX, w_o, XIN, xin_is_input, XOUT):
        P = Phase(nc, name)
        wo, wob = load_w(P, "wo", w_o, 8, D)
        mxr = Rot(P, "mx", 2, [128, 8, 512], BF16)
        xr = Rot(P, "x", 3, [128, D], F32)
        psr = Rot(P, "ps", 4, [128, 512], F32, psum=True)
        XO = P.buf("XO")
        for t in range(NBO):
            sl = slice(t * 512, (t + 1) * 512)
            mx, mxb = mxr.next()
            P.dma(mx[:], MIX.ap()[:, :, sl].rearrange("c p s -> p c s"), writes=[mxb])
            for j in range(4):
                r0 = t * 512 + j * 128
                xt, xb = xr.next()
                P.dma(xt[:], XIN.ap()[r0:r0 + 128, :], writes=[xb])
                for n in range(2):
                    ps, psb = psr.next()
                    mm_group_tm(P, ps, psb, mx, mxb, slice(j * 128, (j + 1) * 128), 8, wo, wob, n * 512, 512)
                    P.op("dve", lambda e, xt=xt, ps=ps, n=n: e.tensor_tensor(out=xt[:, n * 512:(n + 1) * 512], in0=xt[:, n * 512:(n + 1) * 512],
                                                                           in1=ps[:], op=ALU.add), reads=[psb, xb], writes=[xb])
                P.dma(XOUT.ap()[r0:r0 + 128, :], xt[:], reads=[xb], writes=[XO], eng="pool")
        P.emit()

    def ffn_phase(name, l, XIN, XOUT, final):
        P = Phase(nc, name)
        ident, identb = make_ident(P)
        wg, wgb = load_w(P, "wg", ffn_g[l], 8, FH)
        wu, wub = load_w(P, "wu", ffn_u[l], 8, FH)
        wd, wdb = load_w(P, "wd", ffn_d[l], NHC, D)
        gains = {"ffn": (rows_in, (2 + l) * D)}
        NA = NormT(P, ident, identb, gains, nbuf=1)
        if final:
            gf = P.sb("gfin", [128, D], F32); gfb = P.buf("gfin")
            P.dma(gf[:], bcast_rows(rows_in, D, 4 * D), writes=[gfb])
        xs = [(P.sb(f"x{i}", [128, D], F32), P.buf(f"x{i}")) for i in range(5)]
        hTr = Rot(P, "hT", 1, [128, 8, 512], BF16)
        actr = Rot(P, "act", 1, [128, NHC, 512], BF16)
        psg = Rot(P, "psg", 3, [128, 512], F32, psum=True)
        psu = Rot(P, "psu", 2, [128, 512], F32, psum=True)
        psd = Rot(P, "psd", 2, [128, 512], F32, psum=True)
        sg = Rot(P, "sg", 2, [128, 512], F32)
        XO = P.buf("XO")
        for t in range(NBO):
            hT, hTb = hTr.next()
            xt_l = []
            for j in range(4):
                xt, xb = xs[(t * 4 + j) % 5]
                r0 = t * 512 + j * 128
                P.dma(xt[:], XIN.ap()[r0:r0 + 128, :], writes=[xb])
                NA.run(xt[:], xb, "ffn", hT, hTb, j * 128)
                xt_l.append((xt, xb))
            act, actb = actr.next()
            for hc in range(NHC):
                pg, pgb = psg.next()
                mm_group(P, pg, pgb, wg, wgb, 8, hc * 128, 128, hT, hTb, slice(0, 512), 512)
                pu, pub = psu.next()
                mm_group(P, pu, pub, wu, wub, 8, hc * 128, 128, hT, hTb, slice(0, 512), 512)
                s_, sb_ = sg.next()
                P.op("act", lambda e, s_=s_, pg=pg: e.activation(out=s_[:], in_=pg[:], func=AF.Silu), reads=[pgb], writes=[sb_])
                P.op("dve", lambda e, s_=s_, pu=pu, hc=hc, act=act: e.tensor_tensor(out=act[:, hc, :], in0=s_[:], in1=pu[:], op=ALU.mult),
                     reads=[sb_, pub], writes=[actb])
            for j in range(4):
                xt, xb = xt_l[j]
                r0 = t * 512 + j * 128
                for n in range(2):
                    pd, pdb = psd.next()
                    mm_group_tm(P, pd, pdb, act, actb, slice(j * 128, (j + 1) * 128), NHC, wd, wdb, n * 512, 512)
                    P.op("dve", lambda e, xt=xt, pd=pd, n=n: e.tensor_tensor(out=xt[:, n * 512:(n + 1) * 512], in0=xt[:, n * 512:(n + 1) * 512],
                                                                           in1=pd[:], op=ALU.add), reads=[pdb, xb], writes=[xb])
                if final:
                    st, stb = NA.rstd(xt[:], xb)
                    P.op("dve", lambda e, xt=xt, st=st: e.scalar_tensor_tensor(out=xt[:], in0=xt[:], scalar=st[:, 2:3], in1=gf[:],
                                                                             op0=ALU.mult, op1=ALU.mult), reads=[xb, stb, gfb], writes=[xb])
                P.dma(XOUT.ap()[r0:r0 + 128, :], xt[:], reads=[xb], writes=[XO], eng="pool")
        P.emit()

    outproj_phase("pD", MIXT, w_o0, x_in, True, X1A)
    if upto == "D":
        return nc
    ffn_phase("pE", 0, X1A, X1, False)
    if upto == "E":
        return nc

    P = Phase(nc, "pF")
    ident, identb = make_ident(P)
    wc, wcb = load_w(P, "wc", w_c, 8, 32 * 128 + D)
    NA = NormT(P, ident, identb, {"attn1": (rows_in, D)})
    xr = Rot(P, "x", 3, [128, D], F32)
    hTr = Rot(P, "hT", 2, [128, 8, 512], BF16)
    psr = Rot(P, "ps", 5, [128, 512], F32, psum=True)
    ps_tm = Rot(P, "ps_tm", 2, [128, 512], F32, psum=True)
    f32r = Rot(P, "f32", 6, [128, 512], F32)
    b16r = Rot(P, "b16", 4, [128, 512], BF16)
    v16 = Rot(P, "v16", 3, [128, D], BF16)
    rtab = Rot(P, "rtab", 4, [128, 512], F32)
    DQ = P.buf("Q1T"); DK = P.buf("K1S"); DV = P.buf("V1S"); DKG = P.buf("K1G"); DVG = P.buf("V1G")
    for t in range(NBO):
        c0 = t * 512
        sl = slice(c0, c0 + 512)
        hT, hTb = hTr.next()
        for j in range(4):
            xt, xb = xr.next()
            P.dma(xt[:], X1.ap()[c0 + j * 128:c0 + (j + 1) * 128, :], writes=[xb])
            NA.run(xt[:], xb, "attn1", hT, hTb, j * 128)
        ct, ctb = rtab.next(); st_, stb_ = rtab.next()
        P.dma(ct[:], ROPEC.ap()[:, sl], writes=[ctb])
        P.dma(st_[:], ROPES.ap()[:, sl], writes=[stb_])
        for j in range(4):
            v, vb = v16.next()
            for n in range(2):
                ps, psb = ps_tm.next()
                mm_group_tm(P, ps, psb, hT, hTb, slice(j * 128, (j + 1) * 128), 8, wc, wcb, 32 * 128 + n * 512, 512)
                P.op("act", lambda e, v=v, ps=ps, n=n: e.copy(out=v[:, n * 512:(n + 1) * 512], in_=ps[:]), reads=[psb], writes=[vb])
            r0 = c0 + j * 128
            P.dma(V1S[r0 // VR].ap()[r0 % VR:r0 % VR + 128, :], v[:], reads=[vb], writes=[DV], eng="pool")
        for which in range(2):
            for h in range(8):
                ps, psb = psr.next()
                mm_group(P, ps, psb, wc, wcb, 8, (which * 16 + h) * 128, 128, hT, hTb, slice(0, 512), 512)
                ps2, ps2b = psr.next()
                mm_group(P, ps2, ps2b, wc, wcb, 8, (which * 16 + 8 + h) * 128, 128, hT, hTb, slice(0, 512), 512)
                t1, t1b = f32r.next()
                P.op("dve", lambda e, t1=t1, ps=ps, ct=ct: e.tensor_tensor(out=t1[:], in0=ps[:], in1=ct[:], op=ALU.mult), reads=[psb, ctb], writes=[t1b])
                t2, t2b = f32r.next()
                P.op("dve", lambda e, t2=t2, ps2=ps2, st_=st_: e.tensor_tensor(out=t2[:], in0=ps2[:], in1=st_[:], op=ALU.mult), reads=[ps2b, stb_], writes=[t2b])
                o, ob = b16r.next()
                P.op("pool", lambda e, o=o, t1=t1, t2=t2: e.tensor_tensor(out=o[:], in0=t1[:], in1=t2[:], op=ALU.add), reads=[t1b, t2b], writes=[ob])
                if which == 0:
                    P.dma(Q1T.ap()[h, :, sl], o[:], reads=[ob], writes=[DQ], eng="pool")
                else:
                    P.dma(K1S[h].ap()[:, sl], o[:], reads=[ob], writes=[DK], eng="pool")
    groups = npair_groups
    for h in range(8):
        P.op("pool", lambda e, h=h: e.collective_compute("AllGather", ALU.bypass, replica_groups=groups, ins=[K1S[h].ap()],
                                                         outs=[K1G[h].ap()]), reads=[DK], writes=[DKG], dma=True, inc=1)
    for c in range(NVC):
        P.op("pool", lambda e, c=c: e.collective_compute("AllGather", ALU.bypass, replica_groups=groups, ins=[V1S[c].ap()],
                                                         outs=[V1G[c].ap()]), reads=[DV], writes=[DVG], dma=True, inc=1)
    P.emit()
    if upto == "F":
        return nc

    def diff_setup(P):
        C = attn_setup_common(P)
        C["KT"] = Rot(P, "K", 2, [128, 2, SH], BF16)
        C["Q0"] = Rot(P, "Qa", 2, [128, SH], BF16)
        C["Q1"] = Rot(P, "Qb", 2, [128, SH], BF16)
        for rot in (C["Q0"], C["Q1"]):
            for (t_, b_) in rot.items:
                P.op("pool", lambda e, t_=t_: e.memset(t_[:], 0.0), writes=[b_])
        C["VT"] = Rot(P, "V", 2, [128, NT, 128], BF16)
        cols = P.sb("cols", [128, 32], F32); colsb = P.buf("cols")
        P.dma(cols[:], cols_in.ap(), writes=[colsb])
        ones16 = P.sb("ones16", [128, 128], BF16); o16b_ = P.buf("ones16")
        P.op("pool", lambda e: e.memset(ones16[:], 1.0), writes=[o16b_])
        lv = P.sb("lv", [128, 256], F32); lvb = P.buf("lv")
        P.dma(lv[:], bcast_rows(rows_in, 256, 6 * D), writes=[lvb])
        lam = P.sb("lam", [128, 8], F32); lamb = P.buf("lam")
        pr = P.sb("pr", [128, 128], F32); prb = P.buf("pr")
        P.op("dve", lambda e: e.tensor_tensor(out=pr[:, 0:64], in0=lv[:, 0:64], in1=lv[:, 64:128], op=ALU.mult), reads=[lvb], writes=[prb])
        P.op("dve", lambda e: e.tensor_tensor(out=pr[:, 64:128], in0=lv[:, 128:192], in1=lv[:, 192:256], op=ALU.mult), reads=[lvb], writes=[prb])
        P.op("dve", lambda e: e.reduce_sum(out=lam[:, 0:2], in_=pr[:].rearrange("p (a b) -> p a b", b=64), axis=AX.X), reads=[prb], writes=[lamb])
        P.op("act", lambda e: e.activation(out=lam[:, 2:4], in_=lam[:, 0:2], func=AF.Exp), reads=[lamb], writes=[lamb])
        P.op("dve", lambda e: e.tensor_tensor(out=lam[:, 4:5], in0=lam[:, 2:3], in1=lam[:, 3:4], op=ALU.subtract), reads=[lamb], writes=[lamb])
        P.op("dve", lambda e: e.tensor_scalar(out=lam[:, 5:6], in0=lam[:, 4:5], scalar1=LAMBDA_INIT, scalar2=-1.0, op0=ALU.add, op1=ALU.mult),
             reads=[lamb], writes=[lamb])
        epsd = P.sb("epsd", [128, 1], F32); epsdb = P.buf("epsd")
        P.op("pool", lambda e: e.memset(epsd[:], EPS / (1.0 - LAMBDA_INIT) ** 2), writes=[epsdb])
        C.update(lam=lam, lamb=lamb, eps=epsd, epsb=epsdb, cols=cols, colsb=colsb, ones16=ones16, ones16b=o16b_,
                 sq16=Rot(P, "sq16", 2, [128, 512], BF16))
        return C

    def diff_finish(P, C, hi, qb, po, acc, pd):
        lam, lamb = C["lam"], C["lamb"]
        d0, d0b = den_matmul(P, C, [acc[0]])
        d1, d1b = pd[1]
        r0, r0b = C["r32"].next()
        P.op("dve", lambda e: e.reciprocal(out=r0[:], in_=d0[:]), reads=[d0b], writes=[r0b])
        r1, r1b = C["r32"].next()
        P.op("dve", lambda e: e.reciprocal(out=r1[:], in_=d1[:]), reads=[d1b], writes=[r1b])
        (o0, o0b), (o1, o1b) = po
        a, ab = C["r32"].next()
        P.op("dve", lambda e: e.tensor_tensor(out=a[:], in0=o0[:], in1=r0[:], op=ALU.mult), reads=[o0b, r0b], writes=[ab])
        t, tb = C["r32"].next()
        P.op("dve", lambda e: e.tensor_tensor(out=t[:], in0=o1[:], in1=r1[:], op=ALU.mult), reads=[o1b, r1b], writes=[tb])
        P.op("dve", lambda e: e.scalar_tensor_tensor(out=a[:], in0=t[:], scalar=lam[:, 5:6], in1=a[:], op0=ALU.mult, op1=ALU.add),
             reads=[ab, tb, lamb], writes=[ab])
        sq, sqb = C["sq16"].next()
        P.op("act", lambda e: e.activation(out=sq[:], in_=a[:], func=AF.Square), reads=[ab], writes=[sqb])
        ss, ssb = C["ss"].next()
        P.op("pe", lambda e: e.matmul(ss[:], lhsT=C["ones16"][:], rhs=sq[:], start=True, stop=True), reads=[C["ones16b"], sqb], writes=[ssb])
        rs, rsb = r0, r0b
        P.op("act", lambda e: e.activation(out=rs[:], in_=ss[:], func=AF.Sqrt, bias=C["eps"][:, 0:1],
                                           scale=1.0 / (128 * (1.0 - LAMBDA_INIT) ** 2)), reads=[ssb, C["epsb"]], writes=[rsb])
        P.op("dve", lambda e: e.reciprocal(out=rs[:], in_=rs[:]), reads=[rsb], writes=[rsb])
        o16, o16b = C["o16"].next()
        P.op("dve", lambda e: e.scalar_tensor_tensor(out=o16[:], in0=a[:], scalar=C["cols"][:, 25:26], in1=rs[:], op0=ALU.mult, op1=ALU.mult),
             reads=[ab, rsb, C["colsb"]], writes=[o16b])
        P.dma(MIXT1.ap()[hi, :, qb * 512:(qb + 1) * 512], o16[:], reads=[o16b], writes=[C["mx"]], eng="pool")

    def diff_head(h):
        def load(P, slot):
            C = diff_C[0]
            kt, ktb = C["KT"].next()
            kap = K1G[h].ap().rearrange("(r p) s -> p r s", r=2)
            for r in range(2):
                P.dma(kt[:, r, :], kap[:, r, :], writes=[ktb])
            q0, q0b = C["Q0"].next()
            q1, q1b = C["Q1"].next()
            P.dma(q0[0:64, :], Q1T.ap()[h][0:64, :], writes=[q0b])
            P.dma(q1[64:128, :], Q1T.ap()[h][64:128, :], writes=[q1b])
            vt, vb = C["VT"].next()
            for r in range(2):
                for c in range(NVC):
                    n0 = r * (SH // 128) + c * (VR // 128)
                    vap = V1G[c].ap()[r * VR:(r + 1) * VR, h * 128:(h + 1) * 128]
                    P.dma(vt[:, n0:n0 + VR // 128, :], vap.rearrange("(n p) d -> p n d", p=128), writes=[vb])
            return dict(kq_bufs=[ktb, q0b, q1b], v=(vt, vb), ktf=kt[:].rearrange("p r s -> p (r s)"), q=(q0, q1))

        def qk(e, ctx, s_, c, kt, qb):
            return e.matmul(s_[:], lhsT=ctx["ktf"][:, kt * 128:(kt + 1) * 128], rhs=ctx["q"][c][:, qb * 512:(qb + 1) * 512],
                            start=True, stop=True)

        return dict(load=load, qk=qk)

    diff_C = [None]

    def diff_setup_wrap(P):
        diff_C[0] = diff_setup(P)
        return diff_C[0]

    attn_fm(nc, "pG", S, SH, [diff_head(h) for h in range(8)], 2, DIFF_SCALE, 1, 4, 6, diff_setup_wrap, diff_finish, den_pe=(1,))
    if upto == "G":
        return nc

    outproj_phase("pH", MIXT1, w_o1, X1, False, X2A)
    ffn_phase("pI", 1, X2A, out, True)
    return nc


def _swap_halves(w, width):
    n = w.shape[1] // width
    w4 = w.reshape(w.shape[0], n, 2, width // 2)
    return np.ascontiguousarray(w4[:, :, ::-1, :]).reshape(w.shape[0], n * width)


def prep_inputs(S, ncores, x, norm_attn, norm_ffn, ffn_w_gate, ffn_w_up, ffn_w_down, ab_w_in, hgrn_lower_bound,
                hgrn_out_norm, mla_q_norm, mla_w_uq, mla_kv_norm, mla_w_ukv, ab_w_out, c_w_in,
                diff_lambda_q1, diff_lambda_k1, diff_lambda_q2, diff_lambda_k2, diff_out_norm, c_w_out, final_norm):
    f32 = np.float32
    w_in = np.asarray(ab_w_in[0], f32)
    sp = np.cumsum([0, 512, 512, 512, 512, 512, 384, 256, 64])
    Wq, Wffw, Wfbw, Wi, Wg, Wcq, Wckv, Wkr = [w_in[:, sp[i]:sp[i + 1]] for i in range(8)]
    Wkr_sw = _swap_halves(Wkr, 64)
    uq = np.asarray(mla_w_uq[0], f32).reshape(384, 4, 192)
    uq_n = uq[:, :, :128].reshape(384, 512)
    uq_r = uq[:, :, 128:].reshape(384, 256)
    w_uq = np.concatenate([uq_n, uq_r, _swap_halves(uq_r, 64)], 1)
    ukv = np.asarray(mla_w_ukv[0], f32).reshape(256, 4, 256)
    w_ukv = np.concatenate([ukv[:, :, :128].reshape(256, 512), ukv[:, :, 128:].reshape(256, 512)], 1)
    cw = np.asarray(c_w_in[0], f32)
    cq, ck, cv = cw[:, :1024], cw[:, 1024:2048], cw[:, 2048:]
    w_c = np.concatenate([cq, _swap_halves(cq, 64), ck, _swap_halves(ck, 64), cv], 1)
    inv = (1.0 / (10000.0 ** (np.arange(0, 64, 2, dtype=f32) / f32(64)))).astype(f32)
    p = np.arange(128)
    sign = np.where((p % 64) < 32, -1.0, 1.0).astype(f32)
    rows = np.zeros((8, D), f32)
    rows[0] = norm_attn[0]; rows[1] = norm_attn[1]; rows[2] = norm_ffn[0]; rows[3] = norm_ffn[1]; rows[4] = final_norm
    rows[5, :128] = diff_out_norm[0]
    rows[6, 0:64] = diff_lambda_q1[0]; rows[6, 64:128] = diff_lambda_k1[0]
    rows[6, 128:192] = diff_lambda_q2[0]; rows[6, 192:256] = diff_lambda_k2[0]
    lbraw = np.asarray(hgrn_lower_bound, f32)
    shared = dict(w_uq=w_uq, w_ukv=w_ukv, w_o0=np.asarray(ab_w_out[0], f32), w_c=w_c, w_o1=np.asarray(c_w_out[0], f32),
                  rows=rows)
    for l in range(2):
        shared[f"ffn_g{l}"] = np.asarray(ffn_w_gate[l], f32)
        shared[f"ffn_u{l}"] = np.asarray(ffn_w_up[l], f32)
        shared[f"ffn_d{l}"] = np.asarray(ffn_w_down[l], f32)
    per_r = []
    for r in range(2):
        fd = (Wffw, Wfbw) if r == 0 else (Wfbw, Wffw)
        w_a = np.concatenate([Wq, fd[0], fd[1], Wg, Wcq, Wckv, Wkr, Wkr, Wkr_sw, Wkr_sw, Wi], 1)
        cols = np.zeros((128, 32), f32)
        for d in range(2):
            od = d if r == 0 else 1 - d
            cols[:, d * 4:(d + 1) * 4] = lbraw[od, 0].reshape(4, 128).T
            cols[:, 8 + d * 4:8 + (d + 1) * 4] = lbraw[od, 1].reshape(4, 128).T
        cols[:, 16:20] = np.asarray(hgrn_out_norm[0], f32).reshape(4, 128).T
        cols[:, 20:23] = np.asarray(mla_q_norm[0], f32).reshape(3, 128).T
        cols[:, 23:25] = np.asarray(mla_kv_norm[0], f32).reshape(2, 128).T
        cols[:, 25] = np.asarray(diff_out_norm[0], f32)
        posv = np.arange(S, dtype=f32) if r == 0 else (S - 1 - np.arange(S)).astype(f32)
        ang = (posv[None, :] * inv[p % 32][:, None]).astype(f32)
        per_r.append(dict(w_a=np.ascontiguousarray(w_a), cols=cols, ropec_t=np.cos(ang).astype(f32),
                          ropes_t=(np.sin(ang) * sign[:, None]).astype(f32)))
    in_maps = []
    for c in range(ncores):
        b, r = c // 2, c % 2
        xb = np.asarray(x[b], f32)
        if r == 1:
            xb = xb[::-1]
        m = dict(shared)
        m.update(per_r[r])
        m["x"] = np.ascontiguousarray(xb)
        in_maps.append(m)
    return in_maps


_CACHE = {}


def run(S, ncores, inputs, debug=False, upto=None):
    key = (S, ncores, debug, upto)
    if key not in _CACHE:
        groups = [[2 * i, 2 * i + 1] for i in range(ncores // 2)]
        _CACHE[key] = build(S, groups, debug, upto)
    nc = _CACHE[key]
    in_maps = prep_inputs(S, ncores, **inputs)
    res = run_bass_kernel_spmd(nc, in_maps, core_ids=list(range(ncores)))
    B = ncores // 2
    SH = S // 2
    out = np.zeros((B, S, D), np.float32)
    for c in range(ncores):
        b, r = c // 2, c % 2
        o = res.results[c]["out"]
        if r == 0:
            out[b, :SH] = o
        else:
            out[b, SH:] = o[::-1]
    return out, res


def kernel(**inputs):
    out, _ = run(8192, 8, inputs)
    return out
```

```python
import math
import os
import numpy as np
from contextlib import ExitStack
import concourse.bass as bass
import concourse.mybir as mybir
from concourse.bass_utils import run_bass_kernel_spmd

F32 = mybir.dt.float32
BF16 = mybir.dt.bfloat16
ALU = mybir.AluOpType
AF = mybir.ActivationFunctionType
AX = mybir.AxisListType

D = 1024
FH = 2816
NHC = FH // 128
EPS = 1e-6
MLA_SCALE = 192 ** -0.5
DIFF_SCALE = 64 ** -0.5
LAMBDA_INIT = 0.8 - 0.6 * math.exp(-0.3 * 1)


class Buf:
    __slots__ = ("name", "last_w", "readers", "sem", "dma_total", "ent", "kind")

    def __init__(self, name):
        self.name = name
        self.last_w = None
        self.readers = []
        self.sem = None
        self.dma_total = 0


class Op:
    __slots__ = ("eng", "fn", "is_dma", "signal", "sigval", "waits", "dwaits", "dbuf", "inc")

    def __init__(self, eng, fn, is_dma):
        self.eng = eng
        self.fn = fn
        self.is_dma = is_dma
        self.signal = False
        self.sigval = 0
        self.waits = []
        self.dwaits = {}
        self.dbuf = None
        self.inc = 16


ENGS = ("pe", "act", "dve", "pool", "sp")
BLOCKNAME = {"pe": "tensor", "act": "scalar", "dve": "vector", "pool": "gpsimd", "sp": "sync"}


_SEMPOOL = [None]


class SemPool:
    def __init__(self, nc, es, n_dma=64):
        self.eng = {e: [es.enter_context(nc.semaphore(f"s_{e}")), 0] for e in ("pe", "act", "dve", "pool")}
        self.dma = {"sp": [[es.enter_context(nc.semaphore(f"dh{i}")), 0] for i in range(44)],
                    "pool": [[es.enter_context(nc.semaphore(f"ds{i}")), 0] for i in range(40)],
                    "cc": [[es.enter_context(nc.semaphore(f"dc{i}")), 0] for i in range(4)]}


class Phase:
    def __init__(self, nc, name, n_dma_sems=60):
        self.nc = nc
        self.name = name
        self.ops = {e: [] for e in ENGS}
        self.es = ExitStack()
        self.pool = _SEMPOOL[0]
        self.esem = {e: self.pool.eng[e][0] for e in ("pe", "act", "dve", "pool")}
        self.free_dma = {k: list(v) for k, v in self.pool.dma.items()}
        self.bufs = []
        self.nbuf = 0

    def sb(self, name, shape, dtype):
        return self.es.enter_context(self.nc.sbuf_tensor(f"{self.name}_{name}", list(shape), dtype))

    def ps(self, name, shape, dtype=F32):
        return self.es.enter_context(self.nc.psum_tensor(f"{self.name}_{name}", list(shape), dtype))

    def buf(self, name=None):
        self.nbuf += 1
        b = Buf(name or f"b{self.nbuf}")
        self.bufs.append(b)
        return b

    def _dep(self, o, w, kind):
        if w is o:
            return
        if w.is_dma:
            if o.is_dma and kind == "waw":
                return
            b = w.dbuf
            o.dwaits[b] = max(o.dwaits.get(b, 0), b.dma_total)
            return
        if w.eng == o.eng and not o.is_dma:
            if o.eng == "pe":
                return
            if kind != "raw":
                return
        w.signal = True
        o.waits.append(w)

    def op(self, eng, fn, reads=(), writes=(), dma=False, inc=16):
        o = Op(eng, fn, dma)
        o.inc = inc
        for b in reads:
            if b.last_w is not None:
                self._dep(o, b.last_w, "raw")
        for b in writes:
            if b.last_w is not None:
                self._dep(o, b.last_w, "waw")
            for r in b.readers:
                self._dep(o, r, "war")
        for b in reads:
            b.readers.append(o)
        for b in writes:
            b.last_w = o
            b.readers = []
        if dma:
            d = writes[0]
            kind = "cc" if inc != 16 else eng
            if d.sem is None:
                ent = self.free_dma[kind].pop()
                d.sem = ent[0]
                d.dma_total = ent[1]
                d.ent = ent
                d.kind = kind
            assert d.kind == kind, (d.name, d.kind, kind)
            d.dma_total += inc
            d.ent[1] = d.dma_total
            o.dbuf = d
        self.ops[eng].append(o)
        return o

    def dma(self, out, in_, reads=(), writes=(), eng="sp", **kw):
        return self.op(eng, lambda e: e.dma_start(out=out, in_=in_, **kw), reads=reads, writes=writes, dma=True)

    def emit(self):
        nc = self.nc
        fin_d = {}
        for b in self.bufs:
            if b.sem is not None:
                fin_d[b] = b.dma_total
        lasts = []
        for e in ("pe", "act", "dve", "pool"):
            cands = [o for o in self.ops[e] if not o.is_dma]
            if cands:
                cands[-1].signal = True
                lasts.append(cands[-1])
        for e in ENGS:
            fin = Op(e, None, False)
            fin.dwaits = dict(fin_d)
            fin.waits = list(lasts)
            self.ops[e].append(fin)
        for e in ("pe", "act", "dve", "pool"):
            n = self.pool.eng[e][1]
            for o in self.ops[e]:
                if o.signal and not o.is_dma:
                    n += 1
                    o.sigval = n
            self.pool.eng[e][1] = n
        with nc.Block() as block:
            for e in ENGS:

                def body(eng, e=e):
                    seen = {}
                    for o in self.ops[e]:
                        req = {}
                        for w in o.waits:
                            s = self.esem[w.eng]
                            if req.get(s, 0) < w.sigval:
                                req[s] = w.sigval
                        for b, tot in o.dwaits.items():
                            if req.get(b.sem, 0) < tot:
                                req[b.sem] = tot
                        for s, v in req.items():
                            if seen.get(s, 0) < v:
                                eng.wait_ge(s, v)
                                seen[s] = v
                        if o.fn is None:
                            continue
                        ins = o.fn(eng)
                        if o.is_dma:
                            ins.then_inc(o.dbuf.sem, o.inc)
                        elif o.signal:
                            ins.then_inc(self.esem[e], 1)

                getattr(block, BLOCKNAME[e])(body)
        self.es.close()


class Rot:
    def __init__(self, P, name, n, shape, dtype, psum=False):
        self.items = []
        for i in range(n):
            t = P.ps(f"{name}{i}", shape, dtype) if psum else P.sb(f"{name}{i}", shape, dtype)
            self.items.append((t, P.buf(f"{name}{i}")))
        self.i = 0

    def next(self):
        it = self.items[self.i % len(self.items)]
        self.i += 1
        return it


def bcast_rows(handle, n, offset=0, parts=128):
    return bass.AP(handle, offset, [[0, parts], [1, n]])


def make_ident(P, name="ident"):
    ident = P.sb(name, [128, 128], BF16)
    b = P.buf(name)

    P.op("pool", lambda e: e.memset(ident[:], 1.0), writes=[b])
    P.op("pool", lambda e: e.affine_select(out=ident[:], in_=ident[:], pattern=[[-1, 128]], compare_op=ALU.is_ge, fill=0.0,
                                           base=0, channel_multiplier=1), reads=[b], writes=[b])
    P.op("pool", lambda e: e.affine_select(out=ident[:], in_=ident[:], pattern=[[1, 128]], compare_op=ALU.is_ge, fill=0.0,
                                           base=0, channel_multiplier=-1), reads=[b], writes=[b])
    return ident, b


def eps_tile(P):
    t = P.sb("epsc", [128, 1], F32)
    b = P.buf("epsc")
    P.op("pool", lambda e: e.memset(t[:], EPS), writes=[b])
    return t, b


class NormT:
    def __init__(self, P, ident, ident_b, gains, nbuf=2):
        self.P = P
        self.ident, self.ident_b = ident, ident_b
        self.g = {}
        for k, (h, off) in gains.items():
            t = P.sb(f"gain_{k}", [128, D], F32)
            b = P.buf(f"gain_{k}")
            P.dma(t[:], bcast_rows(h, D, off), writes=[b])
            self.g[k] = (t, b)
        self.sq = Rot(P, "nt_sq", nbuf, [128, D], BF16)
        self.st = Rot(P, "nt_st", 4, [128, 4], F32)
        self.hb = Rot(P, "nt_hb", nbuf, [128, D], BF16)
        self.pt = Rot(P, "nt_pt", 1, [128, 8, 128], BF16, psum=True)
        self.eps, self.epsb = eps_tile(P)

    def rstd(self, xt, xb, width=D):
        P = self.P
        sq, sqb = self.sq.next()
        st, stb = self.st.next()
        P.op("act", lambda e: e.activation(out=sq[:, 0:width], in_=xt, func=AF.Square, accum_out=st[:, 0:1]),
             reads=[xb], writes=[sqb, stb])
        P.op("act", lambda e: e.activation(out=st[:, 1:2], in_=st[:, 0:1], func=AF.Sqrt, bias=self.eps[:, 0:1], scale=1.0 / width),
             reads=[stb, self.epsb], writes=[stb])
        P.op("dve", lambda e: e.reciprocal(out=st[:, 2:3], in_=st[:, 1:2]), reads=[stb], writes=[stb])
        return st, stb

    def run(self, xt, xb, gain, hT, hTb, col0):
        P = self.P
        gt, gb = self.g[gain]
        st, stb = self.rstd(xt, xb)
        hb, hbb = self.hb.next()
        P.op("dve", lambda e: e.scalar_tensor_tensor(out=hb[:], in0=xt, scalar=st[:, 2:3], in1=gt[:],
                                                     op0=ALU.mult, op1=ALU.mult), reads=[xb, stb, gb], writes=[hbb])
        pt, ptb = self.pt.next()

        def tr(e):
            ins = None
            for c in range(8):
                ins = e.transpose(out=pt[:, c, :], in_=hb[:, c * 128:(c + 1) * 128], identity=self.ident[:])
            return ins

        P.op("pe", tr, reads=[hbb, self.ident_b], writes=[ptb])
        P.op("act", lambda e: e.copy(out=hT[:, :, col0:col0 + 128], in_=pt[:]), reads=[ptb], writes=[hTb])


WCOLS = 1024


def load_w(P, name, handle, kchunks, ncols):
    t = P.sb(name, [128, kchunks, ncols], BF16)
    b = P.buf(name)
    src = handle.ap().rearrange("(c p) n -> p c n", p=128)
    for c0 in range(kchunks):
        for n0 in range(0, ncols, WCOLS):
            n1 = min(ncols, n0 + WCOLS)
            P.dma(t[:, c0, n0:n1], src[:, c0, n0:n1], writes=[b], eng="pool")
    return t, b


def mm_group(P, ps, psb, w, wb, kch, wc0, m, rhs, rhsb, rhs_slice, n):
    def f(e):
        ins = None
        for k in range(kch):
            ins = e.matmul(ps[0:m, 0:n], lhsT=w[:, k, wc0:wc0 + m], rhs=rhs[:, k, rhs_slice], start=(k == 0),
                           stop=(k == kch - 1))
        return ins

    P.op("pe", f, reads=[wb, rhsb], writes=[psb])


def mm_group_tm(P, ps, psb, lhs, lhsb, lhs_slice, kch, w, wb, wc0, n):
    def f(e):
        ins = None
        for k in range(kch):
            ins = e.matmul(ps[:, 0:n], lhsT=lhs[:, k, lhs_slice], rhs=w[:, k, wc0:wc0 + n], start=(k == 0),
                           stop=(k == kch - 1))
        return ins

    P.op("pe", f, reads=[lhsb, wb], writes=[psb])


LOOK = 2


def run_pipelined(iters, look):
    n = len(iters)
    for i in range(n + look):
        if i < n:
            iters[i][0]()
        if i - look >= 0:
            it = iters[i - look]
            it[1]()
            if it[2] is not None:
                it[2]()


def attn_fm(nc, name, S, SH, heads, ncomp, scale, look, n_s, n_p, setup, finish, den_pe=()):
    P = Phase(nc, name)
    NT = S // 128
    nq = 512
    NQB = SH // nq
    C = setup(P)
    ps_s = Rot(P, "ps_s", n_s, [128, 512], F32, psum=True)
    ps_o = [Rot(P, f"ps_o{c}", (2 if ncomp == 1 else 1), [128, 512], F32, psum=True) for c in range(ncomp)]
    pT = Rot(P, "pT", n_p, [128, 512], BF16)
    nacc = 2
    accr = [Rot(P, f"acc{i}", 2, [128, 512], F32) for i in range(nacc)]
    pdr = {c: Rot(P, f"pd{c}", 1, [128, 512], F32, psum=True) for c in den_pe}
    for hi, head in enumerate(heads):
        ctx = head["load"](P, hi % 2)
        iters = []
        for qb in range(NQB):
            po = [ps_o[c].next() for c in range(ncomp)]
            acc = [accr[i].next() for i in range(nacc)]
            pd = {c: pdr[c].next() for c in den_pe}
            for kt in range(NT):
                sl = [ps_s.next() for c in range(ncomp)]
                pl = [pT.next() for c in range(ncomp)]

                def rec_qk(sl=sl, pl=pl, kt=kt, qb=qb, ctx=ctx, head=head):
                    for c in range(ncomp):
                        s_, sb_ = sl[c]
                        P.op("pe", lambda e, s_=s_, c=c: head["qk"](e, ctx, s_, c, kt, qb), reads=ctx["kq_bufs"], writes=[sb_])
                    for c in range(ncomp):
                        s_, sb_ = sl[c]
                        p_, pb_ = pl[c]
                        P.op("act", lambda e, s_=s_, p_=p_: e.activation(out=p_[:], in_=s_[:], func=AF.Exp, scale=scale),
                             reads=[sb_], writes=[pb_])

                def rec_pv(pl=pl, kt=kt, po=po, acc=acc, ctx=ctx, pd=pd):
                    vt, vb = ctx["v"]
                    for c in range(ncomp):
                        p_, pb_ = pl[c]
                        o_, ob_ = po[c]
                        P.op("pe", lambda e, p_=p_, o_=o_: e.matmul(o_[:], lhsT=vt[:, kt, :], rhs=p_[:], start=(kt == 0),
                                                                     stop=(kt == NT - 1)), reads=[pb_, vb], writes=[ob_])
                    for c in den_pe:
                        p_, pb_ = pl[c]
                        d_, db_ = pd[c]
                        P.op("pe", lambda e, p_=p_, d_=d_: e.matmul(d_[:], lhsT=C["ones16"][:], rhs=p_[:], start=(kt == 0),
                                                                     stop=(kt == NT - 1)), reads=[pb_, C["ones16b"]], writes=[db_])
                    for c in range(ncomp):
                        if c in den_pe:
                            continue
                        p_, pb_ = pl[c]
                        if ncomp == 2:
                            ai, first = c, (kt == 0)
                        else:
                            ai, first = 0, (kt == 0)
                        a_, ab_ = acc[ai]
                        eng = "dve" if ai == 0 else "pool"
                        if first:
                            P.op(eng, lambda e, a_=a_, p_=p_: e.tensor_copy(out=a_[:], in_=p_[:]), reads=[pb_], writes=[ab_])
                        else:
                            P.op(eng, lambda e, a_=a_, p_=p_: e.tensor_tensor(out=a_[:], in0=a_[:], in1=p_[:], op=ALU.add),
                                 reads=[pb_, ab_], writes=[ab_])

                post = None
                if kt == NT - 1:
                    post = (lambda hi=hi, qb=qb, po=po, acc=acc, pd=pd: finish(P, C, hi, qb, po, acc, pd))
                iters.append((rec_qk, rec_pv, post))
        run_pipelined(iters, look)
    P.emit()


def build(S, npair_groups, debug=False, upto=None):
    SH = S // 2
    NBA = S // 512
    NBO = SH // 512
    NT = S // 128
    NCH = S // 64
    nc = bass.Bass("TRN2", target_bir_lowering=False)
    _es = ExitStack()
    _SEMPOOL[0] = SemPool(nc, _es)

    def din(name, shape):
        return nc.dram_tensor(name, list(shape), F32, kind="ExternalInput")

    def scr(name, shape, dt, dbg=False):
        if dbg and debug:
            return nc.dram_tensor(name, list(shape), dt, kind="ExternalOutput")
        return nc.dram_tensor(name, list(shape), dt)

    x_in = din("x", [S, D])
    w_a = din("w_a", [D, 23 * 128 + 512])
    w_uq = din("w_uq", [384, 1024])
    w_ukv = din("w_ukv", [256, 1024])
    w_o0 = din("w_o0", [D, D])
    w_c = din("w_c", [D, 16 * 128 + 16 * 128 + D])
    w_o1 = din("w_o1", [D, D])
    ffn_g = [din(f"ffn_g{l}", [D, FH]) for l in range(2)]
    ffn_u = [din(f"ffn_u{l}", [D, FH]) for l in range(2)]
    ffn_d = [din(f"ffn_d{l}", [FH, D]) for l in range(2)]
    cols_in = din("cols", [128, 32])
    rows_in = din("rows", [8, D])
    out = nc.dram_tensor("out", [SH, D], F32, kind="ExternalOutput")

    ROPEC = din("ropec_t", [128, S])
    ROPES = din("ropes_t", [128, S])
    QT0 = scr("QT0", [2, 4, 128, SH], BF16)
    KT0 = scr("KT0", [2, 4, 128, SH], BF16)
    KH0 = scr("KH0", [2, 4, S, 128], BF16)
    DEC0 = scr("DEC0", [2, 4, 128, NCH], F32)
    V0 = scr("V0", [S, 512], BF16)
    GATE0 = scr("GATE0", [4, 128, SH], F32)
    OFW = scr("OFW", [4, 128, SH], F32)
    QN = scr("QN", [4, 128, SH], BF16)
    QR = scr("QR", [2, 128, SH], BF16)
    KN = scr("KN", [4, 128, S], BF16)
    KR = scr("KR", [128, S], BF16)
    VBm = scr("VBm", [4, S, 128], BF16)
    MIXT = scr("MIXT", [8, 128, SH], BF16, dbg=True)
    X1A = scr("X1A", [SH, D], F32)
    X1 = scr("X1", [SH, D], F32, dbg=True)
    Q1T = scr("Q1T", [8, 128, SH], BF16)
    VR = min(512, SH)
    NVC = SH // VR
    K1S = [scr(f"K1S{h}", [128, SH], BF16) for h in range(8)]
    V1S = [scr(f"V1S{c}", [VR, D], BF16) for c in range(NVC)]
    K1G = [scr(f"K1G{h}", [2 * 128, SH], BF16) for h in range(8)]
    V1G = [scr(f"V1G{c}", [2 * VR, D], BF16) for c in range(NVC)]
    MIXT1 = scr("MIXT1", [8, 128, SH], BF16, dbg=True)
    X2A = scr("X2A", [SH, D], F32)

    P = Phase(nc, "pA")
    ident, identb = make_ident(P)
    wa, wab = load_w(P, "wa", w_a, 8, 23 * 128 + 512)
    wuq, wuqb = load_w(P, "wuq", w_uq, 3, 1024)
    wukv, wukvb = load_w(P, "wukv", w_ukv, 2, 1024)
    cols = P.sb("cols", [128, 32], F32); colsb = P.buf("cols")
    P.dma(cols[:], cols_in.ap(), writes=[colsb])
    lbt = P.sb("lbt", [128, 32], F32); lbb = P.buf("lbt")
    P.op("dve", lambda e: e.tensor_tensor(out=lbt[:, 0:8], in0=cols[:, 0:8], in1=cols[:, 8:16], op=ALU.subtract),
         reads=[colsb], writes=[lbb])
    P.op("act", lambda e: e.activation(out=lbt[:, 8:16], in_=lbt[:, 0:8], func=AF.Sigmoid), reads=[lbb], writes=[lbb])
    P.op("dve", lambda e: e.tensor_scalar(out=lbt[:, 16:24], in0=lbt[:, 8:16], scalar1=-1.0, scalar2=1.0, op0=ALU.mult,
                                          op1=ALU.add), reads=[lbb], writes=[lbb])
    P.op("dve", lambda e: e.tensor_scalar(out=lbt[:, 24:32], in0=lbt[:, 16:24], scalar1=-1.0, scalar2=None, op0=ALU.mult),
         reads=[lbb], writes=[lbb])
    ones = P.sb("ones", [128, 128], BF16); onesb = P.buf("ones")
    P.op("pool", lambda e: e.memset(ones[:], 1.0), writes=[onesb])
    rmask = P.sb("rmask", [128, 8, 64], F32); rmb = P.buf("rmask")

    P.op("pool", lambda e: e.memset(rmask[:], 1.0), writes=[rmb])
    P.op("pool", lambda e: e.memset(rmask[:, :, 0:1], 0.0), reads=[rmb], writes=[rmb])
    NA = NormT(P, ident, identb, {"attn0": (rows_in, 0)})
    xr = Rot(P, "x", 3, [128, D], F32)
    hTr = Rot(P, "hT", 2, [128, 8, 512], BF16)
    psr = Rot(P, "ps", 4, [128, 512], F32, psum=True)
    ps_ss = Rot(P, "ps_ss", 1, [128, 512], F32, psum=True)
    ps_tm = Rot(P, "ps_tm", 1, [128, 512], F32, psum=True)
    ps_kt = Rot(P, "ps_kt", 1, [128, 4, 128], BF16, psum=True)
    f32r = Rot(P, "f32", 12, [128, 512], F32)
    b16r = Rot(P, "b16", 8, [128, 512], BF16)
    sgr = Rot(P, "sgr", 8, [128, 512], F32)
    qsr = Rot(P, "qs", 5, [128, 512], F32)
    rtab = Rot(P, "rtab", 4, [128, 512], F32)
    lat = Rot(P, "lat", 2, [128, 3, 512], F32)
    latn = Rot(P, "latn", 2, [128, 3, 512], BF16)
    kh16 = Rot(P, "kh16", 3, [128, 4, 128], BF16)
    v16 = Rot(P, "v16", 3, [128, 512], BF16)
    decr = Rot(P, "dec", 4, [128, 8], F32)
    D_ = {k: P.buf(k) for k in "QT0 KT0 KH0 DEC0 V0 GATE0 QN QR KN KR VBm".split()}

    for t in range(NBA):
        own = t < NBO
        c0 = t * 512
        sl = slice(c0, c0 + 512)
        hT, hTb = hTr.next()
        for j in range(4):
            xt, xb = xr.next()
            P.dma(xt[:], x_in.ap()[c0 + j * 128:c0 + (j + 1) * 128, :], writes=[xb])
            NA.run(xt[:], xb, "attn0", hT, hTb, j * 128)
        ct, ctb = rtab.next(); st_, stb_ = rtab.next()
        P.dma(ct[:], ROPEC.ap()[:, sl], writes=[ctb])
        P.dma(st_[:], ROPES.ap()[:, sl], writes=[stb_])

        def fm(group, ps, psb):
            mm_group(P, ps, psb, wa, wab, 8, group * 128, 128, hT, hTb, slice(0, 512), 512)

        for j in range(4):
            ps, psb = ps_tm.next()
            mm_group_tm(P, ps, psb, hT, hTb, slice(j * 128, (j + 1) * 128), 8, wa, wab, 23 * 128, 512)
            v, vb = v16.next()
            P.op("act", lambda e, v=v, ps=ps: e.copy(out=v[:], in_=ps[:]), reads=[psb], writes=[vb])
            P.dma(V0.ap()[c0 + j * 128:c0 + (j + 1) * 128, :], v[:], reads=[vb], writes=[D_["V0"]], eng="pool")
        qs = []
        if own:
            for h in range(4):
                ps, psb = psr.next()
                fm(h, ps, psb)
                q, qb_ = qsr.next()
                P.op("act", lambda e, q=q, ps=ps: e.activation(out=q[:], in_=ps[:], func=AF.Silu), reads=[psb], writes=[qb_])
                qs.append((q, qb_))
            for h in range(4):
                ps, psb = psr.next()
                fm(12 + h, ps, psb)
                g, gb = f32r.next()
                P.op("act", lambda e, g=g, ps=ps: e.activation(out=g[:], in_=ps[:], func=AF.Silu), reads=[psb], writes=[gb])
                P.dma(GATE0.ap()[h, :, sl], g[:], reads=[gb], writes=[D_["GATE0"]], eng="pool")
        sig_l = []
        for d in ((0, 1) if own else (1,)):
            for h in range(4):
                ps, psb = psr.next()
                fm(4 + 4 * d + h, ps, psb)
                sg, sgb = sgr.next()
                P.op("act", lambda e, sg=sg, ps=ps: e.activation(out=sg[:], in_=ps[:], func=AF.Sigmoid), reads=[psb], writes=[sgb])
                sig_l.append((d, h, sg, sgb))
        for (d, h, sg, sgb) in sig_l:
            ci = d * 4 + h
            g, gb = f32r.next()
            P.op("act", lambda e, g=g, sg=sg, ci=ci: e.activation(out=g[:], in_=sg[:], func=AF.Ln, bias=lbt[:, 8 + ci:9 + ci],
                                                                 scale=lbt[:, 16 + ci:17 + ci]), reads=[sgb, lbb], writes=[gb])
            k, kb = f32r.next()
            P.op("pool", lambda e, k=k, sg=sg, ci=ci: e.tensor_scalar(out=k[:], in0=sg[:], scalar1=lbt[:, 24 + ci:25 + ci],
                                                                     scalar2=lbt[:, 16 + ci:17 + ci], op0=ALU.mult, op1=ALU.add),
                 reads=[sgb, lbb], writes=[kb])
            b_, bb = f32r.next()
            P.op("dve", lambda e, b_=b_, g=g: e.tensor_tensor_scan(out=b_[:], data0=rmask[:].rearrange("p a b -> p (a b)"),
                                                                  data1=g[:], initial=0.0, op0=ALU.mult, op1=ALU.add),
                 reads=[gb, rmb], writes=[bb])
            b3 = b_[:].rearrange("p (a b) -> p a b", b=64)
            dl, dlb = f32r.next()
            P.op("pool", lambda e, dl=dl, b3=b3: e.tensor_tensor(out=dl[:].rearrange("p (a b) -> p a b", b=64), in0=b3,
                                                               in1=b3[:, :, 63:64].to_broadcast([128, 8, 64]), op=ALU.subtract),
                 reads=[bb], writes=[dlb])
            dec, decb = decr.next()
            P.op("act", lambda e, dec=dec, b3=b3: e.activation(out=dec[:], in_=b3[:, :, 63], func=AF.Exp), reads=[bb], writes=[decb])
            P.dma(DEC0.ap()[d, h, :, t * 8:(t + 1) * 8], dec[:], reads=[decb], writes=[D_["DEC0"]], eng="pool")
            if d == 0:
                bq, bqb = b_, bb
                ex, exb = dl, dlb
                ex_scale = -1.0
            else:
                bq, bqb = f32r.next()
                P.op("pool", lambda e, bq=bq, g=g, dl=dl: e.tensor_tensor(out=bq[:], in0=g[:], in1=dl[:], op=ALU.subtract),
                     reads=[gb, dlb], writes=[bqb])
                ex, exb = f32r.next()
                P.op("pool", lambda e, ex=ex, b_=b_, g=g: e.tensor_tensor(out=ex[:], in0=b_[:], in1=g[:], op=ALU.subtract),
                     reads=[bb, gb], writes=[exb])
                ex_scale = 1.0
            ek, ekb = f32r.next()
            P.op("act", lambda e, ek=ek, ex=ex, ex_scale=ex_scale: e.activation(out=ek[:], in_=ex[:], func=AF.Exp, scale=ex_scale),
                 reads=[exb], writes=[ekb])
            kh, khb = b16r.next()
            P.op("dve", lambda e, kh=kh, k=k, ek=ek: e.tensor_tensor(out=kh[:], in0=k[:], in1=ek[:], op=ALU.mult),
                 reads=[kb, ekb], writes=[khb])
            pk, pkb = ps_kt.next()

            def trk(e, pk=pk, kh=kh):
                ins = None
                for j in range(4):
                    ins = e.transpose(out=pk[:, j, :], in_=kh[:, j * 128:(j + 1) * 128], identity=ident[:])
                return ins

            P.op("pe", trk, reads=[khb, identb], writes=[pkb])
            kt_, ktb = kh16.next()
            P.op("act", lambda e, kt_=kt_, pk=pk: e.copy(out=kt_[:], in_=pk[:]), reads=[pkb], writes=[ktb])
            P.dma(KH0.ap()[d, h, sl, :].rearrange("(j p) k -> p j k", p=128), kt_[:], reads=[ktb], writes=[D_["KH0"]], eng="pool")
            if own:
                eq, eqb = f32r.next()
                P.op("act", lambda e, eq=eq, bq=bq: e.activation(out=eq[:], in_=bq[:], func=AF.Exp), reads=[bqb], writes=[eqb])
                en, enb = f32r.next()
                P.op("act", lambda e, en=en, bq=bq: e.activation(out=en[:], in_=bq[:], func=AF.Exp, scale=-1.0), reads=[bqb], writes=[enb])
                qt, qtb = b16r.next()
                q, qb_ = qs[h]
                P.op("dve", lambda e, qt=qt, q=q, eq=eq: e.scalar_tensor_tensor(out=qt[:], in0=q[:], scalar=128 ** -0.5, in1=eq[:],
                                                                              op0=ALU.mult, op1=ALU.mult), reads=[qb_, eqb], writes=[qtb])
                P.dma(QT0.ap()[d, h, :, sl], qt[:], reads=[qtb], writes=[D_["QT0"]], eng="pool")
                kt2, kt2b = b16r.next()
                P.op("pool", lambda e, kt2=kt2, k=k, en=en: e.tensor_tensor(out=kt2[:], in0=k[:], in1=en[:], op=ALU.mult),
                     reads=[kb, enb], writes=[kt2b])
                P.dma(KT0.ap()[d, h, :, sl], kt2[:], reads=[kt2b], writes=[D_["KT0"]], eng="pool")
        lat_jobs = ([(16, 3, 20)] if own else []) + [(19, 2, 23)]
        for (g0, ng, gcol) in lat_jobs:
            lt, ltb = lat.next()
            for i in range(ng):
                ps, psb = psr.next()
                fm(g0 + i, ps, psb)
                P.op("act", lambda e, lt=lt, ps=ps, i=i: e.copy(out=lt[:, i, :], in_=ps[:]), reads=[psb], writes=[ltb])
            ss, ssb = ps_ss.next()
            sqs = []
            for i in range(ng):
                sq, sqb = b16r.next()
                P.op("pool", lambda e, sq=sq, lt=lt, i=i: e.tensor_tensor(out=sq[:], in0=lt[:, i, :], in1=lt[:, i, :], op=ALU.mult),
                     reads=[ltb], writes=[sqb])
                sqs.append((sq, sqb))

            def ssm(e, ss=ss, sqs=sqs):
                ins = None
                for i, (sq, _) in enumerate(sqs):
                    ins = e.matmul(ss[:], lhsT=ones[:], rhs=sq[:], start=(i == 0), stop=(i == len(sqs) - 1))
                return ins

            P.op("pe", ssm, reads=[onesb] + [b for _, b in sqs], writes=[ssb])
            rs, rsb = f32r.next()
            P.op("act", lambda e, rs=rs, ss=ss, ng=ng: e.activation(out=rs[:], in_=ss[:], func=AF.Sqrt, bias=NA.eps[:, 0:1],
                                                                   scale=1.0 / (ng * 128)), reads=[ssb, NA.epsb], writes=[rsb])
            P.op("dve", lambda e, rs=rs: e.reciprocal(out=rs[:], in_=rs[:]), reads=[rsb], writes=[rsb])
            ln, lnb = latn.next()
            for i in range(ng):
                P.op("dve", lambda e, ln=ln, lt=lt, rs=rs, i=i, gcol=gcol: e.scalar_tensor_tensor(
                    out=ln[:, i, :], in0=lt[:, i, :], scalar=cols[:, gcol + i:gcol + i + 1], in1=rs[:], op0=ALU.mult, op1=ALU.mult),
                    reads=[ltb, rsb, colsb], writes=[lnb])
            if g0 == 16:
                for h in range(4):
                    ps, psb = psr.next()
                    mm_group(P, ps, psb, wuq, wuqb, 3, h * 128, 128, ln, lnb, slice(0, 512), 512)
                    o, ob = b16r.next()
                    P.op("act", lambda e, o=o, ps=ps: e.copy(out=o[:], in_=ps[:]), reads=[psb], writes=[ob])
                    P.dma(QN.ap()[h, :, sl], o[:], reads=[ob], writes=[D_["QN"]], eng="pool")
                for pr in range(2):
                    ps, psb = psr.next()
                    mm_group(P, ps, psb, wuq, wuqb, 3, (4 + pr) * 128, 128, ln, lnb, slice(0, 512), 512)
                    ps2, ps2b = psr.next()
                    mm_group(P, ps2, ps2b, wuq, wuqb, 3, (6 + pr) * 128, 128, ln, lnb, slice(0, 512), 512)
                    t1, t1b = f32r.next()
                    P.op("dve", lambda e, t1=t1, ps=ps, ct=ct: e.tensor_tensor(out=t1[:], in0=ps[:], in1=ct[:], op=ALU.mult),
                         reads=[psb, ctb], writes=[t1b])
                    t2, t2b = f32r.next()
                    P.op("dve", lambda e, t2=t2, ps2=ps2, st_=st_: e.tensor_tensor(out=t2[:], in0=ps2[:], in1=st_[:], op=ALU.mult),
                         reads=[ps2b, stb_], writes=[t2b])
                    o, ob = b16r.next()
                    P.op("pool", lambda e, o=o, t1=t1, t2=t2: e.tensor_tensor(out=o[:], in0=t1[:], in1=t2[:], op=ALU.add),
                         reads=[t1b, t2b], writes=[ob])
                    P.dma(QR.ap()[pr, :, sl], o[:], reads=[ob], writes=[D_["QR"]], eng="pool")
            else:
                for h in range(4):
                    ps, psb = psr.next()
                    mm_group(P, ps, psb, wukv, wukvb, 2, h * 128, 128, ln, lnb, slice(0, 512), 512)
                    o, ob = b16r.next()
                    P.op("act", lambda e, o=o, ps=ps: e.copy(out=o[:], in_=ps[:]), reads=[psb], writes=[ob])
                    P.dma(KN.ap()[h, :, sl], o[:], reads=[ob], writes=[D_["KN"]], eng="pool")
                for j in range(4):
                    ps, psb = ps_tm.next()
                    mm_group_tm(P, ps, psb, ln, lnb, slice(j * 128, (j + 1) * 128), 2, wukv, wukvb, 512, 512)
                    v, vb = v16.next()
                    P.op("act", lambda e, v=v, ps=ps: e.copy(out=v[:], in_=ps[:]), reads=[psb], writes=[vb])
                    P.dma(VBm.ap()[:, c0 + j * 128:c0 + (j + 1) * 128, :].rearrange("h p d -> p h d"),
                          v[:].rearrange("p (h d) -> p h d", d=128), reads=[vb], writes=[D_["VBm"]], eng="pool")
        ps, psb = psr.next()
        fm(21, ps, psb)
        ps2, ps2b = psr.next()
        fm(22, ps2, ps2b)
        t1, t1b = f32r.next()
        P.op("dve", lambda e, t1=t1, ps=ps, ct=ct: e.tensor_tensor(out=t1[:], in0=ps[:], in1=ct[:], op=ALU.mult), reads=[psb, ctb], writes=[t1b])
        t2, t2b = f32r.next()
        P.op("dve", lambda e, t2=t2, ps2=ps2, st_=st_: e.tensor_tensor(out=t2[:], in0=ps2[:], in1=st_[:], op=ALU.mult), reads=[ps2b, stb_], writes=[t2b])
        o, ob = b16r.next()
        P.op("pool", lambda e, o=o, t1=t1, t2=t2: e.tensor_tensor(out=o[:], in0=t1[:], in1=t2[:], op=ALU.add), reads=[t1b, t2b], writes=[ob])
        P.dma(KR.ap()[:, sl], o[:], reads=[ob], writes=[D_["KR"]], eng="pool")
    P.emit()
    if upto == "A":
        return nc

    P = Phase(nc, "pB")
    cols = P.sb("cols", [128, 32], F32); colsb = P.buf("cols")
    P.dma(cols[:], cols_in.ap(), writes=[colsb])
    ones = P.sb("ones", [128, 128], BF16); onesb = P.buf("ones")
    P.op("pool", lambda e: e.memset(ones[:], 1.0), writes=[onesb])
    epsB, epsBb = eps_tile(P)
    masks = []
    for d in range(2):
        m = P.sb(f"mask{d}", [128, 128], F32); mb = P.buf(f"mask{d}")

        P.op("pool", lambda e, m=m: e.memset(m[:], 1.0), writes=[mb])
        if d == 0:
            P.op("pool", lambda e, m=m: e.affine_select(out=m[:], in_=m[:], pattern=[[1, 128]], compare_op=ALU.is_ge, fill=0.0,
                                                       base=0, channel_multiplier=-1), reads=[mb], writes=[mb])
            P.op("pool", lambda e, m=m: e.memset(m[0:64, 64:128], 0.0), reads=[mb], writes=[mb])
        else:
            P.op("pool", lambda e, m=m: e.affine_select(out=m[:], in_=m[:], pattern=[[-1, 128]], compare_op=ALU.is_ge, fill=0.0,
                                                       base=0, channel_multiplier=1), reads=[mb], writes=[mb])
            P.op("pool", lambda e, m=m: e.memset(m[64:128, 0:64], 0.0), reads=[mb], writes=[mb])
        m2 = P.sb(f"maskf{d}", [128, 128], F32); m2b = P.buf(f"maskf{d}")
        P.op("dve", lambda e, m=m, m2=m2: e.tensor_copy(out=m2[:], in_=m[:]), reads=[mb], writes=[m2b])
        masks.append((m2, m2b))
    St = P.sb("St", [128, 4, 128], F32); Sb = [P.buf(f"S{h}") for h in range(4)]
    S16 = P.sb("S16", [128, 4, 128], BF16); S16b = P.buf("S16")
    qtr = Rot(P, "qt", 2, [128, 4, 512], BF16)
    ktr = Rot(P, "kt", 2, [128, 4, 512], BF16)
    khr = Rot(P, "kh", 2, [128, 4, 4, 128], BF16)
    vr = Rot(P, "v", 2, [128, 4, 512], BF16)
    dcr = Rot(P, "dc", 2, [128, 4, 8], F32)
    ofr = Rot(P, "of", 2, [128, 4, 512], F32)
    gtr = Rot(P, "gt", 2, [128, 4, 512], F32)
    psO = [Rot(P, f"psO{h}", 1, [128, 512], F32, psum=True) for h in range(4)]
    psXr = Rot(P, "psX", 2, [128, 4, 128], F32, psum=True)
    psSr = Rot(P, "psS", 1, [128, 4, 128], F32, psum=True)
    ps_ss = Rot(P, "ps_ss", 1, [128, 512], F32, psum=True)
    atr = Rot(P, "at", 2, [128, 4, 128], BF16)
    sc32r = Rot(P, "sc32", 2, [128, 4, 128], F32)
    ev = Rot(P, "ev", 6, [128, 512], F32)
    e16 = Rot(P, "e16", 4, [128, 512], BF16)
    DOF = P.buf("OFW"); DMX = P.buf("MIXT")

    def reset_state():
        for h in range(4):
            P.op("pool", lambda e, h=h: e.memset(St[:, h, :], 0.0), writes=[Sb[h]])
        P.op("pool", lambda e: e.memset(S16[:], 0.0), writes=[S16b])

    def hgrn_block(d, t, full, final):
        c0 = t * 512
        sl = slice(c0, c0 + 512)
        kh, khb = khr.next()
        for h in range(4):
            P.dma(kh[:, h, :, :], KH0.ap()[d, h, sl, :].rearrange("(j p) k -> p j k", p=128), writes=[khb])
        v, vb = vr.next()
        P.dma(v[:], V0.ap()[sl, :].rearrange("(j p) c -> p j c", p=128), writes=[vb])
        dc, dcb = dcr.next()
        P.dma(dc[:], DEC0.ap()[d, :, :, t * 8:(t + 1) * 8].rearrange("h p c -> p h c"), writes=[dcb])
        if full:
            qt, qtb = qtr.next()
            P.dma(qt[:], QT0.ap()[d, :, :, sl].rearrange("h p s -> p h s"), writes=[qtb])
            kt, ktb = ktr.next()
            P.dma(kt[:], KT0.ap()[d, :, :, sl].rearrange("h p s -> p h s"), writes=[ktb])
            pso = [psO[h].next() for h in range(4)]
        if final:
            of, ofb = ofr.next()
            P.dma(of[:], OFW.ap()[:, :, sl].rearrange("h p s -> p h s"), reads=[DOF], writes=[ofb])
            gt, gtb = gtr.next()
            P.dma(gt[:], GATE0.ap()[:, :, sl].rearrange("h p s -> p h s"), writes=[gtb])
        m, mb = masks[d]
        tiles = range(4) if d == 0 else range(3, -1, -1)
        for j in tiles:
            chunks = (0, 1) if d == 0 else (1, 0)
            if full:
                pS, pSb = psSr.next()

                def sc(e, pS=pS, j=j):
                    ins = None
                    for h in range(4):
                        ins = e.matmul(pS[:, h, :], lhsT=kt[:, h, j * 128:(j + 1) * 128], rhs=qt[:, h, j * 128:(j + 1) * 128],
                                       start=True, stop=True)
                    return ins

                P.op("pe", sc, reads=[ktb, qtb], writes=[pSb])
                sc32, sc32b = sc32r.next()
                P.op("act", lambda e, sc32=sc32, pS=pS: e.copy(out=sc32[:], in_=pS[:]), reads=[pSb], writes=[sc32b])
                at, atb = atr.next()
                P.op("pool", lambda e, at=at, sc32=sc32: e.tensor_tensor(out=at[:], in0=sc32[:],
                                                                       in1=m[:].unsqueeze(1).to_broadcast([128, 4, 128]), op=ALU.mult),
                     reads=[sc32b, mb], writes=[atb])
            for c in chunks:
                pr = slice(64 * c, 64 * c + 64)
                co = j * 128 + c * 64
                cidx = j * 2 + c
                if full:
                    for h in range(4):
                        po, pob = pso[h]

                        def om(e, at=at, po=po, h=h, co=co, c=c, j=j):
                            e.matmul(po[:, co:co + 64], lhsT=v[:, j, h * 128:(h + 1) * 128], rhs=at[:, h, c * 64:(c + 1) * 64],
                                     start=True, stop=False)
                            return e.matmul(po[:, co:co + 64], lhsT=S16[:, h, :], rhs=qt[:, h, co:co + 64], start=False, stop=True)

                        P.op("pe", om, reads=[vb, atb, S16b, qtb], writes=[pob])
                pX, pXb = psXr.next()

                def su(e, pX=pX, pr=pr, j=j):
                    ins = None
                    for h in range(4):
                        ins = e.matmul(pX[:, h, :], lhsT=kh[pr, h, j, :], rhs=v[pr, j, h * 128:(h + 1) * 128], start=True, stop=True)
                    return ins

                P.op("pe", su, reads=[khb, vb], writes=[pXb])
                for h in range(4):
                    P.op("dve", lambda e, pX=pX, h=h, cidx=cidx: e.scalar_tensor_tensor(
                        out=St[:, h, :], in0=St[:, h, :], scalar=dc[:, h, cidx:cidx + 1], in1=pX[:, h, :], op0=ALU.mult, op1=ALU.add),
                        reads=[Sb[h], dcb, pXb], writes=[Sb[h]])
                P.op("act", lambda e: e.copy(out=S16[:], in_=St[:]), reads=Sb, writes=[S16b])
        if full:
            for h in range(4):
                po, pob = pso[h]
                if not final:
                    o, ob = ev.next()
                    P.op("act", lambda e, o=o, po=po: e.copy(out=o[:], in_=po[:]), reads=[pob], writes=[ob])
                    P.dma(OFW.ap()[h, :, sl], o[:], reads=[ob], writes=[DOF], eng="pool")
                else:
                    tot, totb = ev.next()
                    P.op("dve", lambda e, tot=tot, po=po, h=h: e.tensor_tensor(out=tot[:], in0=po[:], in1=of[:, h, :], op=ALU.add),
                         reads=[pob, ofb], writes=[totb])
                    sq, sqb = e16.next()
                    P.op("act", lambda e, sq=sq, tot=tot: e.activation(out=sq[:], in_=tot[:], func=AF.Square), reads=[totb], writes=[sqb])
                    ss, ssb = ps_ss.next()
                    P.op("pe", lambda e, ss=ss, sq=sq: e.matmul(ss[:], lhsT=ones[:], rhs=sq[:], start=True, stop=True),
                         reads=[onesb, sqb], writes=[ssb])
                    rs, rsb = ev.next()
                    P.op("act", lambda e, rs=rs, ss=ss: e.activation(out=rs[:], in_=ss[:], func=AF.Sqrt, bias=epsB[:, 0:1], scale=1.0 / 128),
                         reads=[ssb, epsBb], writes=[rsb])
                    P.op("dve", lambda e, rs=rs: e.reciprocal(out=rs[:], in_=rs[:]), reads=[rsb], writes=[rsb])
                    P.op("pool", lambda e, tot=tot, rs=rs: e.tensor_tensor(out=tot[:], in0=tot[:], in1=rs[:], op=ALU.mult),
                         reads=[totb, rsb], writes=[totb])
                    o, ob = e16.next()
                    P.op("dve", lambda e, o=o, tot=tot, h=h: e.scalar_tensor_tensor(out=o[:], in0=tot[:], scalar=cols[:, 16 + h:17 + h],
                                                                                   in1=gt[:, h, :], op0=ALU.mult, op1=ALU.mult),
                         reads=[totb, colsb, gtb], writes=[ob])
                    P.dma(MIXT.ap()[h, :, sl], o[:], reads=[ob], writes=[DMX], eng="pool")

    reset_state()
    for t in range(NBO):
        hgrn_block(0, t, True, False)
    reset_state()
    for t in range(NBA - 1, NBO - 1, -1):
        hgrn_block(1, t, False, False)
    for t in range(NBO - 1, -1, -1):
        hgrn_block(1, t, True, True)
    P.emit()
    if upto == "B":
        return nc

    def attn_setup_common(P):
        C = dict(mx=P.buf("MIXOUT"))
        ones32 = P.sb("ones32", [128, 128], F32); o32b = P.buf("ones32")
        P.op("pool", lambda e: e.memset(ones32[:], 1.0), writes=[o32b])
        C.update(ones32=ones32, o32b=o32b, ss=Rot(P, "ss", 1, [128, 512], F32, psum=True),
                 r32=Rot(P, "r32", 4, [128, 512], F32), o16=Rot(P, "o16", 2, [128, 512], BF16))
        return C

    def mla_setup(P):
        C = attn_setup_common(P)
        C["KT"] = [(P.sb(f"K{i}", [128, S], BF16), P.buf(f"K{i}")) for i in range(2)]
        C["QT"] = [(P.sb(f"Q{i}", [128, SH], BF16), P.buf(f"Q{i}")) for i in range(2)]
        C["VT"] = (P.sb("V", [128, NT, 128], BF16), P.buf("V"))
        P.op("pool", lambda e: e.memset(C["KT"][1][0][:], 0.0), writes=[C["KT"][1][1]])
        P.op("pool", lambda e: e.memset(C["QT"][1][0][:], 0.0), writes=[C["QT"][1][1]])
        return C

    def den_matmul(P, C, accs):
        ss, ssb = C["ss"].next()

        def f(e):
            ins = None
            for i, (a_, _) in enumerate(accs):
                ins = e.matmul(ss[:], lhsT=C["ones32"][:], rhs=a_[:], start=(i == 0), stop=(i == len(accs) - 1))
            return ins

        P.op("pe", f, reads=[C["o32b"]] + [ab_ for _, ab_ in accs], writes=[ssb])
        return ss, ssb

    def mla_finish(P, C, hi, qb, po, acc, pd):
        ss, ssb = den_matmul(P, C, [acc[0]])
        r, rb = C["r32"].next()
        P.op("dve", lambda e: e.reciprocal(out=r[:], in_=ss[:]), reads=[ssb], writes=[rb])
        o_, ob_ = po[0]
        o16, o16b = C["o16"].next()
        P.op("dve", lambda e: e.tensor_tensor(out=o16[:], in0=o_[:], in1=r[:], op=ALU.mult), reads=[ob_, rb], writes=[o16b])
        P.dma(MIXT.ap()[4 + hi, :, qb * 512:(qb + 1) * 512], o16[:], reads=[o16b], writes=[C["mx"]], eng="pool")

    def mla_head(h):
        pb = 64 * (h % 2)

        def load(P, slot):
            C = mla_C[0]
            (k0, k0b), (k1, k1b) = C["KT"]
            (q0, q0b), (q1, q1b) = C["QT"]
            vt, vb = C["VT"]
            half = S // 2
            for lo, hi_ in ((0, half), (half, S)):
                P.dma(k0[:, lo:hi_], KN.ap()[h][:, lo:hi_], writes=[k0b])
                P.dma(k1[0:64, lo:hi_], KR.ap()[0:64, lo:hi_], writes=[k1b])
            P.dma(q0[:], QN.ap()[h], writes=[q0b])
            P.dma(q1[0:64, :], QR.ap()[h // 2, pb:pb + 64, :], writes=[q1b])
            vsrc = VBm.ap()[h].rearrange("(n p) d -> p n d", p=128)
            for n0 in range(0, NT, 8):
                n1 = min(NT, n0 + 8)
                P.dma(vt[:, n0:n1, :], vsrc[:, n0:n1, :], writes=[vb])
            return dict(kq_bufs=[k0b, k1b, q0b, q1b], v=(vt, vb), k0=k0, k1=k1, q0=q0, q1=q1, pb=pb)

        def qk(e, ctx, s_, c, kt, qb):
            pb_ = ctx["pb"]
            e.matmul(s_[:], lhsT=ctx["k0"][:, kt * 128:(kt + 1) * 128], rhs=ctx["q0"][:, qb * 512:(qb + 1) * 512], start=True, stop=False)
            return e.matmul(s_[:], lhsT=ctx["k1"][:, kt * 128:(kt + 1) * 128],
                            rhs=ctx["q1"][:, qb * 512:(qb + 1) * 512], start=False, stop=True)

        return dict(load=load, qk=qk)

    mla_C = [None]

    def mla_setup_wrap(P):
        mla_C[0] = mla_setup(P)
        return mla_C[0]

    attn_fm(nc, "pC", S, SH, [mla_head(h) for h in range(4)], 1, MLA_SCALE, 2, 3, 4, mla_setup_wrap, mla_finish)
    if upto == "C":
        return nc

    def outproj_phase(name, MIX, w_o, XIN, xin_is_input, XOUT):
        P = Phase(nc, name)
        wo, wob = load_w(P, "wo", w_o, 8, D)
        mxr = Rot(P, "mx", 2, [128, 8, 512], BF16)
        xr = Rot(P, "x", 3, [128, D], F32)
        psr = Rot(P, "ps", 4, [128, 512], F32, psum=True)
        XO = P.buf("XO")
        for t in range(NBO):
            sl = slice(t * 512, (t + 1) * 512)
            mx, mxb = mxr.next()
            P.dma(mx[:], MIX.ap()[:, :, sl].rearrange("c p s -> p c s"), writes=[mxb])
            for j in range(4):
                r0 = t * 512 + j * 128
                xt, xb = xr.next()
                P.dma(xt[:], XIN.ap()[r0:r0 + 128, :], writes=[xb])
                for n in range(2):
                    ps, psb = psr.next()
                    mm_group_tm(P, ps, psb, mx, mxb, slice(j * 128, (j + 1) * 128), 8, wo, wob, n * 512, 512)
                    P.op("dve", lambda e, xt=xt, ps=ps, n=n: e.tensor_tensor(out=xt[:, n * 512:(n + 1) * 512], in0=xt[:, n * 512:(n + 1) * 512],
                                                                           in1=ps[:], op=ALU.add), reads=[psb, xb], writes=[xb])
                P.dma(XOUT.ap()[r0:r0 + 128, :], xt[:], reads=[xb], writes=[XO], eng="pool")
        P.emit()

    def ffn_phase(name, l, XIN, XOUT, final):
        P = Phase(nc, name)
        ident, identb = make_ident(P)
        wg, wgb = load_w(P, "wg", ffn_g[l], 8, FH)
        wu, wub = load_w(P, "wu", ffn_u[l], 8, FH)
        wd, wdb = load_w(P, "wd", ffn_d[l], NHC, D)
        gains = {"ffn": (rows_in, (2 + l) * D)}
        NA = NormT(P, ident, identb, gains, nbuf=1)
        if final:
            gf = P.sb("gfin", [128, D], F32); gfb = P.buf("gfin")
            P.dma(gf[:], bcast_rows(rows_in, D, 4 * D), writes=[gfb])
        xs = [(P.sb(f"x{i}", [128, D], F32), P.buf(f"x{i}")) for i in range(5)]
        hTr = Rot(P, "hT", 1, [128, 8, 512], BF16)
        actr = Rot(P, "act", 1, [128, NHC, 512], BF16)
        psg = Rot(P, "psg", 3, [128, 512], F32, psum=True)
        psu = Rot(P, "psu", 2, [128, 512], F32, psum=True)
        psd = Rot(P, "psd", 2, [128, 512], F32, psum=True)
        sg = Rot(P, "sg", 2, [128, 512], F32)
        XO = P.buf("XO")
        for t in range(NBO):
            hT, hTb = hTr.next()
            xt_l = []
            for j in range(4):
                xt, xb = xs[(t * 4 + j) % 5]
                r0 = t * 512 + j * 128
                P.dma(xt[:], XIN.ap()[r0:r0 + 128, :], writes=[xb])
                NA.run(xt[:], xb, "ffn", hT, hTb, j * 128)
                xt_l.append((xt, xb))
            act, actb = actr.next()
            for hc in range(NHC):
                pg, pgb = psg.next()
                mm_group(P, pg, pgb, wg, wgb, 8, hc * 128, 128, hT, hTb, slice(0, 512), 512)
                pu, pub = psu.next()
                mm_group(P, pu, pub, wu, wub, 8, hc * 128, 128, hT, hTb, slice(0, 512), 512)
                s_, sb_ = sg.next()
                P.op("act", lambda e, s_=s_, pg=pg: e.activation(out=s_[:], in_=pg[:], func=AF.Silu), reads=[pgb], writes=[sb_])
                P.op("dve", lambda e, s_=s_, pu=pu, hc=hc, act=act: e.tensor_tensor(out=act[:, hc, :], in0=s_[:], in1=pu[:], op=ALU.mult),
                     reads=[sb_, pub], writes=[actb])
            for j in range(4):
                xt, xb = xt_l[j]
                r0 = t * 512 + j * 128
                for n in range(2):
                    pd, pdb = psd.next()
                    mm_group_tm(P, pd, pdb, act, actb, slice(j * 128, (j + 1) * 128), NHC, wd, wdb, n * 512, 512)
                    P.op("dve", lambda e, xt=xt, pd=pd, n=n: e.tensor_tensor(out=xt[:, n * 512:(n + 1) * 512], in0=xt[:, n * 512:(n + 1) * 512],
                                                                           in1=pd[:], op=ALU.add), reads=[pdb, xb], writes=[xb])
                if final:
                    st, stb = NA.rstd(xt[:], xb)
                    P.op("dve", lambda e, xt=xt, st=st: e.scalar_tensor_tensor(out=xt[:], in0=xt[:], scalar=st[:, 2:3], in1=gf[:],
                                                                             op0=ALU.mult, op1=ALU.mult), reads=[xb, stb, gfb], writes=[xb])
                P.dma(XOUT.ap()[r0:r0 + 128, :], xt[:], reads=[xb], writes=[XO], eng="pool")
        P.emit()

    outproj_phase("pD", MIXT, w_o0, x_in, True, X1A)
    if upto == "D":
        return nc
    ffn_phase("pE", 0, X1A, X1, False)
    if upto == "E":
        return nc

    P = Phase(nc, "pF")
    ident, identb = make_ident(P)
    wc, wcb = load_w(P, "wc", w_c, 8, 32 * 128 + D)
    NA = NormT(P, ident, identb, {"attn1": (rows_in, D)})
    xr = Rot(P, "x", 3, [128, D], F32)
    hTr = Rot(P, "hT", 2, [128, 8, 512], BF16)
    psr = Rot(P, "ps", 5, [128, 512], F32, psum=True)
    ps_tm = Rot(P, "ps_tm", 2, [128, 512], F32, psum=True)
    f32r = Rot(P, "f32", 6, [128, 512], F32)
    b16r = Rot(P, "b16", 4, [128, 512], BF16)
    v16 = Rot(P, "v16", 3, [128, D], BF16)
    rtab = Rot(P, "rtab", 4, [128, 512], F32)
    DQ = P.buf("Q1T"); DK = P.buf("K1S"); DV = P.buf("V1S"); DKG = P.buf("K1G"); DVG = P.buf("V1G")
    for t in range(NBO):
        c0 = t * 512
        sl = slice(c0, c0 + 512)
        hT, hTb = hTr.next()
        for j in range(4):
            xt, xb = xr.next()
            P.dma(xt[:], X1.ap()[c0 + j * 128:c0 + (j + 1) * 128, :], writes=[xb])
            NA.run(xt[:], xb, "attn1", hT, hTb, j * 128)
        ct, ctb = rtab.next(); st_, stb_ = rtab.next()
        P.dma(ct[:], ROPEC.ap()[:, sl], writes=[ctb])
        P.dma(st_[:], ROPES.ap()[:, sl], writes=[stb_])
        for j in range(4):
            v, vb = v16.next()
            for n in range(2):
                ps, psb = ps_tm.next()
                mm_group_tm(P, ps, psb, hT, hTb, slice(j * 128, (j + 1) * 128), 8, wc, wcb, 32 * 128 + n * 512, 512)
                P.op("act", lambda e, v=v, ps=ps, n=n: e.copy(out=v[:, n * 512:(n + 1) * 512], in_=ps[:]), reads=[psb], writes=[vb])
            r0 = c0 + j * 128
            P.dma(V1S[r0 // VR].ap()[r0 % VR:r0 % VR + 128, :], v[:], reads=[vb], writes=[DV], eng="pool")
        for which in range(2):
            for h in range(8):
                ps, psb = psr.next()
                mm_group(P, ps, psb, wc, wcb, 8, (which * 16 + h) * 128, 128, hT, hTb, slice(0, 512), 512)
                ps2, ps2b = psr.next()
                mm_group(P, ps2, ps2b, wc, wcb, 8, (which * 16 + 8 + h) * 128, 128, hT, hTb, slice(0, 512), 512)
                t1, t1b = f32r.next()
                P.op("dve", lambda e, t1=t1, ps=ps, ct=ct: e.tensor_tensor(out=t1[:], in0=ps[:], in1=ct[:], op=ALU.mult), reads=[psb, ctb], writes=[t1b])
                t2, t2b = f32r.next()
                P.op("dve", lambda e, t2=t2, ps2=ps2, st_=st_: e.tensor_tensor(out=t2[:], in0=ps2[:], in1=st_[:], op=ALU.mult), reads=[ps2b, stb_], writes=[t2b])
                o, ob = b16r.next()
                P.op("pool", lambda e, o=o, t1=t1, t2=t2: e.tensor_tensor(out=o[:], in0=t1[:], in1=t2[:], op=ALU.add), reads=[t1b, t2b], writes=[ob])
                if which == 0:
                    P.dma(Q1T.ap()[h, :, sl], o[:], reads=[ob], writes=[DQ], eng="pool")
                else:
                    P.dma(K1S[h].ap()[:, sl], o[:], reads=[ob], writes=[DK], eng="pool")
    groups = npair_groups
    for h in range(8):
        P.op("pool", lambda e, h=h: e.collective_compute("AllGather", ALU.bypass, replica_groups=groups, ins=[K1S[h].ap()],
                                                         outs=[K1G[h].ap()]), reads=[DK], writes=[DKG], dma=True, inc=1)
    for c in range(NVC):
        P.op("pool", lambda e, c=c: e.collective_compute("AllGather", ALU.bypass, replica_groups=groups, ins=[V1S[c].ap()],
                                                         outs=[V1G[c].ap()]), reads=[DV], writes=[DVG], dma=True, inc=1)
    P.emit()
    if upto == "F":
        return nc

    def diff_setup(P):
        C = attn_setup_common(P)
        C["KT"] = Rot(P, "K", 2, [128, 2, SH], BF16)
        C["Q0"] = Rot(P, "Qa", 2, [128, SH], BF16)
        C["Q1"] = Rot(P, "Qb", 2, [128, SH], BF16)
        for rot in (C["Q0"], C["Q1"]):
            for (t_, b_) in rot.items:
                P.op("pool", lambda e, t_=t_: e.memset(t_[:], 0.0), writes=[b_])
        C["VT"] = Rot(P, "V", 2, [128, NT, 128], BF16)
        cols = P.sb("cols", [128, 32], F32); colsb = P.buf("cols")
        P.dma(cols[:], cols_in.ap(), writes=[colsb])
        ones16 = P.sb("ones16", [128, 128], BF16); o16b_ = P.buf("ones16")
        P.op("pool", lambda e: e.memset(ones16[:], 1.0), writes=[o16b_])
        lv = P.sb("lv", [128, 256], F32); lvb = P.buf("lv")
        P.dma(lv[:], bcast_rows(rows_in, 256, 6 * D), writes=[lvb])
        lam = P.sb("lam", [128, 8], F32); lamb = P.buf("lam")
        pr = P.sb("pr", [128, 128], F32); prb = P.buf("pr")
        P.op("dve", lambda e: e.tensor_tensor(out=pr[:, 0:64], in0=lv[:, 0:64], in1=lv[:, 64:128], op=ALU.mult), reads=[lvb], writes=[prb])
        P.op("dve", lambda e: e.tensor_tensor(out=pr[:, 64:128], in0=lv[:, 128:192], in1=lv[:, 192:256], op=ALU.mult), reads=[lvb], writes=[prb])
        P.op("dve", lambda e: e.reduce_sum(out=lam[:, 0:2], in_=pr[:].rearrange("p (a b) -> p a b", b=64), axis=AX.X), reads=[prb], writes=[lamb])
        P.op("act", lambda e: e.activation(out=lam[:, 2:4], in_=lam[:, 0:2], func=AF.Exp), reads=[lamb], writes=[lamb])
        P.op("dve", lambda e: e.tensor_tensor(out=lam[:, 4:5], in0=lam[:, 2:3], in1=lam[:, 3:4], op=ALU.subtract), reads=[lamb], writes=[lamb])
        P.op("dve", lambda e: e.tensor_scalar(out=lam[:, 5:6], in0=lam[:, 4:5], scalar1=LAMBDA_INIT, scalar2=-1.0, op0=ALU.add, op1=ALU.mult),
             reads=[lamb], writes=[lamb])
        epsd = P.sb("epsd", [128, 1], F32); epsdb = P.buf("epsd")
        P.op("pool", lambda e: e.memset(epsd[:], EPS / (1.0 - LAMBDA_INIT) ** 2), writes=[epsdb])
        C.update(lam=lam, lamb=lamb, eps=epsd, epsb=epsdb, cols=cols, colsb=colsb, ones16=ones16, ones16b=o16b_,
                 sq16=Rot(P, "sq16", 2, [128, 512], BF16))
        return C

    def diff_finish(P, C, hi, qb, po, acc, pd):
        lam, lamb = C["lam"], C["lamb"]
        d0, d0b = den_matmul(P, C, [acc[0]])
        d1, d1b = pd[1]
        r0, r0b = C["r32"].next()
        P.op("dve", lambda e: e.reciprocal(out=r0[:], in_=d0[:]), reads=[d0b], writes=[r0b])
        r1, r1b = C["r32"].next()
        P.op("dve", lambda e: e.reciprocal(out=r1[:], in_=d1[:]), reads=[d1b], writes=[r1b])
        (o0, o0b), (o1, o1b) = po
        a, ab = C["r32"].next()
        P.op("dve", lambda e: e.tensor_tensor(out=a[:], in0=o0[:], in1=r0[:], op=ALU.mult), reads=[o0b, r0b], writes=[ab])
        t, tb = C["r32"].next()
        P.op("dve", lambda e: e.tensor_tensor(out=t[:], in0=o1[:], in1=r1[:], op=ALU.mult), reads=[o1b, r1b], writes=[tb])
        P.op("dve", lambda e: e.scalar_tensor_tensor(out=a[:], in0=t[:], scalar=lam[:, 5:6], in1=a[:], op0=ALU.mult, op1=ALU.add),
             reads=[ab, tb, lamb], writes=[ab])
        sq, sqb = C["sq16"].next()
        P.op("act", lambda e: e.activation(out=sq[:], in_=a[:], func=AF.Square), reads=[ab], writes=[sqb])
        ss, ssb = C["ss"].next()
        P.op("pe", lambda e: e.matmul(ss[:], lhsT=C["ones16"][:], rhs=sq[:], start=True, stop=True), reads=[C["ones16b"], sqb], writes=[ssb])
        rs, rsb = r0, r0b
        P.op("act", lambda e: e.activation(out=rs[:], in_=ss[:], func=AF.Sqrt, bias=C["eps"][:, 0:1],
                                           scale=1.0 / (128 * (1.0 - LAMBDA_INIT) ** 2)), reads=[ssb, C["epsb"]], writes=[rsb])
        P.op("dve", lambda e: e.reciprocal(out=rs[:], in_=rs[:]), reads=[rsb], writes=[rsb])
        o16, o16b = C["o16"].next()
        P.op("dve", lambda e: e.scalar_tensor_tensor(out=o16[:], in0=a[:], scalar=C["cols"][:, 25:26], in1=rs[:], op0=ALU.mult, op1=ALU.mult),
             reads=[ab, rsb, C["colsb"]], writes=[o16b])
        P.dma(MIXT1.ap()[hi, :, qb * 512:(qb + 1) * 512], o16[:], reads=[o16b], writes=[C["mx"]], eng="pool")

    def diff_head(h):
        def load(P, slot):
            C = diff_C[0]
            kt, ktb = C["KT"].next()
            kap = K1G[h].ap().rearrange("(r p) s -> p r s", r=2)
            for r in range(2):
                P.dma(kt[:, r, :], kap[:, r, :], writes=[ktb])
            q0, q0b = C["Q0"].next()
            q1, q1b = C["Q1"].next()
            P.dma(q0[0:64, :], Q1T.ap()[h][0:64, :], writes=[q0b])
            P.dma(q1[64:128, :], Q1T.ap()[h][64:128, :], writes=[q1b])
            vt, vb = C["VT"].next()
            for r in range(2):
                for c in range(NVC):
                    n0 = r * (SH // 128) + c * (VR // 128)
                    vap = V1G[c].ap()[r * VR:(r + 1) * VR, h * 128:(h + 1) * 128]
                    P.dma(vt[:, n0:n0 + VR // 128, :], vap.rearrange("(n p) d -> p n d", p=128), writes=[vb])
            return dict(kq_bufs=[ktb, q0b, q1b], v=(vt, vb), ktf=kt[:].rearrange("p r s -> p (r s)"), q=(q0, q1))

        def qk(e, ctx, s_, c, kt, qb):
            return e.matmul(s_[:], lhsT=ctx["ktf"][:, kt * 128:(kt + 1) * 128], rhs=ctx["q"][c][:, qb * 512:(qb + 1) * 512],
                            start=True, stop=True)

        return dict(load=load, qk=qk)

    diff_C = [None]

    def diff_setup_wrap(P):
        diff_C[0] = diff_setup(P)
        return diff_C[0]

    attn_fm(nc, "pG", S, SH, [diff_head(h) for h in range(8)], 2, DIFF_SCALE, 1, 4, 6, diff_setup_wrap, diff_finish, den_pe=(1,))
    if upto == "G":
        return nc

    outproj_phase("pH", MIXT1, w_o1, X1, False, X2A)
    ffn_phase("pI", 1, X2A, out, True)
    return nc


def _swap_halves(w, width):
    n = w.shape[1] // width
    w4 = w.reshape(w.shape[0], n, 2, width // 2)
    return np.ascontiguousarray(w4[:, :, ::-1, :]).reshape(w.shape[0], n * width)


def prep_inputs(S, ncores, x, norm_attn, norm_ffn, ffn_w_gate, ffn_w_up, ffn_w_down, ab_w_in, hgrn_lower_bound,
                hgrn_out_norm, mla_q_norm, mla_w_uq, mla_kv_norm, mla_w_ukv, ab_w_out, c_w_in,
                diff_lambda_q1, diff_lambda_k1, diff_lambda_q2, diff_lambda_k2, diff_out_norm, c_w_out, final_norm):
    f32 = np.float32
    w_in = np.asarray(ab_w_in[0], f32)
    sp = np.cumsum([0, 512, 512, 512, 512, 512, 384, 256, 64])
    Wq, Wffw, Wfbw, Wi, Wg, Wcq, Wckv, Wkr = [w_in[:, sp[i]:sp[i + 1]] for i in range(8)]
    Wkr_sw = _swap_halves(Wkr, 64)
    uq = np.asarray(mla_w_uq[0], f32).reshape(384, 4, 192)
    uq_n = uq[:, :, :128].reshape(384, 512)
    uq_r = uq[:, :, 128:].reshape(384, 256)
    w_uq = np.concatenate([uq_n, uq_r, _swap_halves(uq_r, 64)], 1)
    ukv = np.asarray(mla_w_ukv[0], f32).reshape(256, 4, 256)
    w_ukv = np.concatenate([ukv[:, :, :128].reshape(256, 512), ukv[:, :, 128:].reshape(256, 512)], 1)
    cw = np.asarray(c_w_in[0], f32)
    cq, ck, cv = cw[:, :1024], cw[:, 1024:2048], cw[:, 2048:]
    w_c = np.concatenate([cq, _swap_halves(cq, 64), ck, _swap_halves(ck, 64), cv], 1)
    inv = (1.0 / (10000.0 ** (np.arange(0, 64, 2, dtype=f32) / f32(64)))).astype(f32)
    p = np.arange(128)
    sign = np.where((p % 64) < 32, -1.0, 1.0).astype(f32)
    rows = np.zeros((8, D), f32)
    rows[0] = norm_attn[0]; rows[1] = norm_attn[1]; rows[2] = norm_ffn[0]; rows[3] = norm_ffn[1]; rows[4] = final_norm
    rows[5, :128] = diff_out_norm[0]
    rows[6, 0:64] = diff_lambda_q1[0]; rows[6, 64:128] = diff_lambda_k1[0]
    rows[6, 128:192] = diff_lambda_q2[0]; rows[6, 192:256] = diff_lambda_k2[0]
    lbraw = np.asarray(hgrn_lower_bound, f32)
    shared = dict(w_uq=w_uq, w_ukv=w_ukv, w_o0=np.asarray(ab_w_out[0], f32), w_c=w_c, w_o1=np.asarray(c_w_out[0], f32),
                  rows=rows)
    for l in range(2):
        shared[f"ffn_g{l}"] = np.asarray(ffn_w_gate[l], f32)
        shared[f"ffn_u{l}"] = np.asarray(ffn_w_up[l], f32)
        shared[f"ffn_d{l}"] = np.asarray(ffn_w_down[l], f32)
    per_r = []
    for r in range(2):
        fd = (Wffw, Wfbw) if r == 0 else (Wfbw, Wffw)
        w_a = np.concatenate([Wq, fd[0], fd[1], Wg, Wcq, Wckv, Wkr, Wkr, Wkr_sw, Wkr_sw, Wi], 1)
        cols = np.zeros((128, 32), f32)
        for d in range(2):
            od = d if r == 0 else 1 - d
            cols[:, d * 4:(d + 1) * 4] = lbraw[od, 0].reshape(4, 128).T
            cols[:, 8 + d * 4:8 + (d + 1) * 4] = lbraw[od, 1].reshape(4, 128).T
        cols[:, 16:20] = np.asarray(hgrn_out_norm[0], f32).reshape(4, 128).T
        cols[:, 20:23] = np.asarray(mla_q_norm[0], f32).reshape(3, 128).T
        cols[:, 23:25] = np.asarray(mla_kv_norm[0], f32).reshape(2, 128).T
        cols[:, 25] = np.asarray(diff_out_norm[0], f32)
        posv = np.arange(S, dtype=f32) if r == 0 else (S - 1 - np.arange(S)).astype(f32)
        ang = (posv[None, :] * inv[p % 32][:, None]).astype(f32)
        per_r.append(dict(w_a=np.ascontiguousarray(w_a), cols=cols, ropec_t=np.cos(ang).astype(f32),
                          ropes_t=(np.sin(ang) * sign[:, None]).astype(f32)))
    in_maps = []
    for c in range(ncores):
        b, r = c // 2, c % 2
        xb = np.asarray(x[b], f32)
        if r == 1:
            xb = xb[::-1]
        m = dict(shared)
        m.update(per_r[r])
        m["x"] = np.ascontiguousarray(xb)
        in_maps.append(m)
    return in_maps


_CACHE = {}


def run(S, ncores, inputs, debug=False, upto=None):
    key = (S, ncores, debug, upto)
    if key not in _CACHE:
        groups = [[2 * i, 2 * i + 1] for i in range(ncores // 2)]
        _CACHE[key] = build(S, groups, debug, upto)
    nc = _CACHE[key]
    in_maps = prep_inputs(S, ncores, **inputs)
    res = run_bass_kernel_spmd(nc, in_maps, core_ids=list(range(ncores)))
    B = ncores // 2
    SH = S // 2
    out = np.zeros((B, S, D), np.float32)
    for c in range(ncores):
        b, r = c // 2, c % 2
        o = res.results[c]["out"]
        if r == 0:
            out[b, :SH] = o
        else:
            out[b, SH:] = o[::-1]
    return out, res


def kernel(**inputs):
    out, _ = run(8192, 8, inputs)
    return out
```

```python
import math
import os
import numpy as np
from contextlib import ExitStack
import concourse.bass as bass
import concourse.mybir as mybir
from concourse.bass_utils import run_bass_kernel_spmd

F32 = mybir.dt.float32
BF16 = mybir.dt.bfloat16
ALU = mybir.AluOpType
AF = mybir.ActivationFunctionType
AX = mybir.AxisListType

D = 1024
FH = 2816
NHC = FH // 128
EPS = 1e-6
MLA_SCALE = 192 ** -0.5
DIFF_SCALE = 64 ** -0.5
LAMBDA_INIT = 0.8 - 0.6 * math.exp(-0.3 * 1)


class Buf:
    __slots__ = ("name", "last_w", "readers", "sem", "dma_total", "ent", "kind")

    def __init__(self, name):
        self.name = name
        self.last_w = None
        self.readers = []
        self.sem = None
        self.dma_total = 0


class Op:
    __slots__ = ("eng", "fn", "is_dma", "signal", "sigval", "waits", "dwaits", "dbuf", "inc")

    def __init__(self, eng, fn, is_dma):
        self.eng = eng
        self.fn = fn
        self.is_dma = is_dma
        self.signal = False
        self.sigval = 0
        self.waits = []
        self.dwaits = {}
        self.dbuf = None
        self.inc = 16


ENGS = ("pe", "act", "dve", "pool", "sp")
BLOCKNAME = {"pe": "tensor", "act": "scalar", "dve": "vector", "pool": "gpsimd", "sp": "sync"}


_SEMPOOL = [None]


class SemPool:
    def __init__(self, nc, es, n_dma=64):
        self.eng = {e: [es.enter_context(nc.semaphore(f"s_{e}")), 0] for e in ("pe", "act", "dve", "pool")}
        self.dma = {"sp": [[es.enter_context(nc.semaphore(f"dh{i}")), 0] for i in range(44)],
                    "pool": [[es.enter_context(nc.semaphore(f"ds{i}")), 0] for i in range(40)],
                    "cc": [[es.enter_context(nc.semaphore(f"dc{i}")), 0] for i in range(4)]}


class Phase:
    def __init__(self, nc, name, n_dma_sems=60):
        self.nc = nc
        self.name = name
        self.ops = {e: [] for e in ENGS}
        self.es = ExitStack()
        self.pool = _SEMPOOL[0]
        self.esem = {e: self.pool.eng[e][0] for e in ("pe", "act", "dve", "pool")}
        self.free_dma = {k: list(v) for k, v in self.pool.dma.items()}
        self.bufs = []
        self.nbuf = 0

    def sb(self, name, shape, dtype):
        return self.es.enter_context(self.nc.sbuf_tensor(f"{self.name}_{name}", list(shape), dtype))

    def ps(self, name, shape, dtype=F32):
        return self.es.enter_context(self.nc.psum_tensor(f"{self.name}_{name}", list(shape), dtype))

    def buf(self, name=None):
        self.nbuf += 1
        b = Buf(name or f"b{self.nbuf}")
        self.bufs.append(b)
        return b

    def _dep(self, o, w, kind):
        if w is o:
            return
        if w.is_dma:
            if o.is_dma and kind == "waw":
                return
            b = w.dbuf
            o.dwaits[b] = max(o.dwaits.get(b, 0), b.dma_total)
            return
        if w.eng == o.eng and not o.is_dma:
            if o.eng == "pe":
                return
            if kind != "raw":
                return
        w.signal = True
        o.waits.append(w)

    def op(self, eng, fn, reads=(), writes=(), dma=False, inc=16):
        o = Op(eng, fn, dma)
        o.inc = inc
        for b in reads:
            if b.last_w is not None:
                self._dep(o, b.last_w, "raw")
        for b in writes:
            if b.last_w is not None:
                self._dep(o, b.last_w, "waw")
            for r in b.readers:
                self._dep(o, r, "war")
        for b in reads:
            b.readers.append(o)
        for b in writes:
            b.last_w = o
            b.readers = []
        if dma:
            d = writes[0]
            kind = "cc" if inc != 16 else eng
            if d.sem is None:
                ent = self.free_dma[kind].pop()
                d.sem = ent[0]
                d.dma_total = ent[1]
                d.ent = ent
                d.kind = kind
            assert d.kind == kind, (d.name, d.kind, kind)
            d.dma_total += inc
            d.ent[1] = d.dma_total
            o.dbuf = d
        self.ops[eng].append(o)
        return o

    def dma(self, out, in_, reads=(), writes=(), eng="sp", **kw):
        return self.op(eng, lambda e: e.dma_start(out=out, in_=in_, **kw), reads=reads, writes=writes, dma=True)

    def emit(self):
        nc = self.nc
        fin_d = {}
        for b in self.bufs:
            if b.sem is not None:
                fin_d[b] = b.dma_total
        lasts = []
        for e in ("pe", "act", "dve", "pool"):
            cands = [o for o in self.ops[e] if not o.is_dma]
            if cands:
                cands[-1].signal = True
                lasts.append(cands[-1])
        for e in ENGS:
            fin = Op(e, None, False)
            fin.dwaits = dict(fin_d)
            fin.waits = list(lasts)
            self.ops[e].append(fin)
        for e in ("pe", "act", "dve", "pool"):
            n = self.pool.eng[e][1]
            for o in self.ops[e]:
                if o.signal and not o.is_dma:
                    n += 1
                    o.sigval = n
            self.pool.eng[e][1] = n
        with nc.Block() as block:
            for e in ENGS:

                def body(eng, e=e):
                    seen = {}
                    for o in self.ops[e]:
                        req = {}
                        for w in o.waits:
                            s = self.esem[w.eng]
                            if req.get(s, 0) < w.sigval:
                                req[s] = w.sigval
                        for b, tot in o.dwaits.items():
                            if req.get(b.sem, 0) < tot:
                                req[b.sem] = tot
                        for s, v in req.items():
                            if seen.get(s, 0) < v:
                                eng.wait_ge(s, v)
                                seen[s] = v
                        if o.fn is None:
                            continue
                        ins = o.fn(eng)
                        if o.is_dma:
                            ins.then_inc(o.dbuf.sem, o.inc)
                        elif o.signal:
                            ins.then_inc(self.esem[e], 1)

                getattr(block, BLOCKNAME[e])(body)
        self.es.close()


class Rot:
    def __init__(self, P, name, n, shape, dtype, psum=False):
        self.items = []
        for i in range(n):
            t = P.ps(f"{name}{i}", shape, dtype) if psum else P.sb(f"{name}{i}", shape, dtype)
            self.items.append((t, P.buf(f"{name}{i}")))
        self.i = 0

    def next(self):
        it = self.items[self.i % len(self.items)]
        self.i += 1
        return it


def bcast_rows(handle, n, offset=0, parts=128):
    return bass.AP(handle, offset, [[0, parts], [1, n]])


def make_ident(P, name="ident"):
    ident = P.sb(name, [128, 128], BF16)
    b = P.buf(name)

    P.op("pool", lambda e: e.memset(ident[:], 1.0), writes=[b])
    P.op("pool", lambda e: e.affine_select(out=ident[:], in_=ident[:], pattern=[[-1, 128]], compare_op=ALU.is_ge, fill=0.0,
                                           base=0, channel_multiplier=1), reads=[b], writes=[b])
    P.op("pool", lambda e: e.affine_select(out=ident[:], in_=ident[:], pattern=[[1, 128]], compare_op=ALU.is_ge, fill=0.0,
                                           base=0, channel_multiplier=-1), reads=[b], writes=[b])
    return ident, b


def eps_tile(P):
    t = P.sb("epsc", [128, 1], F32)
    b = P.buf("epsc")
    P.op("pool", lambda e: e.memset(t[:], EPS), writes=[b])
    return t, b


class NormT:
    def __init__(self, P, ident, ident_b, gains, nbuf=2):
        self.P = P
        self.ident, self.ident_b = ident, ident_b
        self.g = {}
        for k, (h, off) in gains.items():
            t = P.sb(f"gain_{k}", [128, D], F32)
            b = P.buf(f"gain_{k}")
            P.dma(t[:], bcast_rows(h, D, off), writes=[b])
            self.g[k] = (t, b)
        self.sq = Rot(P, "nt_sq", nbuf, [128, D], BF16)
        self.st = Rot(P, "nt_st", 4, [128, 4], F32)
        self.hb = Rot(P, "nt_hb", nbuf, [128, D], BF16)
        self.pt = Rot(P, "nt_pt", 1, [128, 8, 128], BF16, psum=True)
        self.eps, self.epsb = eps_tile(P)

    def rstd(self, xt, xb, width=D):
        P = self.P
        sq, sqb = self.sq.next()
        st, stb = self.st.next()
        P.op("act", lambda e: e.activation(out=sq[:, 0:width], in_=xt, func=AF.Square, accum_out=st[:, 0:1]),
             reads=[xb], writes=[sqb, stb])
        P.op("act", lambda e: e.activation(out=st[:, 1:2], in_=st[:, 0:1], func=AF.Sqrt, bias=self.eps[:, 0:1], scale=1.0 / width),
             reads=[stb, self.epsb], writes=[stb])
        P.op("dve", lambda e: e.reciprocal(out=st[:, 2:3], in_=st[:, 1:2]), reads=[stb], writes=[stb])
        return st, stb

    def run(self, xt, xb, gain, hT, hTb, col0):
        P = self.P
        gt, gb = self.g[gain]
        st, stb = self.rstd(xt, xb)
        hb, hbb = self.hb.next()
        P.op("dve", lambda e: e.scalar_tensor_tensor(out=hb[:], in0=xt, scalar=st[:, 2:3], in1=gt[:],
                                                     op0=ALU.mult, op1=ALU.mult), reads=[xb, stb, gb], writes=[hbb])
        pt, ptb = self.pt.next()

        def tr(e):
            ins = None
            for c in range(8):
                ins = e.transpose(out=pt[:, c, :], in_=hb[:, c * 128:(c + 1) * 128], identity=self.ident[:])
            return ins

        P.op("pe", tr, reads=[hbb, self.ident_b], writes=[ptb])
        P.op("act", lambda e: e.copy(out=hT[:, :, col0:col0 + 128], in_=pt[:]), reads=[ptb], writes=[hTb])


WCOLS = 1024


def load_w(P, name, handle, kchunks, ncols):
    t = P.sb(name, [128, kchunks, ncols], BF16)
    b = P.buf(name)
    src = handle.ap().rearrange("(c p) n -> p c n", p=128)
    for c0 in range(kchunks):
        for n0 in range(0, ncols, WCOLS):
            n1 = min(ncols, n0 + WCOLS)
            P.dma(t[:, c0, n0:n1], src[:, c0, n0:n1], writes=[b], eng="pool")
    return t, b


def mm_group(P, ps, psb, w, wb, kch, wc0, m, rhs, rhsb, rhs_slice, n):
    def f(e):
        ins = None
        for k in range(kch):
            ins = e.matmul(ps[0:m, 0:n], lhsT=w[:, k, wc0:wc0 + m], rhs=rhs[:, k, rhs_slice], start=(k == 0),
                           stop=(k == kch - 1))
        return ins

    P.op("pe", f, reads=[wb, rhsb], writes=[psb])


def mm_group_tm(P, ps, psb, lhs, lhsb, lhs_slice, kch, w, wb, wc0, n):
    def f(e):
        ins = None
        for k in range(kch):
            ins = e.matmul(ps[:, 0:n], lhsT=lhs[:, k, lhs_slice], rhs=w[:, k, wc0:wc0 + n], start=(k == 0),
                           stop=(k == kch - 1))
        return ins

    P.op("pe", f, reads=[lhsb, wb], writes=[psb])


LOOK = 2


def run_pipelined(iters, look):
    n = len(iters)
    for i in range(n + look):
        if i < n:
            iters[i][0]()
        if i - look >= 0:
            it = iters[i - look]
            it[1]()
            if it[2] is not None:
                it[2]()


def attn_fm(nc, name, S, SH, heads, ncomp, scale, look, n_s, n_p, setup, finish, den_pe=()):
    P = Phase(nc, name)
    NT = S // 128
    nq = 512
    NQB = SH // nq
    C = setup(P)
    ps_s = Rot(P, "ps_s", n_s, [128, 512], F32, psum=True)
    ps_o = [Rot(P, f"ps_o{c}", (2 if ncomp == 1 else 1), [128, 512], F32, psum=True) for c in range(ncomp)]
    pT = Rot(P, "pT", n_p, [128, 512], BF16)
    nacc = 2
    accr = [Rot(P, f"acc{i}", 2, [128, 512], F32) for i in range(nacc)]
    pdr = {c: Rot(P, f"pd{c}", 1, [128, 512], F32, psum=True) for c in den_pe}
    for hi, head in enumerate(heads):
        ctx = head["load"](P, hi % 2)
        iters = []
        for qb in range(NQB):
            po = [ps_o[c].next() for c in range(ncomp)]
            acc = [accr[i].next() for i in range(nacc)]
            pd = {c: pdr[c].next() for c in den_pe}
            for kt in range(NT):
                sl = [ps_s.next() for c in range(ncomp)]
                pl = [pT.next() for c in range(ncomp)]

                def rec_qk(sl=sl, pl=pl, kt=kt, qb=qb, ctx=ctx, head=head):
                    for c in range(ncomp):
                        s_, sb_ = sl[c]
                        P.op("pe", lambda e, s_=s_, c=c: head["qk"](e, ctx, s_, c, kt, qb), reads=ctx["kq_bufs"], writes=[sb_])
                    for c in range(ncomp):
                        s_, sb_ = sl[c]
                        p_, pb_ = pl[c]
                        P.op("act", lambda e, s_=s_, p_=p_: e.activation(out=p_[:], in_=s_[:], func=AF.Exp, scale=scale),
                             reads=[sb_], writes=[pb_])

                def rec_pv(pl=pl, kt=kt, po=po, acc=acc, ctx=ctx, pd=pd):
                    vt, vb = ctx["v"]
                    for c in range(ncomp):
                        p_, pb_ = pl[c]
                        o_, ob_ = po[c]
                        P.op("pe", lambda e, p_=p_, o_=o_: e.matmul(o_[:], lhsT=vt[:, kt, :], rhs=p_[:], start=(kt == 0),
                                                                     stop=(kt == NT - 1)), reads=[pb_, vb], writes=[ob_])
                    for c in den_pe:
                        p_, pb_ = pl[c]
                        d_, db_ = pd[c]
                        P.op("pe", lambda e, p_=p_, d_=d_: e.matmul(d_[:], lhsT=C["ones16"][:], rhs=p_[:], start=(kt == 0),
                                                                     stop=(kt == NT - 1)), reads=[pb_, C["ones16b"]], writes=[db_])
                    for c in range(ncomp):
                        if c in den_pe:
                            continue
                        p_, pb_ = pl[c]
                        if ncomp == 2:
                            ai, first = c, (kt == 0)
                        else:
                            ai, first = 0, (kt == 0)
                        a_, ab_ = acc[ai]
                        eng = "dve" if ai == 0 else "pool"
                        if first:
                            P.op(eng, lambda e, a_=a_, p_=p_: e.tensor_copy(out=a_[:], in_=p_[:]), reads=[pb_], writes=[ab_])
                        else:
                            P.op(eng, lambda e, a_=a_, p_=p_: e.tensor_tensor(out=a_[:], in0=a_[:], in1=p_[:], op=ALU.add),
                                 reads=[pb_, ab_], writes=[ab_])

                post = None
                if kt == NT - 1:
                    post = (lambda hi=hi, qb=qb, po=po, acc=acc, pd=pd: finish(P, C, hi, qb, po, acc, pd))
                iters.append((rec_qk, rec_pv, post))
        run_pipelined(iters, look)
    P.emit()


def build(S, npair_groups, debug=False, upto=None):
    SH = S // 2
    NBA = S // 512
    NBO = SH // 512
    NT = S // 128
    NCH = S // 64
    nc = bass.Bass("TRN2", target_bir_lowering=False)
    _es = ExitStack()
    _SEMPOOL[0] = SemPool(nc, _es)

    def din(name, shape):
        return nc.dram_tensor(name, list(shape), F32, kind="ExternalInput")

    def scr(name, shape, dt, dbg=False):
        if dbg and debug:
            return nc.dram_tensor(name, list(shape), dt, kind="ExternalOutput")
        return nc.dram_tensor(name, list(shape), dt)

    x_in = din("x", [S, D])
    w_a = din("w_a", [D, 23 * 128 + 512])
    w_uq = din("w_uq", [384, 1024])
    w_ukv = din("w_ukv", [256, 1024])
    w_o0 = din("w_o0", [D, D])
    w_c = din("w_c", [D, 16 * 128 + 16 * 128 + D])
    w_o1 = din("w_o1", [D, D])
    ffn_g = [din(f"ffn_g{l}", [D, FH]) for l in range(2)]
    ffn_u = [din(f"ffn_u{l}", [D, FH]) for l in range(2)]
    ffn_d = [din(f"ffn_d{l}", [FH, D]) for l in range(2)]
    cols_in = din("cols", [128, 32])
    rows_in = din("rows", [8, D])
    out = nc.dram_tensor("out", [SH, D], F32, kind="ExternalOutput")

    ROPEC = din("ropec_t", [128, S])
    ROPES = din("ropes_t", [128, S])
    QT0 = scr("QT0", [2, 4, 128, SH], BF16)
    KT0 = scr("KT0", [2, 4, 128, SH], BF16)
    KH0 = scr("KH0", [2, 4, S, 128], BF16)
    DEC0 = scr("DEC0", [2, 4, 128, NCH], F32)
    V0 = scr("V0", [S, 512], BF16)
    GATE0 = scr("GATE0", [4, 128, SH], F32)
    OFW = scr("OFW", [4, 128, SH], F32)
    QN = scr("QN", [4, 128, SH], BF16)
    QR = scr("QR", [2, 128, SH], BF16)
    KN = scr("KN", [4, 128, S], BF16)
    KR = scr("KR", [128, S], BF16)
    VBm = scr("VBm", [4, S, 128], BF16)
    MIXT = scr("MIXT", [8, 128, SH], BF16, dbg=True)
    X1A = scr("X1A", [SH, D], F32)
    X1 = scr("X1", [SH, D], F32, dbg=True)
    Q1T = scr("Q1T", [8, 128, SH], BF16)
    VR = min(512, SH)
    NVC = SH // VR
    K1S = [scr(f"K1S{h}", [128, SH], BF16) for h in range(8)]
    V1S = [scr(f"V1S{c}", [VR, D], BF16) for c in range(NVC)]
    K1G = [scr(f"K1G{h}", [2 * 128, SH], BF16) for h in range(8)]
    V1G = [scr(f"V1G{c}", [2 * VR, D], BF16) for c in range(NVC)]
    MIXT1 = scr("MIXT1", [8, 128, SH], BF16, dbg=True)
    X2A = scr("X2A", [SH, D], F32)

    P = Phase(nc, "pA")
    ident, identb = make_ident(P)
    wa, wab = load_w(P, "wa", w_a, 8, 23 * 128 + 512)
    wuq, wuqb = load_w(P, "wuq", w_uq, 3, 1024)
    wukv, wukvb = load_w(P, "wukv", w_ukv, 2, 1024)
    cols = P.sb("cols", [128, 32], F32); colsb = P.buf("cols")
    P.dma(cols[:], cols_in.ap(), writes=[colsb])
    lbt = P.sb("lbt", [128, 32], F32); lbb = P.buf("lbt")
    P.op("dve", lambda e: e.tensor_tensor(out=lbt[:, 0:8], in0=cols[:, 0:8], in1=cols[:, 8:16], op=ALU.subtract),
         reads=[colsb], writes=[lbb])
    P.op("act", lambda e: e.activation(out=lbt[:, 8:16], in_=lbt[:, 0:8], func=AF.Sigmoid), reads=[lbb], writes=[lbb])
    P.op("dve", lambda e: e.tensor_scalar(out=lbt[:, 16:24], in0=lbt[:, 8:16], scalar1=-1.0, scalar2=1.0, op0=ALU.mult,
                                          op1=ALU.add), reads=[lbb], writes=[lbb])
    P.op("dve", lambda e: e.tensor_scalar(out=lbt[:, 24:32], in0=lbt[:, 16:24], scalar1=-1.0, scalar2=None, op0=ALU.mult),
         reads=[lbb], writes=[lbb])
    ones = P.sb("ones", [128, 128], BF16); onesb = P.buf("ones")
    P.op("pool", lambda e: e.memset(ones[:], 1.0), writes=[onesb])
    rmask = P.sb("rmask", [128, 8, 64], F32); rmb = P.buf("rmask")

    P.op("pool", lambda e: e.memset(rmask[:], 1.0), writes=[rmb])
    P.op("pool", lambda e: e.memset(rmask[:, :, 0:1], 0.0), reads=[rmb], writes=[rmb])
    NA = NormT(P, ident, identb, {"attn0": (rows_in, 0)})
    xr = Rot(P, "x", 4, [128, D], F32)
    hTr = Rot(P, "hT", 2, [128, 8, 512], BF16)
    psr = Rot(P, "ps", 4, [128, 512], F32, psum=True)
    ps_ss = Rot(P, "ps_ss", 1, [128, 512], F32, psum=True)
    ps_tm = Rot(P, "ps_tm", 1, [128, 512], F32, psum=True)
    ps_kt = Rot(P, "ps_kt", 1, [128, 4, 128], BF16, psum=True)
    f32r = Rot(P, "f32", 12, [128, 512], F32)
    b16r = Rot(P, "b16", 8, [128, 512], BF16)
    sgr = Rot(P, "sgr", 8, [128, 512], F32)
    qsr = Rot(P, "qs", 5, [128, 512], F32)
    rtab = Rot(P, "rtab", 4, [128, 512], F32)
    lat = Rot(P, "lat", 2, [128, 3, 512], F32)
    latn = Rot(P, "latn", 2, [128, 3, 512], BF16)
    kh16 = Rot(P, "kh16", 3, [128, 4, 128], BF16)
    v16 = Rot(P, "v16", 3, [128, 512], BF16)
    decr = Rot(P, "dec", 4, [128, 8], F32)
    D_ = {k: P.buf(k) for k in "QT0 KT0 KH0 DEC0 V0 GATE0 QN QR KN KR VBm".split()}

    def loadsA(t):
        c0 = t * 512
        xs_ = []
        for j in range(4):
            xt, xb = xr.next()
            P.dma(xt[:], x_in.ap()[c0 + j * 128:c0 + (j + 1) * 128, :], writes=[xb])
            xs_.append((xt, xb))
        ct, ctb = rtab.next(); st_, stb_ = rtab.next()
        P.dma(ct[:], ROPEC.ap()[:, c0:c0 + 512], writes=[ctb])
        P.dma(st_[:], ROPES.ap()[:, c0:c0 + 512], writes=[stb_])
        return xs_, ct, ctb, st_, stb_

    pend = {0: loadsA(0)}
    for t in range(NBA):
        own = t < NBO
        c0 = t * 512
        sl = slice(c0, c0 + 512)
        xs_, ct, ctb, st_, stb_ = pend.pop(t)
        hT, hTb = hTr.next()
        for j in range(4):
            NA.run(xs_[j][0][:], xs_[j][1], "attn0", hT, hTb, j * 128)
        if t + 1 < NBA:
            pend[t + 1] = loadsA(t + 1)

        def fm(group, ps, psb):
            mm_group(P, ps, psb, wa, wab, 8, group * 128, 128, hT, hTb, slice(0, 512), 512)

        for j in range(4):
            ps, psb = ps_tm.next()
            mm_group_tm(P, ps, psb, hT, hTb, slice(j * 128, (j + 1) * 128), 8, wa, wab, 23 * 128, 512)
            v, vb = v16.next()
            P.op("act", lambda e, v=v, ps=ps: e.copy(out=v[:], in_=ps[:]), reads=[psb], writes=[vb])
            P.dma(V0.ap()[c0 + j * 128:c0 + (j + 1) * 128, :], v[:], reads=[vb], writes=[D_["V0"]])
        qs = []
        if own:
            for h in range(4):
                ps, psb = psr.next()
                fm(h, ps, psb)
                q, qb_ = qsr.next()
                P.op("act", lambda e, q=q, ps=ps: e.activation(out=q[:], in_=ps[:], func=AF.Silu), reads=[psb], writes=[qb_])
                qs.append((q, qb_))
            for h in range(4):
                ps, psb = psr.next()
                fm(12 + h, ps, psb)
                g, gb = f32r.next()
                P.op("act", lambda e, g=g, ps=ps: e.activation(out=g[:], in_=ps[:], func=AF.Silu), reads=[psb], writes=[gb])
                P.dma(GATE0.ap()[h, :, sl], g[:], reads=[gb], writes=[D_["GATE0"]])
        sig_l = []
        for d in ((0, 1) if own else (1,)):
            for h in range(4):
                ps, psb = psr.next()
                fm(4 + 4 * d + h, ps, psb)
                sg, sgb = sgr.next()
                P.op("act", lambda e, sg=sg, ps=ps: e.activation(out=sg[:], in_=ps[:], func=AF.Sigmoid), reads=[psb], writes=[sgb])
                sig_l.append((d, h, sg, sgb))
        for (d, h, sg, sgb) in sig_l:
            ci = d * 4 + h
            g, gb = f32r.next()
            P.op("act", lambda e, g=g, sg=sg, ci=ci: e.activation(out=g[:], in_=sg[:], func=AF.Ln, bias=lbt[:, 8 + ci:9 + ci],
                                                                 scale=lbt[:, 16 + ci:17 + ci]), reads=[sgb, lbb], writes=[gb])
            k, kb = f32r.next()
            P.op("pool", lambda e, k=k, sg=sg, ci=ci: e.tensor_scalar(out=k[:], in0=sg[:], scalar1=lbt[:, 24 + ci:25 + ci],
                                                                     scalar2=lbt[:, 16 + ci:17 + ci], op0=ALU.mult, op1=ALU.add),
                 reads=[sgb, lbb], writes=[kb])
            b_, bb = f32r.next()
            P.op("dve", lambda e, b_=b_, g=g: e.tensor_tensor_scan(out=b_[:], data0=rmask[:].rearrange("p a b -> p (a b)"),
                                                                  data1=g[:], initial=0.0, op0=ALU.mult, op1=ALU.add),
                 reads=[gb, rmb], writes=[bb])
            b3 = b_[:].rearrange("p (a b) -> p a b", b=64)
            dl, dlb = f32r.next()
            P.op("pool", lambda e, dl=dl, b3=b3: e.tensor_tensor(out=dl[:].rearrange("p (a b) -> p a b", b=64), in0=b3,
                                                               in1=b3[:, :, 63:64].to_broadcast([128, 8, 64]), op=ALU.subtract),
                 reads=[bb], writes=[dlb])
            dec, decb = decr.next()
            P.op("act", lambda e, dec=dec, b3=b3: e.activation(out=dec[:], in_=b3[:, :, 63], func=AF.Exp), reads=[bb], writes=[decb])
            P.dma(DEC0.ap()[d, h, :, t * 8:(t + 1) * 8], dec[:], reads=[decb], writes=[D_["DEC0"]])
            if d == 0:
                bq, bqb = b_, bb
                ex, exb = dl, dlb
                ex_scale = -1.0
            else:
                bq, bqb = f32r.next()
                P.op("pool", lambda e, bq=bq, g=g, dl=dl: e.tensor_tensor(out=bq[:], in0=g[:], in1=dl[:], op=ALU.subtract),
                     reads=[gb, dlb], writes=[bqb])
                ex, exb = f32r.next()
                P.op("pool", lambda e, ex=ex, b_=b_, g=g: e.tensor_tensor(out=ex[:], in0=b_[:], in1=g[:], op=ALU.subtract),
                     reads=[bb, gb], writes=[exb])
                ex_scale = 1.0
            ek, ekb = f32r.next()
            P.op("act", lambda e, ek=ek, ex=ex, ex_scale=ex_scale: e.activation(out=ek[:], in_=ex[:], func=AF.Exp, scale=ex_scale),
                 reads=[exb], writes=[ekb])
            kh, khb = b16r.next()
            P.op("dve", lambda e, kh=kh, k=k, ek=ek: e.tensor_tensor(out=kh[:], in0=k[:], in1=ek[:], op=ALU.mult),
                 reads=[kb, ekb], writes=[khb])
            pk, pkb = ps_kt.next()

            def trk(e, pk=pk, kh=kh):
                ins = None
                for j in range(4):
                    ins = e.transpose(out=pk[:, j, :], in_=kh[:, j * 128:(j + 1) * 128], identity=ident[:])
                return ins

            P.op("pe", trk, reads=[khb, identb], writes=[pkb])
            kt_, ktb = kh16.next()
            P.op("act", lambda e, kt_=kt_, pk=pk: e.copy(out=kt_[:], in_=pk[:]), reads=[pkb], writes=[ktb])
            P.dma(KH0.ap()[d, h, sl, :].rearrange("(j p) k -> p j k", p=128), kt_[:], reads=[ktb], writes=[D_["KH0"]])
            if own:
                eq, eqb = f32r.next()
                P.op("act", lambda e, eq=eq, bq=bq: e.activation(out=eq[:], in_=bq[:], func=AF.Exp), reads=[bqb], writes=[eqb])
                en, enb = f32r.next()
                P.op("act", lambda e, en=en, bq=bq: e.activation(out=en[:], in_=bq[:], func=AF.Exp, scale=-1.0), reads=[bqb], writes=[enb])
                qt, qtb = b16r.next()
                q, qb_ = qs[h]
                P.op("dve", lambda e, qt=qt, q=q, eq=eq: e.scalar_tensor_tensor(out=qt[:], in0=q[:], scalar=128 ** -0.5, in1=eq[:],
                                                                              op0=ALU.mult, op1=ALU.mult), reads=[qb_, eqb], writes=[qtb])
                P.dma(QT0.ap()[d, h, :, sl], qt[:], reads=[qtb], writes=[D_["QT0"]])
                kt2, kt2b = b16r.next()
                P.op("pool", lambda e, kt2=kt2, k=k, en=en: e.tensor_tensor(out=kt2[:], in0=k[:], in1=en[:], op=ALU.mult),
                     reads=[kb, enb], writes=[kt2b])
                P.dma(KT0.ap()[d, h, :, sl], kt2[:], reads=[kt2b], writes=[D_["KT0"]])
        lat_jobs = ([(16, 3, 20)] if own else []) + [(19, 2, 23)]
        for (g0, ng, gcol) in lat_jobs:
            lt, ltb = lat.next()
            for i in range(ng):
                ps, psb = psr.next()
                fm(g0 + i, ps, psb)
                P.op("act", lambda e, lt=lt, ps=ps, i=i: e.copy(out=lt[:, i, :], in_=ps[:]), reads=[psb], writes=[ltb])
            ss, ssb = ps_ss.next()
            sqs = []
            for i in range(ng):
                sq, sqb = b16r.next()
                P.op("pool", lambda e, sq=sq, lt=lt, i=i: e.tensor_tensor(out=sq[:], in0=lt[:, i, :], in1=lt[:, i, :], op=ALU.mult),
                     reads=[ltb], writes=[sqb])
                sqs.append((sq, sqb))

            def ssm(e, ss=ss, sqs=sqs):
                ins = None
                for i, (sq, _) in enumerate(sqs):
                    ins = e.matmul(ss[:], lhsT=ones[:], rhs=sq[:], start=(i == 0), stop=(i == len(sqs) - 1))
                return ins

            P.op("pe", ssm, reads=[onesb] + [b for _, b in sqs], writes=[ssb])
            rs, rsb = f32r.next()
            P.op("act", lambda e, rs=rs, ss=ss, ng=ng: e.activation(out=rs[:], in_=ss[:], func=AF.Sqrt, bias=NA.eps[:, 0:1],
                                                                   scale=1.0 / (ng * 128)), reads=[ssb, NA.epsb], writes=[rsb])
            P.op("dve", lambda e, rs=rs: e.reciprocal(out=rs[:], in_=rs[:]), reads=[rsb], writes=[rsb])
            ln, lnb = latn.next()
            for i in range(ng):
                P.op("dve", lambda e, ln=ln, lt=lt, rs=rs, i=i, gcol=gcol: e.scalar_tensor_tensor(
                    out=ln[:, i, :], in0=lt[:, i, :], scalar=cols[:, gcol + i:gcol + i + 1], in1=rs[:], op0=ALU.mult, op1=ALU.mult),
                    reads=[ltb, rsb, colsb], writes=[lnb])
            if g0 == 16:
                for h in range(4):
                    ps, psb = psr.next()
                    mm_group(P, ps, psb, wuq, wuqb, 3, h * 128, 128, ln, lnb, slice(0, 512), 512)
                    o, ob = b16r.next()
                    P.op("act", lambda e, o=o, ps=ps: e.copy(out=o[:], in_=ps[:]), reads=[psb], writes=[ob])
                    P.dma(QN.ap()[h, :, sl], o[:], reads=[ob], writes=[D_["QN"]])
                for pr in range(2):
                    ps, psb = psr.next()
                    mm_group(P, ps, psb, wuq, wuqb, 3, (4 + pr) * 128, 128, ln, lnb, slice(0, 512), 512)
                    ps2, ps2b = psr.next()
                    mm_group(P, ps2, ps2b, wuq, wuqb, 3, (6 + pr) * 128, 128, ln, lnb, slice(0, 512), 512)
                    t1, t1b = f32r.next()
                    P.op("dve", lambda e, t1=t1, ps=ps, ct=ct: e.tensor_tensor(out=t1[:], in0=ps[:], in1=ct[:], op=ALU.mult),
                         reads=[psb, ctb], writes=[t1b])
                    t2, t2b = f32r.next()
                    P.op("dve", lambda e, t2=t2, ps2=ps2, st_=st_: e.tensor_tensor(out=t2[:], in0=ps2[:], in1=st_[:], op=ALU.mult),
                         reads=[ps2b, stb_], writes=[t2b])
                    o, ob = b16r.next()
                    P.op("pool", lambda e, o=o, t1=t1, t2=t2: e.tensor_tensor(out=o[:], in0=t1[:], in1=t2[:], op=ALU.add),
                         reads=[t1b, t2b], writes=[ob])
                    P.dma(QR.ap()[pr, :, sl], o[:], reads=[ob], writes=[D_["QR"]])
            else:
                for h in range(4):
                    ps, psb = psr.next()
                    mm_group(P, ps, psb, wukv, wukvb, 2, h * 128, 128, ln, lnb, slice(0, 512), 512)
                    o, ob = b16r.next()
                    P.op("act", lambda e, o=o, ps=ps: e.copy(out=o[:], in_=ps[:]), reads=[psb], writes=[ob])
                    P.dma(KN.ap()[h, :, sl], o[:], reads=[ob], writes=[D_["KN"]])
                for j in range(4):
                    ps, psb = ps_tm.next()
                    mm_group_tm(P, ps, psb, ln, lnb, slice(j * 128, (j + 1) * 128), 2, wukv, wukvb, 512, 512)
                    v, vb = v16.next()
                    P.op("act", lambda e, v=v, ps=ps: e.copy(out=v[:], in_=ps[:]), reads=[psb], writes=[vb])
                    P.dma(VBm.ap()[:, c0 + j * 128:c0 + (j + 1) * 128, :].rearrange("h p d -> p h d"),
                          v[:].rearrange("p (h d) -> p h d", d=128), reads=[vb], writes=[D_["VBm"]])
        ps, psb = psr.next()
        fm(21, ps, psb)
        ps2, ps2b = psr.next()
        fm(22, ps2, ps2b)
        t1, t1b = f32r.next()
        P.op("dve", lambda e, t1=t1, ps=ps, ct=ct: e.tensor_tensor(out=t1[:], in0=ps[:], in1=ct[:], op=ALU.mult), reads=[psb, ctb], writes=[t1b])
        t2, t2b = f32r.next()
        P.op("dve", lambda e, t2=t2, ps2=ps2, st_=st_: e.tensor_tensor(out=t2[:], in0=ps2[:], in1=st_[:], op=ALU.mult), reads=[ps2b, stb_], writes=[t2b])
        o, ob = b16r.next()
        P.op("pool", lambda e, o=o, t1=t1, t2=t2: e.tensor_tensor(out=o[:], in0=t1[:], in1=t2[:], op=ALU.add), reads=[t1b, t2b], writes=[ob])
        P.dma(KR.ap()[:, sl], o[:], reads=[ob], writes=[D_["KR"]])
    P.emit()
    if upto == "A":
        return nc

    P = Phase(nc, "pB")
    cols = P.sb("cols", [128, 32], F32); colsb = P.buf("cols")
    P.dma(cols[:], cols_in.ap(), writes=[colsb])
    ones = P.sb("ones", [128, 128], BF16); onesb = P.buf("ones")
    P.op("pool", lambda e: e.memset(ones[:], 1.0), writes=[onesb])
    epsB, epsBb = eps_tile(P)
    masks = []
    for d in range(2):
        m = P.sb(f"mask{d}", [128, 128], F32); mb = P.buf(f"mask{d}")

        P.op("pool", lambda e, m=m: e.memset(m[:], 1.0), writes=[mb])
        if d == 0:
            P.op("pool", lambda e, m=m: e.affine_select(out=m[:], in_=m[:], pattern=[[1, 128]], compare_op=ALU.is_ge, fill=0.0,
                                                       base=0, channel_multiplier=-1), reads=[mb], writes=[mb])
            P.op("pool", lambda e, m=m: e.memset(m[0:64, 64:128], 0.0), reads=[mb], writes=[mb])
        else:
            P.op("pool", lambda e, m=m: e.affine_select(out=m[:], in_=m[:], pattern=[[-1, 128]], compare_op=ALU.is_ge, fill=0.0,
                                                       base=0, channel_multiplier=1), reads=[mb], writes=[mb])
            P.op("pool", lambda e, m=m: e.memset(m[64:128, 0:64], 0.0), reads=[mb], writes=[mb])
        m2 = P.sb(f"maskf{d}", [128, 128], F32); m2b = P.buf(f"maskf{d}")
        P.op("dve", lambda e, m=m, m2=m2: e.tensor_copy(out=m2[:], in_=m[:]), reads=[mb], writes=[m2b])
        masks.append((m2, m2b))
    St = P.sb("St", [128, 4, 128], F32); Sb = [P.buf(f"S{h}") for h in range(4)]
    S16 = P.sb("S16", [128, 4, 128], BF16); S16b = P.buf("S16")
    qtr = Rot(P, "qt", 2, [128, 4, 512], BF16)
    ktr = Rot(P, "kt", 2, [128, 4, 512], BF16)
    khr = Rot(P, "kh", 2, [128, 4, 4, 128], BF16)
    vr = Rot(P, "v", 2, [128, 4, 512], BF16)
    dcr = Rot(P, "dc", 2, [128, 4, 8], F32)
    ofr = Rot(P, "of", 2, [128, 4, 512], F32)
    gtr = Rot(P, "gt", 2, [128, 4, 512], F32)
    psO = [Rot(P, f"psO{h}", 1, [128, 512], F32, psum=True) for h in range(4)]
    psXr = Rot(P, "psX", 2, [128, 4, 128], F32, psum=True)
    psSr = Rot(P, "psS", 1, [128, 4, 128], F32, psum=True)
    ps_ss = Rot(P, "ps_ss", 1, [128, 512], F32, psum=True)
    atr = Rot(P, "at", 2, [128, 4, 128], BF16)
    sc32r = Rot(P, "sc32", 2, [128, 4, 128], F32)
    ev = Rot(P, "ev", 6, [128, 512], F32)
    e16 = Rot(P, "e16", 4, [128, 512], BF16)
    DOF = P.buf("OFW"); DMX = P.buf("MIXT")

    def reset_state():
        for h in range(4):
            P.op("pool", lambda e, h=h: e.memset(St[:, h, :], 0.0), writes=[Sb[h]])
        P.op("pool", lambda e: e.memset(S16[:], 0.0), writes=[S16b])

    def hgrn_block(d, t, full, final):
        c0 = t * 512
        sl = slice(c0, c0 + 512)
        kh, khb = khr.next()
        for h in range(4):
            P.dma(kh[:, h, :, :], KH0.ap()[d, h, sl, :].rearrange("(j p) k -> p j k", p=128), writes=[khb])
        v, vb = vr.next()
        P.dma(v[:], V0.ap()[sl, :].rearrange("(j p) c -> p j c", p=128), writes=[vb])
        dc, dcb = dcr.next()
        P.dma(dc[:], DEC0.ap()[d, :, :, t * 8:(t + 1) * 8].rearrange("h p c -> p h c"), writes=[dcb])
        if full:
            qt, qtb = qtr.next()
            P.dma(qt[:], QT0.ap()[d, :, :, sl].rearrange("h p s -> p h s"), writes=[qtb])
            kt, ktb = ktr.next()
            P.dma(kt[:], KT0.ap()[d, :, :, sl].rearrange("h p s -> p h s"), writes=[ktb])
            pso = [psO[h].next() for h in range(4)]
        if final:
            of, ofb = ofr.next()
            P.dma(of[:], OFW.ap()[:, :, sl].rearrange("h p s -> p h s"), reads=[DOF], writes=[ofb])
            gt, gtb = gtr.next()
            P.dma(gt[:], GATE0.ap()[:, :, sl].rearrange("h p s -> p h s"), writes=[gtb])
        m, mb = masks[d]
        tiles = range(4) if d == 0 else range(3, -1, -1)
        for j in tiles:
            chunks = (0, 1) if d == 0 else (1, 0)
            if full:
                pS, pSb = psSr.next()

                def sc(e, pS=pS, j=j):
                    ins = None
                    for h in range(4):
                        ins = e.matmul(pS[:, h, :], lhsT=kt[:, h, j * 128:(j + 1) * 128], rhs=qt[:, h, j * 128:(j + 1) * 128],
                                       start=True, stop=True)
                    return ins

                P.op("pe", sc, reads=[ktb, qtb], writes=[pSb])
                sc32, sc32b = sc32r.next()
                P.op("act", lambda e, sc32=sc32, pS=pS: e.copy(out=sc32[:], in_=pS[:]), reads=[pSb], writes=[sc32b])
                at, atb = atr.next()
                P.op("pool", lambda e, at=at, sc32=sc32: e.tensor_tensor(out=at[:], in0=sc32[:],
                                                                       in1=m[:].unsqueeze(1).to_broadcast([128, 4, 128]), op=ALU.mult),
                     reads=[sc32b, mb], writes=[atb])
            for c in chunks:
                pr = slice(64 * c, 64 * c + 64)
                co = j * 128 + c * 64
                cidx = j * 2 + c
                if full:
                    for h in range(4):
                        po, pob = pso[h]

                        def om(e, at=at, po=po, h=h, co=co, c=c, j=j):
                            e.matmul(po[:, co:co + 64], lhsT=v[:, j, h * 128:(h + 1) * 128], rhs=at[:, h, c * 64:(c + 1) * 64],
                                     start=True, stop=False)
                            return e.matmul(po[:, co:co + 64], lhsT=S16[:, h, :], rhs=qt[:, h, co:co + 64], start=False, stop=True)

                        P.op("pe", om, reads=[vb, atb, S16b, qtb], writes=[pob])
                pX, pXb = psXr.next()

                def su(e, pX=pX, pr=pr, j=j):
                    ins = None
                    for h in range(4):
                        ins = e.matmul(pX[:, h, :], lhsT=kh[pr, h, j, :], rhs=v[pr, j, h * 128:(h + 1) * 128], start=True, stop=True)
                    return ins

                P.op("pe", su, reads=[khb, vb], writes=[pXb])
                for h in range(4):
                    P.op("dve", lambda e, pX=pX, h=h, cidx=cidx: e.scalar_tensor_tensor(
                        out=St[:, h, :], in0=St[:, h, :], scalar=dc[:, h, cidx:cidx + 1], in1=pX[:, h, :], op0=ALU.mult, op1=ALU.add),
                        reads=[Sb[h], dcb, pXb], writes=[Sb[h]])
                P.op("act", lambda e: e.copy(out=S16[:], in_=St[:]), reads=Sb, writes=[S16b])
        if full:
            for h in range(4):
                po, pob = pso[h]
                if not final:
                    o, ob = ev.next()
                    P.op("act", lambda e, o=o, po=po: e.copy(out=o[:], in_=po[:]), reads=[pob], writes=[ob])
                    P.dma(OFW.ap()[h, :, sl], o[:], reads=[ob], writes=[DOF], eng="pool")
                else:
                    tot, totb = ev.next()
                    P.op("dve", lambda e, tot=tot, po=po, h=h: e.tensor_tensor(out=tot[:], in0=po[:], in1=of[:, h, :], op=ALU.add),
                         reads=[pob, ofb], writes=[totb])
                    sq, sqb = e16.next()
                    P.op("act", lambda e, sq=sq, tot=tot: e.activation(out=sq[:], in_=tot[:], func=AF.Square), reads=[totb], writes=[sqb])
                    ss, ssb = ps_ss.next()
                    P.op("pe", lambda e, ss=ss, sq=sq: e.matmul(ss[:], lhsT=ones[:], rhs=sq[:], start=True, stop=True),
                         reads=[onesb, sqb], writes=[ssb])
                    rs, rsb = ev.next()
                    P.op("act", lambda e, rs=rs, ss=ss: e.activation(out=rs[:], in_=ss[:], func=AF.Sqrt, bias=epsB[:, 0:1], scale=1.0 / 128),
                         reads=[ssb, epsBb], writes=[rsb])
                    P.op("dve", lambda e, rs=rs: e.reciprocal(out=rs[:], in_=rs[:]), reads=[rsb], writes=[rsb])
                    P.op("pool", lambda e, tot=tot, rs=rs: e.tensor_tensor(out=tot[:], in0=tot[:], in1=rs[:], op=ALU.mult),
                         reads=[totb, rsb], writes=[totb])
                    o, ob = e16.next()
                    P.op("dve", lambda e, o=o, tot=tot, h=h: e.scalar_tensor_tensor(out=o[:], in0=tot[:], scalar=cols[:, 16 + h:17 + h],
                                                                                   in1=gt[:, h, :], op0=ALU.mult, op1=ALU.mult),
                         reads=[totb, colsb, gtb], writes=[ob])
                    P.dma(MIXT.ap()[h, :, sl], o[:], reads=[ob], writes=[DMX], eng="pool")

    reset_state()
    for t in range(NBO):
        hgrn_block(0, t, True, False)
    reset_state()
    for t in range(NBA - 1, NBO - 1, -1):
        hgrn_block(1, t, False, False)
    for t in range(NBO - 1, -1, -1):
        hgrn_block(1, t, True, True)
    P.emit()
    if upto == "B":
        return nc

    def attn_setup_common(P):
        C = dict(mx=P.buf("MIXOUT"))
        ones32 = P.sb("ones32", [128, 128], F32); o32b = P.buf("ones32")
        P.op("pool", lambda e: e.memset(ones32[:], 1.0), writes=[o32b])
        C.update(ones32=ones32, o32b=o32b, ss=Rot(P, "ss", 1, [128, 512], F32, psum=True),
                 r32=Rot(P, "r32", 4, [128, 512], F32), o16=Rot(P, "o16", 2, [128, 512], BF16))
        return C

    def mla_setup(P):
        C = attn_setup_common(P)
        C["KT"] = [(P.sb(f"K{i}", [128, S], BF16), P.buf(f"K{i}")) for i in range(2)]
        C["QT"] = [(P.sb(f"Q{i}", [128, SH], BF16), P.buf(f"Q{i}")) for i in range(2)]
        C["VT"] = (P.sb("V", [128, NT, 128], BF16), P.buf("V"))
        P.op("pool", lambda e: e.memset(C["KT"][1][0][:], 0.0), writes=[C["KT"][1][1]])
        P.op("pool", lambda e: e.memset(C["QT"][1][0][:], 0.0), writes=[C["QT"][1][1]])
        return C

    def den_matmul(P, C, accs):
        ss, ssb = C["ss"].next()

        def f(e):
            ins = None
            for i, (a_, _) in enumerate(accs):
                ins = e.matmul(ss[:], lhsT=C["ones32"][:], rhs=a_[:], start=(i == 0), stop=(i == len(accs) - 1))
            return ins

        P.op("pe", f, reads=[C["o32b"]] + [ab_ for _, ab_ in accs], writes=[ssb])
        return ss, ssb

    def mla_finish(P, C, hi, qb, po, acc, pd):
        ss, ssb = den_matmul(P, C, [acc[0]])
        r, rb = C["r32"].next()
        P.op("dve", lambda e: e.reciprocal(out=r[:], in_=ss[:]), reads=[ssb], writes=[rb])
        o_, ob_ = po[0]
        o16, o16b = C["o16"].next()
        P.op("dve", lambda e: e.tensor_tensor(out=o16[:], in0=o_[:], in1=r[:], op=ALU.mult), reads=[ob_, rb], writes=[o16b])
        P.dma(MIXT.ap()[4 + hi, :, qb * 512:(qb + 1) * 512], o16[:], reads=[o16b], writes=[C["mx"]], eng="pool")

    def mla_head(h):
        pb = 64 * (h % 2)

        def load(P, slot):
            C = mla_C[0]
            (k0, k0b), (k1, k1b) = C["KT"]
            (q0, q0b), (q1, q1b) = C["QT"]
            vt, vb = C["VT"]
            half = S // 2
            for lo, hi_ in ((0, half), (half, S)):
                P.dma(k0[:, lo:hi_], KN.ap()[h][:, lo:hi_], writes=[k0b])
                P.dma(k1[0:64, lo:hi_], KR.ap()[0:64, lo:hi_], writes=[k1b])
            P.dma(q0[:], QN.ap()[h], writes=[q0b])
            P.dma(q1[0:64, :], QR.ap()[h // 2, pb:pb + 64, :], writes=[q1b])
            vsrc = VBm.ap()[h].rearrange("(n p) d -> p n d", p=128)
            for n0 in range(0, NT, 8):
                n1 = min(NT, n0 + 8)
                P.dma(vt[:, n0:n1, :], vsrc[:, n0:n1, :], writes=[vb])
            return dict(kq_bufs=[k0b, k1b, q0b, q1b], v=(vt, vb), k0=k0, k1=k1, q0=q0, q1=q1, pb=pb)

        def qk(e, ctx, s_, c, kt, qb):
            pb_ = ctx["pb"]
            e.matmul(s_[:], lhsT=ctx["k0"][:, kt * 128:(kt + 1) * 128], rhs=ctx["q0"][:, qb * 512:(qb + 1) * 512], start=True, stop=False)
            return e.matmul(s_[:], lhsT=ctx["k1"][:, kt * 128:(kt + 1) * 128],
                            rhs=ctx["q1"][:, qb * 512:(qb + 1) * 512], start=False, stop=True)

        return dict(load=load, qk=qk)

    mla_C = [None]

    def mla_setup_wrap(P):
        mla_C[0] = mla_setup(P)
        return mla_C[0]

    attn_fm(nc, "pC", S, SH, [mla_head(h) for h in range(4)], 1, MLA_SCALE, 2, 3, 4, mla_setup_wrap, mla_finish)
    if upto == "C":
        return nc

    def outproj_phase(name, MIX, w_o, XIN, xin_is_input, XOUT):
        P = Phase(nc, name)
        wo, wob = load_w(P, "wo", w_o, 8, D)
        mxr = Rot(P, "mx", 2, [128, 8, 512], BF16)
        xr = Rot(P, "x", 3, [128, D], F32)
        psr = Rot(P, "ps", 4, [128, 512], F32, psum=True)
        XO = P.buf("XO")
        for t in range(NBO):
            sl = slice(t * 512, (t + 1) * 512)
            mx, mxb = mxr.next()
            P.dma(mx[:], MIX.ap()[:, :, sl].rearrange("c p s -> p c s"), writes=[mxb])
            for j in range(4):
                r0 = t * 512 + j * 128
                xt, xb = xr.next()
                P.dma(xt[:], XIN.ap()[r0:r0 + 128, :], writes=[xb])
                for n in range(2):
                    ps, psb = psr.next()
                    mm_group_tm(P, ps, psb, mx, mxb, slice(j * 128, (j + 1) * 128), 8, wo, wob, n * 512, 512)
                    P.op("dve", lambda e, xt=xt, ps=ps, n=n: e.tensor_tensor(out=xt[:, n * 512:(n + 1) * 512], in0=xt[:, n * 512:(n + 1) * 512],
                                                                           in1=ps[:], op=ALU.add), reads=[psb, xb], writes=[xb])
                P.dma(XOUT.ap()[r0:r0 + 128, :], xt[:], reads=[xb], writes=[XO], eng="pool")
        P.emit()

    def ffn_phase(name, l, XIN, XOUT, final):
        P = Phase(nc, name)
        ident, identb = make_ident(P)
        wg, wgb = load_w(P, "wg", ffn_g[l], 8, FH)
        wu, wub = load_w(P, "wu", ffn_u[l], 8, FH)
        wd, wdb = load_w(P, "wd", ffn_d[l], NHC, D)
        gains = {"ffn": (rows_in, (2 + l) * D)}
        NA = NormT(P, ident, identb, gains, nbuf=1)
        if final:
            gf = P.sb("gfin", [128, D], F32); gfb = P.buf("gfin")
            P.dma(gf[:], bcast_rows(rows_in, D, 4 * D), writes=[gfb])
        xs = [(P.sb(f"x{i}", [128, D], F32), P.buf(f"x{i}")) for i in range(5)]
        hTr = Rot(P, "hT", 1, [128, 8, 512], BF16)
        actr = Rot(P, "act", 1, [128, NHC, 512], BF16)
        psg = Rot(P, "psg", 3, [128, 512], F32, psum=True)
        psu = Rot(P, "psu", 2, [128, 512], F32, psum=True)
        psd = Rot(P, "psd", 2, [128, 512], F32, psum=True)
        sg = Rot(P, "sg", 2, [128, 512], F32)
        XO = P.buf("XO")
        for t in range(NBO):
            hT, hTb = hTr.next()
            xt_l = []
            for j in range(4):
                xt, xb = xs[(t * 4 + j) % 5]
                r0 = t * 512 + j * 128
                P.dma(xt[:], XIN.ap()[r0:r0 + 128, :], writes=[xb])
                NA.run(xt[:], xb, "ffn", hT, hTb, j * 128)
                xt_l.append((xt, xb))
            act, actb = actr.next()
            for hc in range(NHC):
                pg, pgb = psg.next()
                mm_group(P, pg, pgb, wg, wgb, 8, hc * 128, 128, hT, hTb, slice(0, 512), 512)
                pu, pub = psu.next()
                mm_group(P, pu, pub, wu, wub, 8, hc * 128, 128, hT, hTb, slice(0, 512), 512)
                s_, sb_ = sg.next()
                P.op("act", lambda e, s_=s_, pg=pg: e.activation(out=s_[:], in_=pg[:], func=AF.Silu), reads=[pgb], writes=[sb_])
                P.op("dve", lambda e, s_=s_, pu=pu, hc=hc, act=act: e.tensor_tensor(out=act[:, hc, :], in0=s_[:], in1=pu[:], op=ALU.mult),
                     reads=[sb_, pub], writes=[actb])
            for j in range(4):
                xt, xb = xt_l[j]
                r0 = t * 512 + j * 128
                for n in range(2):
                    pd, pdb = psd.next()
                    mm_group_tm(P, pd, pdb, act, actb, slice(j * 128, (j + 1) * 128), NHC, wd, wdb, n * 512, 512)
                    P.op("dve", lambda e, xt=xt, pd=pd, n=n: e.tensor_tensor(out=xt[:, n * 512:(n + 1) * 512], in0=xt[:, n * 512:(n + 1) * 512],
                                                                           in1=pd[:], op=ALU.add), reads=[pdb, xb], writes=[xb])
                if final:
                    st, stb = NA.rstd(xt[:], xb)
                    P.op("dve", lambda e, xt=xt, st=st: e.scalar_tensor_tensor(out=xt[:], in0=xt[:], scalar=st[:, 2:3], in1=gf[:],
                                                                             op0=ALU.mult, op1=ALU.mult), reads=[xb, stb, gfb], writes=[xb])
                P.dma(XOUT.ap()[r0:r0 + 128, :], xt[:], reads=[xb], writes=[XO], eng="pool")
        P.emit()

    outproj_phase("pD", MIXT, w_o0, x_in, True, X1A)
    if upto == "D":
        return nc
    ffn_phase("pE", 0, X1A, X1, False)
    if upto == "E":
        return nc

    P = Phase(nc, "pF")
    ident, identb = make_ident(P)
    wc, wcb = load_w(P, "wc", w_c, 8, 32 * 128 + D)
    NA = NormT(P, ident, identb, {"attn1": (rows_in, D)})
    xr = Rot(P, "x", 3, [128, D], F32)
    hTr = Rot(P, "hT", 2, [128, 8, 512], BF16)
    psr = Rot(P, "ps", 5, [128, 512], F32, psum=True)
    ps_tm = Rot(P, "ps_tm", 2, [128, 512], F32, psum=True)
    f32r = Rot(P, "f32", 6, [128, 512], F32)
    b16r = Rot(P, "b16", 4, [128, 512], BF16)
    v16 = Rot(P, "v16", 3, [128, D], BF16)
    rtab = Rot(P, "rtab", 4, [128, 512], F32)
    DQ = P.buf("Q1T"); DK = P.buf("K1S"); DV = P.buf("V1S"); DKG = P.buf("K1G"); DVG = P.buf("V1G")
    for t in range(NBO):
        c0 = t * 512
        sl = slice(c0, c0 + 512)
        hT, hTb = hTr.next()
        for j in range(4):
            xt, xb = xr.next()
            P.dma(xt[:], X1.ap()[c0 + j * 128:c0 + (j + 1) * 128, :], writes=[xb])
            NA.run(xt[:], xb, "attn1", hT, hTb, j * 128)
        ct, ctb = rtab.next(); st_, stb_ = rtab.next()
        P.dma(ct[:], ROPEC.ap()[:, sl], writes=[ctb])
        P.dma(st_[:], ROPES.ap()[:, sl], writes=[stb_])
        for j in range(4):
            v, vb = v16.next()
            for n in range(2):
                ps, psb = ps_tm.next()
                mm_group_tm(P, ps, psb, hT, hTb, slice(j * 128, (j + 1) * 128), 8, wc, wcb, 32 * 128 + n * 512, 512)
                P.op("act", lambda e, v=v, ps=ps, n=n: e.copy(out=v[:, n * 512:(n + 1) * 512], in_=ps[:]), reads=[psb], writes=[vb])
            r0 = c0 + j * 128
            P.dma(V1S[r0 // VR].ap()[r0 % VR:r0 % VR + 128, :], v[:], reads=[vb], writes=[DV], eng="pool")
        for which in range(2):
            for h in range(8):
                ps, psb = psr.next()
                mm_group(P, ps, psb, wc, wcb, 8, (which * 16 + h) * 128, 128, hT, hTb, slice(0, 512), 512)
                ps2, ps2b = psr.next()
                mm_group(P, ps2, ps2b, wc, wcb, 8, (which * 16 + 8 + h) * 128, 128, hT, hTb, slice(0, 512), 512)
                t1, t1b = f32r.next()
                P.op("dve", lambda e, t1=t1, ps=ps, ct=ct: e.tensor_tensor(out=t1[:], in0=ps[:], in1=ct[:], op=ALU.mult), reads=[psb, ctb], writes=[t1b])
                t2, t2b = f32r.next()
                P.op("dve", lambda e, t2=t2, ps2=ps2, st_=st_: e.tensor_tensor(out=t2[:], in0=ps2[:], in1=st_[:], op=ALU.mult), reads=[ps2b, stb_], writes=[t2b])
                o, ob = b16r.next()
                P.op("pool", lambda e, o=o, t1=t1, t2=t2: e.tensor_tensor(out=o[:], in0=t1[:], in1=t2[:], op=ALU.add), reads=[t1b, t2b], writes=[ob])
                if which == 0:
                    P.dma(Q1T.ap()[h, :, sl], o[:], reads=[ob], writes=[DQ], eng="pool")
                else:
                    P.dma(K1S[h].ap()[:, sl], o[:], reads=[ob], writes=[DK], eng="pool")
    groups = npair_groups
    for h in range(8):
        P.op("pool", lambda e, h=h: e.collective_compute("AllGather", ALU.bypass, replica_groups=groups, ins=[K1S[h].ap()],
                                                         outs=[K1G[h].ap()]), reads=[DK], writes=[DKG], dma=True, inc=1)
    for c in range(NVC):
        P.op("pool", lambda e, c=c: e.collective_compute("AllGather", ALU.bypass, replica_groups=groups, ins=[V1S[c].ap()],
                                                         outs=[V1G[c].ap()]), reads=[DV], writes=[DVG], dma=True, inc=1)
    P.emit()
    if upto == "F":
        return nc

    def diff_setup(P):
        C = attn_setup_common(P)
        C["KT"] = Rot(P, "K", 2, [128, 2, SH], BF16)
        C["Q0"] = Rot(P, "Qa", 2, [128, SH], BF16)
        C["Q1"] = Rot(P, "Qb", 2, [128, SH], BF16)
        for rot in (C["Q0"], C["Q1"]):
            for (t_, b_) in rot.items:
                P.op("pool", lambda e, t_=t_: e.memset(t_[:], 0.0), writes=[b_])
        C["VT"] = Rot(P, "V", 2, [128, NT, 128], BF16)
        cols = P.sb("cols", [128, 32], F32); colsb = P.buf("cols")
        P.dma(cols[:], cols_in.ap(), writes=[colsb])
        ones16 = P.sb("ones16", [128, 128], BF16); o16b_ = P.buf("ones16")
        P.op("pool", lambda e: e.memset(ones16[:], 1.0), writes=[o16b_])
        lv = P.sb("lv", [128, 256], F32); lvb = P.buf("lv")
        P.dma(lv[:], bcast_rows(rows_in, 256, 6 * D), writes=[lvb])
        lam = P.sb("lam", [128, 8], F32); lamb = P.buf("lam")
        pr = P.sb("pr", [128, 128], F32); prb = P.buf("pr")
        P.op("dve", lambda e: e.tensor_tensor(out=pr[:, 0:64], in0=lv[:, 0:64], in1=lv[:, 64:128], op=ALU.mult), reads=[lvb], writes=[prb])
        P.op("dve", lambda e: e.tensor_tensor(out=pr[:, 64:128], in0=lv[:, 128:192], in1=lv[:, 192:256], op=ALU.mult), reads=[lvb], writes=[prb])
        P.op("dve", lambda e: e.reduce_sum(out=lam[:, 0:2], in_=pr[:].rearrange("p (a b) -> p a b", b=64), axis=AX.X), reads=[prb], writes=[lamb])
        P.op("act", lambda e: e.activation(out=lam[:, 2:4], in_=lam[:, 0:2], func=AF.Exp), reads=[lamb], writes=[lamb])
        P.op("dve", lambda e: e.tensor_tensor(out=lam[:, 4:5], in0=lam[:, 2:3], in1=lam[:, 3:4], op=ALU.subtract), reads=[lamb], writes=[lamb])
        P.op("dve", lambda e: e.tensor_scalar(out=lam[:, 5:6], in0=lam[:, 4:5], scalar1=LAMBDA_INIT, scalar2=-1.0, op0=ALU.add, op1=ALU.mult),
             reads=[lamb], writes=[lamb])
        epsd = P.sb("epsd", [128, 1], F32); epsdb = P.buf("epsd")
        P.op("pool", lambda e: e.memset(epsd[:], EPS / (1.0 - LAMBDA_INIT) ** 2), writes=[epsdb])
        C.update(lam=lam, lamb=lamb, eps=epsd, epsb=epsdb, cols=cols, colsb=colsb, ones16=ones16, ones16b=o16b_,
                 sq16=Rot(P, "sq16", 2, [128, 512], BF16))
        return C

    def diff_finish(P, C, hi, qb, po, acc, pd):
        lam, lamb = C["lam"], C["lamb"]
        d0, d0b = den_matmul(P, C, [acc[0]])
        d1, d1b = pd[1]
        r0, r0b = C["r32"].next()
        P.op("dve", lambda e: e.reciprocal(out=r0[:], in_=d0[:]), reads=[d0b], writes=[r0b])
        r1, r1b = C["r32"].next()
        P.op("dve", lambda e: e.reciprocal(out=r1[:], in_=d1[:]), reads=[d1b], writes=[r1b])
        (o0, o0b), (o1, o1b) = po
        a, ab = C["r32"].next()
        P.op("dve", lambda e: e.tensor_tensor(out=a[:], in0=o0[:], in1=r0[:], op=ALU.mult), reads=[o0b, r0b], writes=[ab])
        t, tb = C["r32"].next()
        P.op("dve", lambda e: e.tensor_tensor(out=t[:], in0=o1[:], in1=r1[:], op=ALU.mult), reads=[o1b, r1b], writes=[tb])
        P.op("dve", lambda e: e.scalar_tensor_tensor(out=a[:], in0=t[:], scalar=lam[:, 5:6], in1=a[:], op0=ALU.mult, op1=ALU.add),
             reads=[ab, tb, lamb], writes=[ab])
        sq, sqb = C["sq16"].next()
        P.op("act", lambda e: e.activation(out=sq[:], in_=a[:], func=AF.Square), reads=[ab], writes=[sqb])
        ss, ssb = C["ss"].next()
        P.op("pe", lambda e: e.matmul(ss[:], lhsT=C["ones16"][:], rhs=sq[:], start=True, stop=True), reads=[C["ones16b"], sqb], writes=[ssb])
        rs, rsb = r0, r0b
        P.op("act", lambda e: e.activation(out=rs[:], in_=ss[:], func=AF.Sqrt, bias=C["eps"][:, 0:1],
                                           scale=1.0 / (128 * (1.0 - LAMBDA_INIT) ** 2)), reads=[ssb, C["epsb"]], writes=[rsb])
        P.op("dve", lambda e: e.reciprocal(out=rs[:], in_=rs[:]), reads=[rsb], writes=[rsb])
        o16, o16b = C["o16"].next()
        P.op("dve", lambda e: e.scalar_tensor_tensor(out=o16[:], in0=a[:], scalar=C["cols"][:, 25:26], in1=rs[:], op0=ALU.mult, op1=ALU.mult),
             reads=[ab, rsb, C["colsb"]], writes=[o16b])
        P.dma(MIXT1.ap()[hi, :, qb * 512:(qb + 1) * 512], o16[:], reads=[o16b], writes=[C["mx"]], eng="pool")

    def diff_head(h):
        def load(P, slot):
            C = diff_C[0]
            kt, ktb = C["KT"].next()
            kap = K1G[h].ap().rearrange("(r p) s -> p r s", r=2)
            for r in range(2):
                P.dma(kt[:, r, :], kap[:, r, :], writes=[ktb])
            q0, q0b = C["Q0"].next()
            q1, q1b = C["Q1"].next()
            P.dma(q0[0:64, :], Q1T.ap()[h][0:64, :], writes=[q0b])
            P.dma(q1[64:128, :], Q1T.ap()[h][64:128, :], writes=[q1b])
            vt, vb = C["VT"].next()
            for r in range(2):
                for c in range(NVC):
                    n0 = r * (SH // 128) + c * (VR // 128)
                    vap = V1G[c].ap()[r * VR:(r + 1) * VR, h * 128:(h + 1) * 128]
                    P.dma(vt[:, n0:n0 + VR // 128, :], vap.rearrange("(n p) d -> p n d", p=128), writes=[vb])
            return dict(kq_bufs=[ktb, q0b, q1b], v=(vt, vb), ktf=kt[:].rearrange("p r s -> p (r s)"), q=(q0, q1))

        def qk(e, ctx, s_, c, kt, qb):
            return e.matmul(s_[:], lhsT=ctx["ktf"][:, kt * 128:(kt + 1) * 128], rhs=ctx["q"][c][:, qb * 512:(qb + 1) * 512],
                            start=True, stop=True)

        return dict(load=load, qk=qk)

    diff_C = [None]

    def diff_setup_wrap(P):
        diff_C[0] = diff_setup(P)
        return diff_C[0]

    attn_fm(nc, "pG", S, SH, [diff_head(h) for h in range(8)], 2, DIFF_SCALE, 1, 4, 6, diff_setup_wrap, diff_finish, den_pe=(1,))
    if upto == "G":
        return nc

    outproj_phase("pH", MIXT1, w_o1, X1, False, X2A)
    ffn_phase("pI", 1, X2A, out, True)
    return nc


def _swap_halves(w, width):
    n = w.shape[1] // width
    w4 = w.reshape(w.shape[0], n, 2, width // 2)
    return np.ascontiguousarray(w4[:, :, ::-1, :]).reshape(w.shape[0], n * width)


def prep_inputs(S, ncores, x, norm_attn, norm_ffn, ffn_w_gate, ffn_w_up, ffn_w_down, ab_w_in, hgrn_lower_bound,
                hgrn_out_norm, mla_q_norm, mla_w_uq, mla_kv_norm, mla_w_ukv, ab_w_out, c_w_in,
                diff_lambda_q1, diff_lambda_k1, diff_lambda_q2, diff_lambda_k2, diff_out_norm, c_w_out, final_norm):
    f32 = np.float32
    w_in = np.asarray(ab_w_in[0], f32)
    sp = np.cumsum([0, 512, 512, 512, 512, 512, 384, 256, 64])
    Wq, Wffw, Wfbw, Wi, Wg, Wcq, Wckv, Wkr = [w_in[:, sp[i]:sp[i + 1]] for i in range(8)]
    Wkr_sw = _swap_halves(Wkr, 64)
    uq = np.asarray(mla_w_uq[0], f32).reshape(384, 4, 192)
    uq_n = uq[:, :, :128].reshape(384, 512)
    uq_r = uq[:, :, 128:].reshape(384, 256)
    w_uq = np.concatenate([uq_n, uq_r, _swap_halves(uq_r, 64)], 1)
    ukv = np.asarray(mla_w_ukv[0], f32).reshape(256, 4, 256)
    w_ukv = np.concatenate([ukv[:, :, :128].reshape(256, 512), ukv[:, :, 128:].reshape(256, 512)], 1)
    cw = np.asarray(c_w_in[0], f32)
    cq, ck, cv = cw[:, :1024], cw[:, 1024:2048], cw[:, 2048:]
    w_c = np.concatenate([cq, _swap_halves(cq, 64), ck, _swap_halves(ck, 64), cv], 1)
    inv = (1.0 / (10000.0 ** (np.arange(0, 64, 2, dtype=f32) / f32(64)))).astype(f32)
    p = np.arange(128)
    sign = np.where((p % 64) < 32, -1.0, 1.0).astype(f32)
    rows = np.zeros((8, D), f32)
    rows[0] = norm_attn[0]; rows[1] = norm_attn[1]; rows[2] = norm_ffn[0]; rows[3] = norm_ffn[1]; rows[4] = final_norm
    rows[5, :128] = diff_out_norm[0]
    rows[6, 0:64] = diff_lambda_q1[0]; rows[6, 64:128] = diff_lambda_k1[0]
    rows[6, 128:192] = diff_lambda_q2[0]; rows[6, 192:256] = diff_lambda_k2[0]
    lbraw = np.asarray(hgrn_lower_bound, f32)
    shared = dict(w_uq=w_uq, w_ukv=w_ukv, w_o0=np.asarray(ab_w_out[0], f32), w_c=w_c, w_o1=np.asarray(c_w_out[0], f32),
                  rows=rows)
    for l in range(2):
        shared[f"ffn_g{l}"] = np.asarray(ffn_w_gate[l], f32)
        shared[f"ffn_u{l}"] = np.asarray(ffn_w_up[l], f32)
        shared[f"ffn_d{l}"] = np.asarray(ffn_w_down[l], f32)
    per_r = []
    for r in range(2):
        fd = (Wffw, Wfbw) if r == 0 else (Wfbw, Wffw)
        w_a = np.concatenate([Wq, fd[0], fd[1], Wg, Wcq, Wckv, Wkr, Wkr, Wkr_sw, Wkr_sw, Wi], 1)
        cols = np.zeros((128, 32), f32)
        for d in range(2):
            od = d if r == 0 else 1 - d
            cols[:, d * 4:(d + 1) * 4] = lbraw[od, 0].reshape(4, 128).T
            cols[:, 8 + d * 4:8 + (d + 1) * 4] = lbraw[od, 1].reshape(4, 128).T
        cols[:, 16:20] = np.asarray(hgrn_out_norm[0], f32).reshape(4, 128).T
        cols[:, 20:23] = np.asarray(mla_q_norm[0], f32).reshape(3, 128).T
        cols[:, 23:25] = np.asarray(mla_kv_norm[0], f32).reshape(2, 128).T
        cols[:, 25] = np.asarray(diff_out_norm[0], f32)
        posv = np.arange(S, dtype=f32) if r == 0 else (S - 1 - np.arange(S)).astype(f32)
        ang = (posv[None, :] * inv[p % 32][:, None]).astype(f32)
        per_r.append(dict(w_a=np.ascontiguousarray(w_a), cols=cols, ropec_t=np.cos(ang).astype(f32),
                          ropes_t=(np.sin(ang) * sign[:, None]).astype(f32)))
    in_maps = []
    for c in range(ncores):
        b, r = c // 2, c % 2
        xb = np.asarray(x[b], f32)
        if r == 1:
            xb = xb[::-1]
        m = dict(shared)
        m.update(per_r[r])
        m["x"] = np.ascontiguousarray(xb)
        in_maps.append(m)
    return in_maps


_CACHE = {}


def run(S, ncores, inputs, debug=False, upto=None):
    key = (S, ncores, debug, upto)
    if key not in _CACHE:
        groups = [[2 * i, 2 * i + 1] for i in range(ncores // 2)]
        _CACHE[key] = build(S, groups, debug, upto)
    nc = _CACHE[key]
    in_maps = prep_inputs(S, ncores, **inputs)
    res = run_bass_kernel_spmd(nc, in_maps, core_ids=list(range(ncores)))
    B = ncores // 2
    SH = S // 2
    out = np.zeros((B, S, D), np.float32)
    for c in range(ncores):
        b, r = c // 2, c % 2
        o = res.results[c]["out"]
        if r == 0:
            out[b, :SH] = o
        else:
            out[b, SH:] = o[::-1]
    return out, res


def kernel(**inputs):
    out, _ = run(8192, 8, inputs)
    return out
```

```python
import math
import os
import numpy as np
from contextlib import ExitStack
import concourse.bass as bass
import concourse.mybir as mybir
from concourse.bass_utils import run_bass_kernel_spmd

F32 = mybir.dt.float32
BF16 = mybir.dt.bfloat16
ALU = mybir.AluOpType
AF = mybir.ActivationFunctionType
AX = mybir.AxisListType

D = 1024
FH = 2816
NHC = FH // 128
EPS = 1e-6
MLA_SCALE = 192 ** -0.5
DIFF_SCALE = 64 ** -0.5
LAMBDA_INIT = 0.8 - 0.6 * math.exp(-0.3 * 1)


class Buf:
    __slots__ = ("name", "last_w", "readers", "sem", "dma_total", "ent", "kind")

    def __init__(self, name):
        self.name = name
        self.last_w = None
        self.readers = []
        self.sem = None
        self.dma_total = 0


class Op:
    __slots__ = ("eng", "fn", "is_dma", "signal", "sigval", "waits", "dwaits", "dbuf", "inc")

    def __init__(self, eng, fn, is_dma):
        self.eng = eng
        self.fn = fn
        self.is_dma = is_dma
        self.signal = False
        self.sigval = 0
        self.waits = []
        self.dwaits = {}
        self.dbuf = None
        self.inc = 16


ENGS = ("pe", "act", "dve", "pool", "sp")
BLOCKNAME = {"pe": "tensor", "act": "scalar", "dve": "vector", "pool": "gpsimd", "sp": "sync"}


_SEMPOOL = [None]


class SemPool:
    def __init__(self, nc, es, n_dma=64):
        self.eng = {e: [es.enter_context(nc.semaphore(f"s_{e}")), 0] for e in ("pe", "act", "dve", "pool")}
        self.dma = {"sp": [[es.enter_context(nc.semaphore(f"dh{i}")), 0] for i in range(44)],
                    "pool": [[es.enter_context(nc.semaphore(f"ds{i}")), 0] for i in range(40)],
                    "cc": [[es.enter_context(nc.semaphore(f"dc{i}")), 0] for i in range(4)]}


class Phase:
    def __init__(self, nc, name, n_dma_sems=60):
        self.nc = nc
        self.name = name
        self.ops = {e: [] for e in ENGS}
        self.es = ExitStack()
        self.pool = _SEMPOOL[0]
        self.esem = {e: self.pool.eng[e][0] for e in ("pe", "act", "dve", "pool")}
        self.free_dma = {k: list(v) for k, v in self.pool.dma.items()}
        self.bufs = []
        self.nbuf = 0

    def sb(self, name, shape, dtype):
        return self.es.enter_context(self.nc.sbuf_tensor(f"{self.name}_{name}", list(shape), dtype))

    def ps(self, name, shape, dtype=F32):
        return self.es.enter_context(self.nc.psum_tensor(f"{self.name}_{name}", list(shape), dtype))

    def buf(self, name=None):
        self.nbuf += 1
        b = Buf(name or f"b{self.nbuf}")
        self.bufs.append(b)
        return b

    def _dep(self, o, w, kind):
        if w is o:
            return
        if w.is_dma:
            if o.is_dma and kind == "waw":
                return
            b = w.dbuf
            o.dwaits[b] = max(o.dwaits.get(b, 0), b.dma_total)
            return
        if w.eng == o.eng and not o.is_dma:
            if o.eng == "pe":
                return
            if kind != "raw":
                return
        w.signal = True
        o.waits.append(w)

    def op(self, eng, fn, reads=(), writes=(), dma=False, inc=16):
        o = Op(eng, fn, dma)
        o.inc = inc
        for b in reads:
            if b.last_w is not None:
                self._dep(o, b.last_w, "raw")
        for b in writes:
            if b.last_w is not None:
                self._dep(o, b.last_w, "waw")
            for r in b.readers:
                self._dep(o, r, "war")
        for b in reads:
            b.readers.append(o)
        for b in writes:
            b.last_w = o
            b.readers = []
        if dma:
            d = writes[0]
            kind = "cc" if inc != 16 else eng
            if d.sem is None:
                ent = self.free_dma[kind].pop()
                d.sem = ent[0]
                d.dma_total = ent[1]
                d.ent = ent
                d.kind = kind
            assert d.kind == kind, (d.name, d.kind, kind)
            d.dma_total += inc
            d.ent[1] = d.dma_total
            o.dbuf = d
        self.ops[eng].append(o)
        return o

    def dma(self, out, in_, reads=(), writes=(), eng="sp", **kw):
        return self.op(eng, lambda e: e.dma_start(out=out, in_=in_, **kw), reads=reads, writes=writes, dma=True)

    def emit(self):
        nc = self.nc
        fin_d = {}
        for b in self.bufs:
            if b.sem is not None:
                fin_d[b] = b.dma_total
        lasts = []
        for e in ("pe", "act", "dve", "pool"):
            cands = [o for o in self.ops[e] if not o.is_dma]
            if cands:
                cands[-1].signal = True
                lasts.append(cands[-1])
        for e in ENGS:
            fin = Op(e, None, False)
            fin.dwaits = dict(fin_d)
            fin.waits = list(lasts)
            self.ops[e].append(fin)
        for e in ("pe", "act", "dve", "pool"):
            n = self.pool.eng[e][1]
            for o in self.ops[e]:
                if o.signal and not o.is_dma:
                    n += 1
                    o.sigval = n
            self.pool.eng[e][1] = n
        with nc.Block() as block:
            for e in ENGS:

                def body(eng, e=e):
                    seen = {}
                    for o in self.ops[e]:
                        req = {}
                        for w in o.waits:
                            s = self.esem[w.eng]
                            if req.get(s, 0) < w.sigval:
                                req[s] = w.sigval
                        for b, tot in o.dwaits.items():
                            if req.get(b.sem, 0) < tot:
                                req[b.sem] = tot
                        for s, v in req.items():
                            if seen.get(s, 0) < v:
                                eng.wait_ge(s, v)
                                seen[s] = v
                        if o.fn is None:
                            continue
                        ins = o.fn(eng)
                        if o.is_dma:
                            ins.then_inc(o.dbuf.sem, o.inc)
                        elif o.signal:
                            ins.then_inc(self.esem[e], 1)

                getattr(block, BLOCKNAME[e])(body)
        self.es.close()


class Rot:
    def __init__(self, P, name, n, shape, dtype, psum=False):
        self.items = []
        for i in range(n):
            t = P.ps(f"{name}{i}", shape, dtype) if psum else P.sb(f"{name}{i}", shape, dtype)
            self.items.append((t, P.buf(f"{name}{i}")))
        self.i = 0

    def next(self):
        it = self.items[self.i % len(self.items)]
        self.i += 1
        return it


def bcast_rows(handle, n, offset=0, parts=128):
    return bass.AP(handle, offset, [[0, parts], [1, n]])


def make_ident(P, name="ident"):
    ident = P.sb(name, [128, 128], BF16)
    b = P.buf(name)

    P.op("pool", lambda e: e.memset(ident[:], 1.0), writes=[b])
    P.op("pool", lambda e: e.affine_select(out=ident[:], in_=ident[:], pattern=[[-1, 128]], compare_op=ALU.is_ge, fill=0.0,
                                           base=0, channel_multiplier=1), reads=[b], writes=[b])
    P.op("pool", lambda e: e.affine_select(out=ident[:], in_=ident[:], pattern=[[1, 128]], compare_op=ALU.is_ge, fill=0.0,
                                           base=0, channel_multiplier=-1), reads=[b], writes=[b])
    return ident, b


def eps_tile(P):
    t = P.sb("epsc", [128, 1], F32)
    b = P.buf("epsc")
    P.op("pool", lambda e: e.memset(t[:], EPS), writes=[b])
    return t, b


class NormT:
    def __init__(self, P, ident, ident_b, gains, nbuf=2):
        self.P = P
        self.ident, self.ident_b = ident, ident_b
        self.g = {}
        for k, (h, off) in gains.items():
            t = P.sb(f"gain_{k}", [128, D], F32)
            b = P.buf(f"gain_{k}")
            P.dma(t[:], bcast_rows(h, D, off), writes=[b])
            self.g[k] = (t, b)
        self.sq = Rot(P, "nt_sq", nbuf, [128, D], BF16)
        self.st = Rot(P, "nt_st", 4, [128, 4], F32)
        self.hb = Rot(P, "nt_hb", nbuf, [128, D], BF16)
        self.pt = Rot(P, "nt_pt", 1, [128, 8, 128], BF16, psum=True)
        self.eps, self.epsb = eps_tile(P)

    def rstd(self, xt, xb, width=D):
        P = self.P
        sq, sqb = self.sq.next()
        st, stb = self.st.next()
        P.op("act", lambda e: e.activation(out=sq[:, 0:width], in_=xt, func=AF.Square, accum_out=st[:, 0:1]),
             reads=[xb], writes=[sqb, stb])
        P.op("act", lambda e: e.activation(out=st[:, 1:2], in_=st[:, 0:1], func=AF.Sqrt, bias=self.eps[:, 0:1], scale=1.0 / width),
             reads=[stb, self.epsb], writes=[stb])
        P.op("dve", lambda e: e.reciprocal(out=st[:, 2:3], in_=st[:, 1:2]), reads=[stb], writes=[stb])
        return st, stb

    def run(self, xt, xb, gain, hT, hTb, col0):
        P = self.P
        gt, gb = self.g[gain]
        st, stb = self.rstd(xt, xb)
        hb, hbb = self.hb.next()
        P.op("dve", lambda e: e.scalar_tensor_tensor(out=hb[:], in0=xt, scalar=st[:, 2:3], in1=gt[:],
                                                     op0=ALU.mult, op1=ALU.mult), reads=[xb, stb, gb], writes=[hbb])
        pt, ptb = self.pt.next()

        def tr(e):
            ins = None
            for c in range(8):
                ins = e.transpose(out=pt[:, c, :], in_=hb[:, c * 128:(c + 1) * 128], identity=self.ident[:])
            return ins

        P.op("pe", tr, reads=[hbb, self.ident_b], writes=[ptb])
        P.op("act", lambda e: e.copy(out=hT[:, :, col0:col0 + 128], in_=pt[:]), reads=[ptb], writes=[hTb])


WCOLS = 1024


def load_w(P, name, handle, kchunks, ncols):
    t = P.sb(name, [128, kchunks, ncols], BF16)
    b = P.buf(name)
    src = handle.ap().rearrange("(c p) n -> p c n", p=128)
    for c0 in range(kchunks):
        for n0 in range(0, ncols, WCOLS):
            n1 = min(ncols, n0 + WCOLS)
            P.dma(t[:, c0, n0:n1], src[:, c0, n0:n1], writes=[b], eng="pool")
    return t, b


def mm_group(P, ps, psb, w, wb, kch, wc0, m, rhs, rhsb, rhs_slice, n):
    def f(e):
        ins = None
        for k in range(kch):
            ins = e.matmul(ps[0:m, 0:n], lhsT=w[:, k, wc0:wc0 + m], rhs=rhs[:, k, rhs_slice], start=(k == 0),
                           stop=(k == kch - 1))
        return ins

    P.op("pe", f, reads=[wb, rhsb], writes=[psb])


def mm_group_tm(P, ps, psb, lhs, lhsb, lhs_slice, kch, w, wb, wc0, n):
    def f(e):
        ins = None
        for k in range(kch):
            ins = e.matmul(ps[:, 0:n], lhsT=lhs[:, k, lhs_slice], rhs=w[:, k, wc0:wc0 + n], start=(k == 0),
                           stop=(k == kch - 1))
        return ins

    P.op("pe", f, reads=[lhsb, wb], writes=[psb])


LOOK = 2


def run_pipelined(iters, look):
    n = len(iters)
    for i in range(n + look):
        if i < n:
            iters[i][0]()
        if i - look >= 0:
            it = iters[i - look]
            it[1]()
            if it[2] is not None:
                it[2]()


def attn_fm(nc, name, S, SH, heads, ncomp, scale, look, n_s, n_p, setup, finish, den_pe=()):
    P = Phase(nc, name)
    NT = S // 128
    nq = 512
    NQB = SH // nq
    C = setup(P)
    ps_s = Rot(P, "ps_s", n_s, [128, 512], F32, psum=True)
    ps_o = [Rot(P, f"ps_o{c}", (2 if ncomp == 1 else 1), [128, 512], F32, psum=True) for c in range(ncomp)]
    pT = Rot(P, "pT", n_p, [128, 512], BF16)
    nacc = 2
    accr = [Rot(P, f"acc{i}", 2, [128, 512], F32) for i in range(nacc)]
    pdr = {c: Rot(P, f"pd{c}", 1, [128, 512], F32, psum=True) for c in den_pe}
    ctxs = {0: heads[0]["load"](P, 0)}
    for hi, head in enumerate(heads):
        ctx = ctxs.pop(hi)
        if hi + 1 < len(heads):
            ctxs[hi + 1] = heads[hi + 1]["load"](P, (hi + 1) % 2)
        iters = []
        for qb in range(NQB):
            po = [ps_o[c].next() for c in range(ncomp)]
            acc = [accr[i].next() for i in range(nacc)]
            pd = {c: pdr[c].next() for c in den_pe}
            for kt in range(NT):
                sl = [ps_s.next() for c in range(ncomp)]
                pl = [pT.next() for c in range(ncomp)]

                def rec_qk(sl=sl, pl=pl, kt=kt, qb=qb, ctx=ctx, head=head):
                    for c in range(ncomp):
                        s_, sb_ = sl[c]
                        P.op("pe", lambda e, s_=s_, c=c: head["qk"](e, ctx, s_, c, kt, qb), reads=ctx["kq_bufs"], writes=[sb_])
                    for c in range(ncomp):
                        s_, sb_ = sl[c]
                        p_, pb_ = pl[c]
                        P.op("act", lambda e, s_=s_, p_=p_: e.activation(out=p_[:], in_=s_[:], func=AF.Exp, scale=scale),
                             reads=[sb_], writes=[pb_])

                def rec_pv(pl=pl, kt=kt, po=po, acc=acc, ctx=ctx, pd=pd):
                    vt, vb = ctx["v"]
                    for c in range(ncomp):
                        p_, pb_ = pl[c]
                        o_, ob_ = po[c]
                        P.op("pe", lambda e, p_=p_, o_=o_: e.matmul(o_[:], lhsT=vt[:, kt, :], rhs=p_[:], start=(kt == 0),
                                                                     stop=(kt == NT - 1)), reads=[pb_, vb], writes=[ob_])
                    for c in den_pe:
                        p_, pb_ = pl[c]
                        d_, db_ = pd[c]
                        P.op("pe", lambda e, p_=p_, d_=d_: e.matmul(d_[:], lhsT=C["ones16"][:], rhs=p_[:], start=(kt == 0),
                                                                     stop=(kt == NT - 1)), reads=[pb_, C["ones16b"]], writes=[db_])
                    for c in range(ncomp):
                        if c in den_pe:
                            continue
                        p_, pb_ = pl[c]
                        if ncomp == 2:
                            ai, first = c, (kt == 0)
                        else:
                            ai, first = 0, (kt == 0)
                        a_, ab_ = acc[ai]
                        eng = "dve" if ai == 0 else "pool"
                        if first:
                            P.op(eng, lambda e, a_=a_, p_=p_: e.tensor_copy(out=a_[:], in_=p_[:]), reads=[pb_], writes=[ab_])
                        else:
                            P.op(eng, lambda e, a_=a_, p_=p_: e.tensor_tensor(out=a_[:], in0=a_[:], in1=p_[:], op=ALU.add),
                                 reads=[pb_, ab_], writes=[ab_])

                post = None
                if kt == NT - 1:
                    post = (lambda hi=hi, qb=qb, po=po, acc=acc, pd=pd: finish(P, C, hi, qb, po, acc, pd))
                iters.append((rec_qk, rec_pv, post))
        run_pipelined(iters, look)
    P.emit()


def build(S, npair_groups, debug=False, upto=None):
    SH = S // 2
    NBA = S // 512
    NBO = SH // 512
    NT = S // 128
    NCH = S // 64
    nc = bass.Bass("TRN2", target_bir_lowering=False)
    _es = ExitStack()
    _SEMPOOL[0] = SemPool(nc, _es)

    def din(name, shape):
        return nc.dram_tensor(name, list(shape), F32, kind="ExternalInput")

    def scr(name, shape, dt, dbg=False):
        if dbg and debug:
            return nc.dram_tensor(name, list(shape), dt, kind="ExternalOutput")
        return nc.dram_tensor(name, list(shape), dt)

    x_in = din("x", [S, D])
    w_a = din("w_a", [D, 23 * 128 + 512])
    w_uq = din("w_uq", [384, 1024])
    w_ukv = din("w_ukv", [256, 1024])
    w_o0 = din("w_o0", [D, D])
    w_c = din("w_c", [D, 16 * 128 + 16 * 128 + D])
    w_o1 = din("w_o1", [D, D])
    ffn_g = [din(f"ffn_g{l}", [D, FH]) for l in range(2)]
    ffn_u = [din(f"ffn_u{l}", [D, FH]) for l in range(2)]
    ffn_d = [din(f"ffn_d{l}", [FH, D]) for l in range(2)]
    cols_in = din("cols", [128, 32])
    rows_in = din("rows", [8, D])
    out = nc.dram_tensor("out", [SH, D], F32, kind="ExternalOutput")

    ROPEC = din("ropec_t", [128, S])
    ROPES = din("ropes_t", [128, S])
    QT0 = scr("QT0", [2, 4, 128, SH], BF16)
    KT0 = scr("KT0", [2, 4, 128, SH], BF16)
    KH0 = scr("KH0", [2, 4, S, 128], BF16)
    DEC0 = scr("DEC0", [2, 4, 128, NCH], F32)
    V0 = scr("V0", [S, 512], BF16)
    GATE0 = scr("GATE0", [4, 128, SH], F32)
    OFW = scr("OFW", [4, 128, SH], F32)
    QN = scr("QN", [4, 128, SH], BF16)
    QR = scr("QR", [2, 128, SH], BF16)
    KN = scr("KN", [4, 128, S], BF16)
    KR = scr("KR", [128, S], BF16)
    VBm = scr("VBm", [4, S, 128], BF16)
    MIXT = scr("MIXT", [8, 128, SH], BF16, dbg=True)
    X1A = scr("X1A", [SH, D], F32)
    X1 = scr("X1", [SH, D], F32, dbg=True)
    Q1T = scr("Q1T", [8, 128, SH], BF16)
    VR = min(512, SH)
    NVC = SH // VR
    K1S = [scr(f"K1S{h}", [128, SH], BF16) for h in range(8)]
    V1S = [scr(f"V1S{c}", [VR, D], BF16) for c in range(NVC)]
    K1G = [scr(f"K1G{h}", [2 * 128, SH], BF16) for h in range(8)]
    V1G = [scr(f"V1G{c}", [2 * VR, D], BF16) for c in range(NVC)]
    MIXT1 = scr("MIXT1", [8, 128, SH], BF16, dbg=True)
    X2A = scr("X2A", [SH, D], F32)

    P = Phase(nc, "pA")
    ident, identb = make_ident(P)
    wa, wab = load_w(P, "wa", w_a, 8, 23 * 128 + 512)
    wuq, wuqb = load_w(P, "wuq", w_uq, 3, 1024)
    wukv, wukvb = load_w(P, "wukv", w_ukv, 2, 1024)
    cols = P.sb("cols", [128, 32], F32); colsb = P.buf("cols")
    P.dma(cols[:], cols_in.ap(), writes=[colsb])
    lbt = P.sb("lbt", [128, 32], F32); lbb = P.buf("lbt")
    P.op("dve", lambda e: e.tensor_tensor(out=lbt[:, 0:8], in0=cols[:, 0:8], in1=cols[:, 8:16], op=ALU.subtract),
         reads=[colsb], writes=[lbb])
    P.op("act", lambda e: e.activation(out=lbt[:, 8:16], in_=lbt[:, 0:8], func=AF.Sigmoid), reads=[lbb], writes=[lbb])
    P.op("dve", lambda e: e.tensor_scalar(out=lbt[:, 16:24], in0=lbt[:, 8:16], scalar1=-1.0, scalar2=1.0, op0=ALU.mult,
                                          op1=ALU.add), reads=[lbb], writes=[lbb])
    P.op("dve", lambda e: e.tensor_scalar(out=lbt[:, 24:32], in0=lbt[:, 16:24], scalar1=-1.0, scalar2=None, op0=ALU.mult),
         reads=[lbb], writes=[lbb])
    ones = P.sb("ones", [128, 128], BF16); onesb = P.buf("ones")
    P.op("pool", lambda e: e.memset(ones[:], 1.0), writes=[onesb])
    rmask = P.sb("rmask", [128, 8, 64], F32); rmb = P.buf("rmask")

    P.op("pool", lambda e: e.memset(rmask[:], 1.0), writes=[rmb])
    P.op("pool", lambda e: e.memset(rmask[:, :, 0:1], 0.0), reads=[rmb], writes=[rmb])
    NA = NormT(P, ident, identb, {"attn0": (rows_in, 0)})
    xr = Rot(P, "x", 4, [128, D], F32)
    hTr = Rot(P, "hT", 2, [128, 8, 512], BF16)
    psr = Rot(P, "ps", 4, [128, 512], F32, psum=True)
    ps_ss = Rot(P, "ps_ss", 1, [128, 512], F32, psum=True)
    ps_tm = Rot(P, "ps_tm", 1, [128, 512], F32, psum=True)
    ps_kt = Rot(P, "ps_kt", 1, [128, 4, 128], BF16, psum=True)
    f32r = Rot(P, "f32", 12, [128, 512], F32)
    b16r = Rot(P, "b16", 8, [128, 512], BF16)
    sgr = Rot(P, "sgr", 8, [128, 512], F32)
    qsr = Rot(P, "qs", 5, [128, 512], F32)
    rtab = Rot(P, "rtab", 4, [128, 512], F32)
    lat = Rot(P, "lat", 2, [128, 3, 512], F32)
    latn = Rot(P, "latn", 2, [128, 3, 512], BF16)
    kh16 = Rot(P, "kh16", 3, [128, 4, 128], BF16)
    v16 = Rot(P, "v16", 3, [128, 512], BF16)
    decr = Rot(P, "dec", 4, [128, 8], F32)
    D_ = {k: P.buf(k) for k in "QT0 KT0 KH0 DEC0 V0 GATE0 QN QR KN KR VBm".split()}

    def loadsA(t):
        c0 = t * 512
        xs_ = []
        for j in range(4):
            xt, xb = xr.next()
            P.dma(xt[:], x_in.ap()[c0 + j * 128:c0 + (j + 1) * 128, :], writes=[xb])
            xs_.append((xt, xb))
        ct, ctb = rtab.next(); st_, stb_ = rtab.next()
        P.dma(ct[:], ROPEC.ap()[:, c0:c0 + 512], writes=[ctb])
        P.dma(st_[:], ROPES.ap()[:, c0:c0 + 512], writes=[stb_])
        return xs_, ct, ctb, st_, stb_

    pend = {0: loadsA(0)}
    for t in range(NBA):
        own = t < NBO
        c0 = t * 512
        sl = slice(c0, c0 + 512)
        xs_, ct, ctb, st_, stb_ = pend.pop(t)
        hT, hTb = hTr.next()
        for j in range(4):
            NA.run(xs_[j][0][:], xs_[j][1], "attn0", hT, hTb, j * 128)
        if t + 1 < NBA:
            pend[t + 1] = loadsA(t + 1)

        def fm(group, ps, psb):
            mm_group(P, ps, psb, wa, wab, 8, group * 128, 128, hT, hTb, slice(0, 512), 512)

        for j in range(4):
            ps, psb = ps_tm.next()
            mm_group_tm(P, ps, psb, hT, hTb, slice(j * 128, (j + 1) * 128), 8, wa, wab, 23 * 128, 512)
            v, vb = v16.next()
            P.op("act", lambda e, v=v, ps=ps: e.copy(out=v[:], in_=ps[:]), reads=[psb], writes=[vb])
            P.dma(V0.ap()[c0 + j * 128:c0 + (j + 1) * 128, :], v[:], reads=[vb], writes=[D_["V0"]])
        qs = []
        if own:
            for h in range(4):
                ps, psb = psr.next()
                fm(h, ps, psb)
                q, qb_ = qsr.next()
                P.op("act", lambda e, q=q, ps=ps: e.activation(out=q[:], in_=ps[:], func=AF.Silu), reads=[psb], writes=[qb_])
                qs.append((q, qb_))
            for h in range(4):
                ps, psb = psr.next()
                fm(12 + h, ps, psb)
                g, gb = f32r.next()
                P.op("act", lambda e, g=g, ps=ps: e.activation(out=g[:], in_=ps[:], func=AF.Silu), reads=[psb], writes=[gb])
                P.dma(GATE0.ap()[h, :, sl], g[:], reads=[gb], writes=[D_["GATE0"]])
        sig_l = []
        for d in ((0, 1) if own else (1,)):
            for h in range(4):
                ps, psb = psr.next()
                fm(4 + 4 * d + h, ps, psb)
                sg, sgb = sgr.next()
                P.op("act", lambda e, sg=sg, ps=ps: e.activation(out=sg[:], in_=ps[:], func=AF.Sigmoid), reads=[psb], writes=[sgb])
                sig_l.append((d, h, sg, sgb))
        for (d, h, sg, sgb) in sig_l:
            ci = d * 4 + h
            g, gb = f32r.next()
            P.op("act", lambda e, g=g, sg=sg, ci=ci: e.activation(out=g[:], in_=sg[:], func=AF.Ln, bias=lbt[:, 8 + ci:9 + ci],
                                                                 scale=lbt[:, 16 + ci:17 + ci]), reads=[sgb, lbb], writes=[gb])
            k, kb = f32r.next()
            P.op("pool", lambda e, k=k, sg=sg, ci=ci: e.tensor_scalar(out=k[:], in0=sg[:], scalar1=lbt[:, 24 + ci:25 + ci],
                                                                     scalar2=lbt[:, 16 + ci:17 + ci], op0=ALU.mult, op1=ALU.add),
                 reads=[sgb, lbb], writes=[kb])
            b_, bb = f32r.next()
            P.op("dve", lambda e, b_=b_, g=g: e.tensor_tensor_scan(out=b_[:], data0=rmask[:].rearrange("p a b -> p (a b)"),
                                                                  data1=g[:], initial=0.0, op0=ALU.mult, op1=ALU.add),
                 reads=[gb, rmb], writes=[bb])
            b3 = b_[:].rearrange("p (a b) -> p a b", b=64)
            dl, dlb = f32r.next()
            P.op("pool", lambda e, dl=dl, b3=b3: e.tensor_tensor(out=dl[:].rearrange("p (a b) -> p a b", b=64), in0=b3,
                                                               in1=b3[:, :, 63:64].to_broadcast([128, 8, 64]), op=ALU.subtract),
                 reads=[bb], writes=[dlb])
            dec, decb = decr.next()
            P.op("act", lambda e, dec=dec, b3=b3: e.activation(out=dec[:], in_=b3[:, :, 63], func=AF.Exp), reads=[bb], writes=[decb])
            P.dma(DEC0.ap()[d, h, :, t * 8:(t + 1) * 8], dec[:], reads=[decb], writes=[D_["DEC0"]])
            if d == 0:
                bq, bqb = b_, bb
                ex, exb = dl, dlb
                ex_scale = -1.0
            else:
                bq, bqb = f32r.next()
                P.op("pool", lambda e, bq=bq, g=g, dl=dl: e.tensor_tensor(out=bq[:], in0=g[:], in1=dl[:], op=ALU.subtract),
                     reads=[gb, dlb], writes=[bqb])
                ex, exb = f32r.next()
                P.op("pool", lambda e, ex=ex, b_=b_, g=g: e.tensor_tensor(out=ex[:], in0=b_[:], in1=g[:], op=ALU.subtract),
                     reads=[bb, gb], writes=[exb])
                ex_scale = 1.0
            ek, ekb = f32r.next()
            P.op("act", lambda e, ek=ek, ex=ex, ex_scale=ex_scale: e.activation(out=ek[:], in_=ex[:], func=AF.Exp, scale=ex_scale),
                 reads=[exb], writes=[ekb])
            kh, khb = b16r.next()
            P.op("dve", lambda e, kh=kh, k=k, ek=ek: e.tensor_tensor(out=kh[:], in0=k[:], in1=ek[:], op=ALU.mult),
                 reads=[kb, ekb], writes=[khb])
            pk, pkb = ps_kt.next()

            def trk(e, pk=pk, kh=kh):
                ins = None
                for j in range(4):
                    ins = e.transpose(out=pk[:, j, :], in_=kh[:, j * 128:(j + 1) * 128], identity=ident[:])
                return ins

            P.op("pe", trk, reads=[khb, identb], writes=[pkb])
            kt_, ktb = kh16.next()
            P.op("act", lambda e, kt_=kt_, pk=pk: e.copy(out=kt_[:], in_=pk[:]), reads=[pkb], writes=[ktb])
            P.dma(KH0.ap()[d, h, sl, :].rearrange("(j p) k -> p j k", p=128), kt_[:], reads=[ktb], writes=[D_["KH0"]])
            if own:
                eq, eqb = f32r.next()
                P.op("act", lambda e, eq=eq, bq=bq: e.activation(out=eq[:], in_=bq[:], func=AF.Exp), reads=[bqb], writes=[eqb])
                en, enb = f32r.next()
                P.op("act", lambda e, en=en, bq=bq: e.activation(out=en[:], in_=bq[:], func=AF.Exp, scale=-1.0), reads=[bqb], writes=[enb])
                qt, qtb = b16r.next()
                q, qb_ = qs[h]
                P.op("dve", lambda e, qt=qt, q=q, eq=eq: e.scalar_tensor_tensor(out=qt[:], in0=q[:], scalar=128 ** -0.5, in1=eq[:],
                                                                              op0=ALU.mult, op1=ALU.mult), reads=[qb_, eqb], writes=[qtb])
                P.dma(QT0.ap()[d, h, :, sl], qt[:], reads=[qtb], writes=[D_["QT0"]])
                kt2, kt2b = b16r.next()
                P.op("pool", lambda e, kt2=kt2, k=k, en=en: e.tensor_tensor(out=kt2[:], in0=k[:], in1=en[:], op=ALU.mult),
                     reads=[kb, enb], writes=[kt2b])
                P.dma(KT0.ap()[d, h, :, sl], kt2[:], reads=[kt2b], writes=[D_["KT0"]])
        lat_jobs = ([(16, 3, 20)] if own else []) + [(19, 2, 23)]
        for (g0, ng, gcol) in lat_jobs:
            lt, ltb = lat.next()
            for i in range(ng):
                ps, psb = psr.next()
                fm(g0 + i, ps, psb)
                P.op("act", lambda e, lt=lt, ps=ps, i=i: e.copy(out=lt[:, i, :], in_=ps[:]), reads=[psb], writes=[ltb])
            ss, ssb = ps_ss.next()
            sqs = []
            for i in range(ng):
                sq, sqb = b16r.next()
                P.op("pool", lambda e, sq=sq, lt=lt, i=i: e.tensor_tensor(out=sq[:], in0=lt[:, i, :], in1=lt[:, i, :], op=ALU.mult),
                     reads=[ltb], writes=[sqb])
                sqs.append((sq, sqb))

            def ssm(e, ss=ss, sqs=sqs):
                ins = None
                for i, (sq, _) in enumerate(sqs):
                    ins = e.matmul(ss[:], lhsT=ones[:], rhs=sq[:], start=(i == 0), stop=(i == len(sqs) - 1))
                return ins

            P.op("pe", ssm, reads=[onesb] + [b for _, b in sqs], writes=[ssb])
            rs, rsb = f32r.next()
            P.op("act", lambda e, rs=rs, ss=ss, ng=ng: e.activation(out=rs[:], in_=ss[:], func=AF.Sqrt, bias=NA.eps[:, 0:1],
                                                                   scale=1.0 / (ng * 128)), reads=[ssb, NA.epsb], writes=[rsb])
            P.op("dve", lambda e, rs=rs: e.reciprocal(out=rs[:], in_=rs[:]), reads=[rsb], writes=[rsb])
            ln, lnb = latn.next()
            for i in range(ng):
                P.op("dve", lambda e, ln=ln, lt=lt, rs=rs, i=i, gcol=gcol: e.scalar_tensor_tensor(
                    out=ln[:, i, :], in0=lt[:, i, :], scalar=cols[:, gcol + i:gcol + i + 1], in1=rs[:], op0=ALU.mult, op1=ALU.mult),
                    reads=[ltb, rsb, colsb], writes=[lnb])
            if g0 == 16:
                for h in range(4):
                    ps, psb = psr.next()
                    mm_group(P, ps, psb, wuq, wuqb, 3, h * 128, 128, ln, lnb, slice(0, 512), 512)
                    o, ob = b16r.next()
                    P.op("act", lambda e, o=o, ps=ps: e.copy(out=o[:], in_=ps[:]), reads=[psb], writes=[ob])
                    P.dma(QN.ap()[h, :, sl], o[:], reads=[ob], writes=[D_["QN"]])
                for pr in range(2):
                    ps, psb = psr.next()
                    mm_group(P, ps, psb, wuq, wuqb, 3, (4 + pr) * 128, 128, ln, lnb, slice(0, 512), 512)
                    ps2, ps2b = psr.next()
                    mm_group(P, ps2, ps2b, wuq, wuqb, 3, (6 + pr) * 128, 128, ln, lnb, slice(0, 512), 512)
                    t1, t1b = f32r.next()
                    P.op("dve", lambda e, t1=t1, ps=ps, ct=ct: e.tensor_tensor(out=t1[:], in0=ps[:], in1=ct[:], op=ALU.mult),
                         reads=[psb, ctb], writes=[t1b])
                    t2, t2b = f32r.next()
                    P.op("dve", lambda e, t2=t2, ps2=ps2, st_=st_: e.tensor_tensor(out=t2[:], in0=ps2[:], in1=st_[:], op=ALU.mult),
                         reads=[ps2b, stb_], writes=[t2b])
                    o, ob = b16r.next()
                    P.op("pool", lambda e, o=o, t1=t1, t2=t2: e.tensor_tensor(out=o[:], in0=t1[:], in1=t2[:], op=ALU.add),
                         reads=[t1b, t2b], writes=[ob])
                    P.dma(QR.ap()[pr, :, sl], o[:], reads=[ob], writes=[D_["QR"]])
            else:
                for h in range(4):
                    ps, psb = psr.next()
                    mm_group(P, ps, psb, wukv, wukvb, 2, h * 128, 128, ln, lnb, slice(0, 512), 512)
                    o, ob = b16r.next()
                    P.op("act", lambda e, o=o, ps=ps: e.copy(out=o[:], in_=ps[:]), reads=[psb], writes=[ob])
                    P.dma(KN.ap()[h, :, sl], o[:], reads=[ob], writes=[D_["KN"]])
                for j in range(4):
                    ps, psb = ps_tm.next()
                    mm_group_tm(P, ps, psb, ln, lnb, slice(j * 128, (j + 1) * 128), 2, wukv, wukvb, 512, 512)
                    v, vb = v16.next()
                    P.op("act", lambda e, v=v, ps=ps: e.copy(out=v[:], in_=ps[:]), reads=[psb], writes=[vb])
                    P.dma(VBm.ap()[:, c0 + j * 128:c0 + (j + 1) * 128, :].rearrange("h p d -> p h d"),
                          v[:].rearrange("p (h d) -> p h d", d=128), reads=[vb], writes=[D_["VBm"]])
        ps, psb = psr.next()
        fm(21, ps, psb)
        ps2, ps2b = psr.next()
        fm(22, ps2, ps2b)
        t1, t1b = f32r.next()
        P.op("dve", lambda e, t1=t1, ps=ps, ct=ct: e.tensor_tensor(out=t1[:], in0=ps[:], in1=ct[:], op=ALU.mult), reads=[psb, ctb], writes=[t1b])
        t2, t2b = f32r.next()
        P.op("dve", lambda e, t2=t2, ps2=ps2, st_=st_: e.tensor_tensor(out=t2[:], in0=ps2[:], in1=st_[:], op=ALU.mult), reads=[ps2b, stb_], writes=[t2b])
        o, ob = b16r.next()
        P.op("pool", lambda e, o=o, t1=t1, t2=t2: e.tensor_tensor(out=o[:], in0=t1[:], in1=t2[:], op=ALU.add), reads=[t1b, t2b], writes=[ob])
        P.dma(KR.ap()[:, sl], o[:], reads=[ob], writes=[D_["KR"]])
    P.emit()
    if upto == "A":
        return nc

    P = Phase(nc, "pB")
    cols = P.sb("cols", [128, 32], F32); colsb = P.buf("cols")
    P.dma(cols[:], cols_in.ap(), writes=[colsb])
    ones = P.sb("ones", [128, 128], BF16); onesb = P.buf("ones")
    P.op("pool", lambda e: e.memset(ones[:], 1.0), writes=[onesb])
    epsB, epsBb = eps_tile(P)
    masks = []
    for d in range(2):
        m = P.sb(f"mask{d}", [128, 128], F32); mb = P.buf(f"mask{d}")

        P.op("pool", lambda e, m=m: e.memset(m[:], 1.0), writes=[mb])
        if d == 0:
            P.op("pool", lambda e, m=m: e.affine_select(out=m[:], in_=m[:], pattern=[[1, 128]], compare_op=ALU.is_ge, fill=0.0,
                                                       base=0, channel_multiplier=-1), reads=[mb], writes=[mb])
            P.op("pool", lambda e, m=m: e.memset(m[0:64, 64:128], 0.0), reads=[mb], writes=[mb])
        else:
            P.op("pool", lambda e, m=m: e.affine_select(out=m[:], in_=m[:], pattern=[[-1, 128]], compare_op=ALU.is_ge, fill=0.0,
                                                       base=0, channel_multiplier=1), reads=[mb], writes=[mb])
            P.op("pool", lambda e, m=m: e.memset(m[64:128, 0:64], 0.0), reads=[mb], writes=[mb])
        m2 = P.sb(f"maskf{d}", [128, 128], F32); m2b = P.buf(f"maskf{d}")
        P.op("dve", lambda e, m=m, m2=m2: e.tensor_copy(out=m2[:], in_=m[:]), reads=[mb], writes=[m2b])
        masks.append((m2, m2b))
    St = P.sb("St", [128, 4, 128], F32); Sb = [P.buf(f"S{h}") for h in range(4)]
    S16 = P.sb("S16", [128, 4, 128], BF16); S16b = P.buf("S16")
    qtr = Rot(P, "qt", 2, [128, 4, 512], BF16)
    ktr = Rot(P, "kt", 2, [128, 4, 512], BF16)
    khr = Rot(P, "kh", 2, [128, 4, 4, 128], BF16)
    vr = Rot(P, "v", 2, [128, 4, 512], BF16)
    dcr = Rot(P, "dc", 2, [128, 4, 8], F32)
    ofr = Rot(P, "of", 2, [128, 4, 512], F32)
    gtr = Rot(P, "gt", 2, [128, 4, 512], F32)
    psO = [Rot(P, f"psO{h}", 1, [128, 512], F32, psum=True) for h in range(4)]
    psXr = Rot(P, "psX", 2, [128, 4, 128], F32, psum=True)
    psSr = Rot(P, "psS", 1, [128, 4, 128], F32, psum=True)
    ps_ss = Rot(P, "ps_ss", 1, [128, 512], F32, psum=True)
    atr = Rot(P, "at", 2, [128, 4, 128], BF16)
    sc32r = Rot(P, "sc32", 2, [128, 4, 128], F32)
    ev = Rot(P, "ev", 6, [128, 512], F32)
    e16 = Rot(P, "e16", 4, [128, 512], BF16)
    DOF = P.buf("OFW"); DMX = P.buf("MIXT")

    def reset_state():
        for h in range(4):
            P.op("pool", lambda e, h=h: e.memset(St[:, h, :], 0.0), writes=[Sb[h]])
        P.op("pool", lambda e: e.memset(S16[:], 0.0), writes=[S16b])

    def hgrn_block(d, t, full, final):
        c0 = t * 512
        sl = slice(c0, c0 + 512)
        kh, khb = khr.next()
        for h in range(4):
            P.dma(kh[:, h, :, :], KH0.ap()[d, h, sl, :].rearrange("(j p) k -> p j k", p=128), writes=[khb])
        v, vb = vr.next()
        P.dma(v[:], V0.ap()[sl, :].rearrange("(j p) c -> p j c", p=128), writes=[vb])
        dc, dcb = dcr.next()
        P.dma(dc[:], DEC0.ap()[d, :, :, t * 8:(t + 1) * 8].rearrange("h p c -> p h c"), writes=[dcb])
        if full:
            qt, qtb = qtr.next()
            P.dma(qt[:], QT0.ap()[d, :, :, sl].rearrange("h p s -> p h s"), writes=[qtb])
            kt, ktb = ktr.next()
            P.dma(kt[:], KT0.ap()[d, :, :, sl].rearrange("h p s -> p h s"), writes=[ktb])
            pso = [psO[h].next() for h in range(4)]
        if final:
            of, ofb = ofr.next()
            P.dma(of[:], OFW.ap()[:, :, sl].rearrange("h p s -> p h s"), reads=[DOF], writes=[ofb])
            gt, gtb = gtr.next()
            P.dma(gt[:], GATE0.ap()[:, :, sl].rearrange("h p s -> p h s"), writes=[gtb])
        m, mb = masks[d]
        tiles = range(4) if d == 0 else range(3, -1, -1)
        for j in tiles:
            chunks = (0, 1) if d == 0 else (1, 0)
            if full:
                pS, pSb = psSr.next()

                def sc(e, pS=pS, j=j):
                    ins = None
                    for h in range(4):
                        ins = e.matmul(pS[:, h, :], lhsT=kt[:, h, j * 128:(j + 1) * 128], rhs=qt[:, h, j * 128:(j + 1) * 128],
                                       start=True, stop=True)
                    return ins

                P.op("pe", sc, reads=[ktb, qtb], writes=[pSb])
                sc32, sc32b = sc32r.next()
                P.op("act", lambda e, sc32=sc32, pS=pS: e.copy(out=sc32[:], in_=pS[:]), reads=[pSb], writes=[sc32b])
                at, atb = atr.next()
                P.op("pool", lambda e, at=at, sc32=sc32: e.tensor_tensor(out=at[:], in0=sc32[:],
                                                                       in1=m[:].unsqueeze(1).to_broadcast([128, 4, 128]), op=ALU.mult),
                     reads=[sc32b, mb], writes=[atb])
            for c in chunks:
                pr = slice(64 * c, 64 * c + 64)
                co = j * 128 + c * 64
                cidx = j * 2 + c
                if full:
                    for h in range(4):
                        po, pob = pso[h]

                        def om(e, at=at, po=po, h=h, co=co, c=c, j=j):
                            e.matmul(po[:, co:co + 64], lhsT=v[:, j, h * 128:(h + 1) * 128], rhs=at[:, h, c * 64:(c + 1) * 64],
                                     start=True, stop=False)
                            return e.matmul(po[:, co:co + 64], lhsT=S16[:, h, :], rhs=qt[:, h, co:co + 64], start=False, stop=True)

                        P.op("pe", om, reads=[vb, atb, S16b, qtb], writes=[pob])
                pX, pXb = psXr.next()

                def su(e, pX=pX, pr=pr, j=j):
                    ins = None
                    for h in range(4):
                        ins = e.matmul(pX[:, h, :], lhsT=kh[pr, h, j, :], rhs=v[pr, j, h * 128:(h + 1) * 128], start=True, stop=True)
                    return ins

                P.op("pe", su, reads=[khb, vb], writes=[pXb])
                for h in range(4):
                    P.op("dve", lambda e, pX=pX, h=h, cidx=cidx: e.scalar_tensor_tensor(
                        out=St[:, h, :], in0=St[:, h, :], scalar=dc[:, h, cidx:cidx + 1], in1=pX[:, h, :], op0=ALU.mult, op1=ALU.add),
                        reads=[Sb[h], dcb, pXb], writes=[Sb[h]])
                P.op("act", lambda e: e.copy(out=S16[:], in_=St[:]), reads=Sb, writes=[S16b])
        if full:
            for h in range(4):
                po, pob = pso[h]
                if not final:
                    o, ob = ev.next()
                    P.op("act", lambda e, o=o, po=po: e.copy(out=o[:], in_=po[:]), reads=[pob], writes=[ob])
                    P.dma(OFW.ap()[h, :, sl], o[:], reads=[ob], writes=[DOF], eng="pool")
                else:
                    tot, totb = ev.next()
                    P.op("dve", lambda e, tot=tot, po=po, h=h: e.tensor_tensor(out=tot[:], in0=po[:], in1=of[:, h, :], op=ALU.add),
                         reads=[pob, ofb], writes=[totb])
                    sq, sqb = e16.next()
                    P.op("act", lambda e, sq=sq, tot=tot: e.activation(out=sq[:], in_=tot[:], func=AF.Square), reads=[totb], writes=[sqb])
                    ss, ssb = ps_ss.next()
                    P.op("pe", lambda e, ss=ss, sq=sq: e.matmul(ss[:], lhsT=ones[:], rhs=sq[:], start=True, stop=True),
                         reads=[onesb, sqb], writes=[ssb])
                    rs, rsb = ev.next()
                    P.op("act", lambda e, rs=rs, ss=ss: e.activation(out=rs[:], in_=ss[:], func=AF.Sqrt, bias=epsB[:, 0:1], scale=1.0 / 128),
                         reads=[ssb, epsBb], writes=[rsb])
                    P.op("dve", lambda e, rs=rs: e.reciprocal(out=rs[:], in_=rs[:]), reads=[rsb], writes=[rsb])
                    P.op("pool", lambda e, tot=tot, rs=rs: e.tensor_tensor(out=tot[:], in0=tot[:], in1=rs[:], op=ALU.mult),
                         reads=[totb, rsb], writes=[totb])
                    o, ob = e16.next()
                    P.op("dve", lambda e, o=o, tot=tot, h=h: e.scalar_tensor_tensor(out=o[:], in0=tot[:], scalar=cols[:, 16 + h:17 + h],
                                                                                   in1=gt[:, h, :], op0=ALU.mult, op1=ALU.mult),
                         reads=[totb, colsb, gtb], writes=[ob])
                    P.dma(MIXT.ap()[h, :, sl], o[:], reads=[ob], writes=[DMX], eng="pool")

    reset_state()
    for t in range(NBO):
        hgrn_block(0, t, True, False)
    reset_state()
    for t in range(NBA - 1, NBO - 1, -1):
        hgrn_block(1, t, False, False)
    for t in range(NBO - 1, -1, -1):
        hgrn_block(1, t, True, True)
    P.emit()
    if upto == "B":
        return nc

    def attn_setup_common(P):
        C = dict(mx=P.buf("MIXOUT"))
        ones32 = P.sb("ones32", [128, 128], F32); o32b = P.buf("ones32")
        P.op("pool", lambda e: e.memset(ones32[:], 1.0), writes=[o32b])
        C.update(ones32=ones32, o32b=o32b, ss=Rot(P, "ss", 1, [128, 512], F32, psum=True),
                 r32=Rot(P, "r32", 4, [128, 512], F32), o16=Rot(P, "o16", 2, [128, 512], BF16))
        return C

    def mla_setup(P):
        C = attn_setup_common(P)
        C["K0"] = Rot(P, "K0", 2, [128, S], BF16)
        C["K1"] = Rot(P, "K1", 2, [128, S], BF16)
        C["Q0"] = Rot(P, "Q0", 2, [128, SH], BF16)
        C["Q1"] = Rot(P, "Q1", 2, [128, SH], BF16)
        C["VT"] = Rot(P, "V", 2, [128, NT, 128], BF16)
        for rot in (C["K1"], C["Q1"]):
            for (t_, b_) in rot.items:
                P.op("pool", lambda e, t_=t_: e.memset(t_[:], 0.0), writes=[b_])
        return C

    def den_matmul(P, C, accs):
        ss, ssb = C["ss"].next()

        def f(e):
            ins = None
            for i, (a_, _) in enumerate(accs):
                ins = e.matmul(ss[:], lhsT=C["ones32"][:], rhs=a_[:], start=(i == 0), stop=(i == len(accs) - 1))
            return ins

        P.op("pe", f, reads=[C["o32b"]] + [ab_ for _, ab_ in accs], writes=[ssb])
        return ss, ssb

    def mla_finish(P, C, hi, qb, po, acc, pd):
        ss, ssb = den_matmul(P, C, [acc[0]])
        r, rb = C["r32"].next()
        P.op("dve", lambda e: e.reciprocal(out=r[:], in_=ss[:]), reads=[ssb], writes=[rb])
        o_, ob_ = po[0]
        o16, o16b = C["o16"].next()
        P.op("dve", lambda e: e.tensor_tensor(out=o16[:], in0=o_[:], in1=r[:], op=ALU.mult), reads=[ob_, rb], writes=[o16b])
        P.dma(MIXT.ap()[4 + hi, :, qb * 512:(qb + 1) * 512], o16[:], reads=[o16b], writes=[C["mx"]], eng="pool")

    def mla_head(h):
        pb = 64 * (h % 2)

        def load(P, slot):
            C = mla_C[0]
            k0, k0b = C["K0"].next()
            k1, k1b = C["K1"].next()
            q0, q0b = C["Q0"].next()
            q1, q1b = C["Q1"].next()
            vt, vb = C["VT"].next()
            half = S // 2
            for lo, hi_ in ((0, half), (half, S)):
                P.dma(k0[:, lo:hi_], KN.ap()[h][:, lo:hi_], writes=[k0b])
                P.dma(k1[0:64, lo:hi_], KR.ap()[0:64, lo:hi_], writes=[k1b])
            P.dma(q0[:], QN.ap()[h], writes=[q0b])
            P.dma(q1[0:64, :], QR.ap()[h // 2, pb:pb + 64, :], writes=[q1b])
            vsrc = VBm.ap()[h].rearrange("(n p) d -> p n d", p=128)
            for n0 in range(0, NT, 8):
                n1 = min(NT, n0 + 8)
                P.dma(vt[:, n0:n1, :], vsrc[:, n0:n1, :], writes=[vb])
            return dict(kq_bufs=[k0b, k1b, q0b, q1b], v=(vt, vb), k0=k0, k1=k1, q0=q0, q1=q1, pb=pb)

        def qk(e, ctx, s_, c, kt, qb):
            pb_ = ctx["pb"]
            e.matmul(s_[:], lhsT=ctx["k0"][:, kt * 128:(kt + 1) * 128], rhs=ctx["q0"][:, qb * 512:(qb + 1) * 512], start=True, stop=False)
            return e.matmul(s_[:], lhsT=ctx["k1"][:, kt * 128:(kt + 1) * 128],
                            rhs=ctx["q1"][:, qb * 512:(qb + 1) * 512], start=False, stop=True)

        return dict(load=load, qk=qk)

    mla_C = [None]

    def mla_setup_wrap(P):
        mla_C[0] = mla_setup(P)
        return mla_C[0]

    attn_fm(nc, "pC", S, SH, [mla_head(h) for h in range(4)], 1, MLA_SCALE, 2, 3, 4, mla_setup_wrap, mla_finish)
    if upto == "C":
        return nc

    def outproj_phase(name, MIX, w_o, XIN, xin_is_input, XOUT):
        P = Phase(nc, name)
        wo, wob = load_w(P, "wo", w_o, 8, D)
        mxr = Rot(P, "mx", 2, [128, 8, 512], BF16)
        xr = Rot(P, "x", 3, [128, D], F32)
        psr = Rot(P, "ps", 4, [128, 512], F32, psum=True)
        XO = P.buf("XO")
        for t in range(NBO):
            sl = slice(t * 512, (t + 1) * 512)
            mx, mxb = mxr.next()
            P.dma(mx[:], MIX.ap()[:, :, sl].rearrange("c p s -> p c s"), writes=[mxb])
            for j in range(4):
                r0 = t * 512 + j * 128
                xt, xb = xr.next()
                P.dma(xt[:], XIN.ap()[r0:r0 + 128, :], writes=[xb])
                for n in range(2):
                    ps, psb = psr.next()
                    mm_group_tm(P, ps, psb, mx, mxb, slice(j * 128, (j + 1) * 128), 8, wo, wob, n * 512, 512)
                    P.op("dve", lambda e, xt=xt, ps=ps, n=n: e.tensor_tensor(out=xt[:, n * 512:(n + 1) * 512], in0=xt[:, n * 512:(n + 1) * 512],
                                                                           in1=ps[:], op=ALU.add), reads=[psb, xb], writes=[xb])
                P.dma(XOUT.ap()[r0:r0 + 128, :], xt[:], reads=[xb], writes=[XO], eng="pool")
        P.emit()

    def ffn_phase(name, l, XIN, XOUT, final):
        P = Phase(nc, name)
        ident, identb = make_ident(P)
        wg, wgb = load_w(P, "wg", ffn_g[l], 8, FH)
        wu, wub = load_w(P, "wu", ffn_u[l], 8, FH)
        wd, wdb = load_w(P, "wd", ffn_d[l], NHC, D)
        gains = {"ffn": (rows_in, (2 + l) * D)}
        NA = NormT(P, ident, identb, gains, nbuf=1)
        if final:
            gf = P.sb("gfin", [128, D], F32); gfb = P.buf("gfin")
            P.dma(gf[:], bcast_rows(rows_in, D, 4 * D), writes=[gfb])
        xs = [(P.sb(f"x{i}", [128, D], F32), P.buf(f"x{i}")) for i in range(5)]
        hTr = Rot(P, "hT", 1, [128, 8, 512], BF16)
        actr = Rot(P, "act", 1, [128, NHC, 512], BF16)
        psg = Rot(P, "psg", 3, [128, 512], F32, psum=True)
        psu = Rot(P, "psu", 2, [128, 512], F32, psum=True)
        psd = Rot(P, "psd", 2, [128, 512], F32, psum=True)
        sg = Rot(P, "sg", 2, [128, 512], F32)
        XO = P.buf("XO")
        for t in range(NBO):
            hT, hTb = hTr.next()
            xt_l = []
            for j in range(4):
                xt, xb = xs[(t * 4 + j) % 5]
                r0 = t * 512 + j * 128
                P.dma(xt[:], XIN.ap()[r0:r0 + 128, :], writes=[xb])
                NA.run(xt[:], xb, "ffn", hT, hTb, j * 128)
                xt_l.append((xt, xb))
            act, actb = actr.next()
            for hc in range(NHC):
                pg, pgb = psg.next()
                mm_group(P, pg, pgb, wg, wgb, 8, hc * 128, 128, hT, hTb, slice(0, 512), 512)
                pu, pub = psu.next()
                mm_group(P, pu, pub, wu, wub, 8, hc * 128, 128, hT, hTb, slice(0, 512), 512)
                s_, sb_ = sg.next()
                P.op("act", lambda e, s_=s_, pg=pg: e.activation(out=s_[:], in_=pg[:], func=AF.Silu), reads=[pgb], writes=[sb_])
                P.op("dve", lambda e, s_=s_, pu=pu, hc=hc, act=act: e.tensor_tensor(out=act[:, hc, :], in0=s_[:], in1=pu[:], op=ALU.mult),
                     reads=[sb_, pub], writes=[actb])
            for j in range(4):
                xt, xb = xt_l[j]
                r0 = t * 512 + j * 128
                for n in range(2):
                    pd, pdb = psd.next()
                    mm_group_tm(P, pd, pdb, act, actb, slice(j * 128, (j + 1) * 128), NHC, wd, wdb, n * 512, 512)
                    P.op("dve", lambda e, xt=xt, pd=pd, n=n: e.tensor_tensor(out=xt[:, n * 512:(n + 1) * 512], in0=xt[:, n * 512:(n + 1) * 512],
                                                                           in1=pd[:], op=ALU.add), reads=[pdb, xb], writes=[xb])
                if final:
                    st, stb = NA.rstd(xt[:], xb)
                    P.op("dve", lambda e, xt=xt, st=st: e.scalar_tensor_tensor(out=xt[:], in0=xt[:], scalar=st[:, 2:3], in1=gf[:],
                                                                             op0=ALU.mult, op1=ALU.mult), reads=[xb, stb, gfb], writes=[xb])
                P.dma(XOUT.ap()[r0:r0 + 128, :], xt[:], reads=[xb], writes=[XO], eng="pool")
        P.emit()

    outproj_phase("pD", MIXT, w_o0, x_in, True, X1A)
    if upto == "D":
        return nc
    ffn_phase("pE", 0, X1A, X1, False)
    if upto == "E":
        return nc

    P = Phase(nc, "pF")
    ident, identb = make_ident(P)
    wc, wcb = load_w(P, "wc", w_c, 8, 32 * 128 + D)
    NA = NormT(P, ident, identb, {"attn1": (rows_in, D)})
    xr = Rot(P, "x", 4, [128, D], F32)
    hTr = Rot(P, "hT", 2, [128, 8, 512], BF16)
    psr = Rot(P, "ps", 5, [128, 512], F32, psum=True)
    ps_tm = Rot(P, "ps_tm", 2, [128, 512], F32, psum=True)
    f32r = Rot(P, "f32", 6, [128, 512], F32)
    b16r = Rot(P, "b16", 4, [128, 512], BF16)
    v16 = Rot(P, "v16", 3, [128, D], BF16)
    rtab = Rot(P, "rtab", 4, [128, 512], F32)
    DQ = P.buf("Q1T"); DK = P.buf("K1S"); DV = P.buf("V1S"); DKG = P.buf("K1G"); DVG = P.buf("V1G")
    def loadsF(t):
        c0 = t * 512
        xs_ = []
        for j in range(4):
            xt, xb = xr.next()
            P.dma(xt[:], X1.ap()[c0 + j * 128:c0 + (j + 1) * 128, :], writes=[xb])
            xs_.append((xt, xb))
        ct, ctb = rtab.next(); st_, stb_ = rtab.next()
        P.dma(ct[:], ROPEC.ap()[:, c0:c0 + 512], writes=[ctb])
        P.dma(st_[:], ROPES.ap()[:, c0:c0 + 512], writes=[stb_])
        return xs_, ct, ctb, st_, stb_

    pend = {0: loadsF(0)}
    for t in range(NBO):
        c0 = t * 512
        sl = slice(c0, c0 + 512)
        xs_, ct, ctb, st_, stb_ = pend.pop(t)
        hT, hTb = hTr.next()
        for j in range(4):
            NA.run(xs_[j][0][:], xs_[j][1], "attn1", hT, hTb, j * 128)
        if t + 1 < NBO:
            pend[t + 1] = loadsF(t + 1)
        for j in range(4):
            v, vb = v16.next()
            for n in range(2):
                ps, psb = ps_tm.next()
                mm_group_tm(P, ps, psb, hT, hTb, slice(j * 128, (j + 1) * 128), 8, wc, wcb, 32 * 128 + n * 512, 512)
                P.op("act", lambda e, v=v, ps=ps, n=n: e.copy(out=v[:, n * 512:(n + 1) * 512], in_=ps[:]), reads=[psb], writes=[vb])
            r0 = c0 + j * 128
            P.dma(V1S[r0 // VR].ap()[r0 % VR:r0 % VR + 128, :], v[:], reads=[vb], writes=[DV])
        for which in range(2):
            for h in range(8):
                ps, psb = psr.next()
                mm_group(P, ps, psb, wc, wcb, 8, (which * 16 + h) * 128, 128, hT, hTb, slice(0, 512), 512)
                ps2, ps2b = psr.next()
                mm_group(P, ps2, ps2b, wc, wcb, 8, (which * 16 + 8 + h) * 128, 128, hT, hTb, slice(0, 512), 512)
                t1, t1b = f32r.next()
                P.op("dve", lambda e, t1=t1, ps=ps, ct=ct: e.tensor_tensor(out=t1[:], in0=ps[:], in1=ct[:], op=ALU.mult), reads=[psb, ctb], writes=[t1b])
                t2, t2b = f32r.next()
                P.op("dve", lambda e, t2=t2, ps2=ps2, st_=st_: e.tensor_tensor(out=t2[:], in0=ps2[:], in1=st_[:], op=ALU.mult), reads=[ps2b, stb_], writes=[t2b])
                o, ob = b16r.next()
                P.op("pool", lambda e, o=o, t1=t1, t2=t2: e.tensor_tensor(out=o[:], in0=t1[:], in1=t2[:], op=ALU.add), reads=[t1b, t2b], writes=[ob])
                if which == 0:
                    P.dma(Q1T.ap()[h, :, sl], o[:], reads=[ob], writes=[DQ])
                else:
                    P.dma(K1S[h].ap()[:, sl], o[:], reads=[ob], writes=[DK])
    groups = npair_groups
    for h in range(8):
        P.op("pool", lambda e, h=h: e.collective_compute("AllGather", ALU.bypass, replica_groups=groups, ins=[K1S[h].ap()],
                                                         outs=[K1G[h].ap()]), reads=[DK], writes=[DKG], dma=True, inc=1)
    for c in range(NVC):
        P.op("pool", lambda e, c=c: e.collective_compute("AllGather", ALU.bypass, replica_groups=groups, ins=[V1S[c].ap()],
                                                         outs=[V1G[c].ap()]), reads=[DV], writes=[DVG], dma=True, inc=1)
    P.emit()
    if upto == "F":
        return nc

    def diff_setup(P):
        C = attn_setup_common(P)
        C["KT"] = Rot(P, "K", 2, [128, 2, SH], BF16)
        C["Q0"] = Rot(P, "Qa", 2, [128, SH], BF16)
        C["Q1"] = Rot(P, "Qb", 2, [128, SH], BF16)
        for rot in (C["Q0"], C["Q1"]):
            for (t_, b_) in rot.items:
                P.op("pool", lambda e, t_=t_: e.memset(t_[:], 0.0), writes=[b_])
        C["VT"] = Rot(P, "V", 2, [128, NT, 128], BF16)
        cols = P.sb("cols", [128, 32], F32); colsb = P.buf("cols")
        P.dma(cols[:], cols_in.ap(), writes=[colsb])
        ones16 = P.sb("ones16", [128, 128], BF16); o16b_ = P.buf("ones16")
        P.op("pool", lambda e: e.memset(ones16[:], 1.0), writes=[o16b_])
        lv = P.sb("lv", [128, 256], F32); lvb = P.buf("lv")
        P.dma(lv[:], bcast_rows(rows_in, 256, 6 * D), writes=[lvb])
        lam = P.sb("lam", [128, 8], F32); lamb = P.buf("lam")
        pr = P.sb("pr", [128, 128], F32); prb = P.buf("pr")
        P.op("dve", lambda e: e.tensor_tensor(out=pr[:, 0:64], in0=lv[:, 0:64], in1=lv[:, 64:128], op=ALU.mult), reads=[lvb], writes=[prb])
        P.op("dve", lambda e: e.tensor_tensor(out=pr[:, 64:128], in0=lv[:, 128:192], in1=lv[:, 192:256], op=ALU.mult), reads=[lvb], writes=[prb])
        P.op("dve", lambda e: e.reduce_sum(out=lam[:, 0:2], in_=pr[:].rearrange("p (a b) -> p a b", b=64), axis=AX.X), reads=[prb], writes=[lamb])
        P.op("act", lambda e: e.activation(out=lam[:, 2:4], in_=lam[:, 0:2], func=AF.Exp), reads=[lamb], writes=[lamb])
        P.op("dve", lambda e: e.tensor_tensor(out=lam[:, 4:5], in0=lam[:, 2:3], in1=lam[:, 3:4], op=ALU.subtract), reads=[lamb], writes=[lamb])
        P.op("dve", lambda e: e.tensor_scalar(out=lam[:, 5:6], in0=lam[:, 4:5], scalar1=LAMBDA_INIT, scalar2=-1.0, op0=ALU.add, op1=ALU.mult),
             reads=[lamb], writes=[lamb])
        epsd = P.sb("epsd", [128, 1], F32); epsdb = P.buf("epsd")
        P.op("pool", lambda e: e.memset(epsd[:], EPS / (1.0 - LAMBDA_INIT) ** 2), writes=[epsdb])
        C.update(lam=lam, lamb=lamb, eps=epsd, epsb=epsdb, cols=cols, colsb=colsb, ones16=ones16, ones16b=o16b_,
                 sq16=Rot(P, "sq16", 2, [128, 512], BF16))
        return C

    def diff_finish(P, C, hi, qb, po, acc, pd):
        lam, lamb = C["lam"], C["lamb"]
        d0, d0b = den_matmul(P, C, [acc[0]])
        d1, d1b = pd[1]
        r0, r0b = C["r32"].next()
        P.op("dve", lambda e: e.reciprocal(out=r0[:], in_=d0[:]), reads=[d0b], writes=[r0b])
        r1, r1b = C["r32"].next()
        P.op("dve", lambda e: e.reciprocal(out=r1[:], in_=d1[:]), reads=[d1b], writes=[r1b])
        (o0, o0b), (o1, o1b) = po
        a, ab = C["r32"].next()
        P.op("dve", lambda e: e.tensor_tensor(out=a[:], in0=o0[:], in1=r0[:], op=ALU.mult), reads=[o0b, r0b], writes=[ab])
        t, tb = C["r32"].next()
        P.op("dve", lambda e: e.tensor_tensor(out=t[:], in0=o1[:], in1=r1[:], op=ALU.mult), reads=[o1b, r1b], writes=[tb])
        P.op("dve", lambda e: e.scalar_tensor_tensor(out=a[:], in0=t[:], scalar=lam[:, 5:6], in1=a[:], op0=ALU.mult, op1=ALU.add),
             reads=[ab, tb, lamb], writes=[ab])
        sq, sqb = C["sq16"].next()
        P.op("act", lambda e: e.activation(out=sq[:], in_=a[:], func=AF.Square), reads=[ab], writes=[sqb])
        ss, ssb = C["ss"].next()
        P.op("pe", lambda e: e.matmul(ss[:], lhsT=C["ones16"][:], rhs=sq[:], start=True, stop=True), reads=[C["ones16b"], sqb], writes=[ssb])
        rs, rsb = r0, r0b
        P.op("act", lambda e: e.activation(out=rs[:], in_=ss[:], func=AF.Sqrt, bias=C["eps"][:, 0:1],
                                           scale=1.0 / (128 * (1.0 - LAMBDA_INIT) ** 2)), reads=[ssb, C["epsb"]], writes=[rsb])
        P.op("dve", lambda e: e.reciprocal(out=rs[:], in_=rs[:]), reads=[rsb], writes=[rsb])
        o16, o16b = C["o16"].next()
        P.op("dve", lambda e: e.scalar_tensor_tensor(out=o16[:], in0=a[:], scalar=C["cols"][:, 25:26], in1=rs[:], op0=ALU.mult, op1=ALU.mult),
             reads=[ab, rsb, C["colsb"]], writes=[o16b])
        P.dma(MIXT1.ap()[hi, :, qb * 512:(qb + 1) * 512], o16[:], reads=[o16b], writes=[C["mx"]], eng="pool")

    def diff_head(h):
        def load(P, slot):
            C = diff_C[0]
            kt, ktb = C["KT"].next()
            kap = K1G[h].ap().rearrange("(r p) s -> p r s", r=2)
            for r in range(2):
                P.dma(kt[:, r, :], kap[:, r, :], writes=[ktb])
            q0, q0b = C["Q0"].next()
            q1, q1b = C["Q1"].next()
            P.dma(q0[0:64, :], Q1T.ap()[h][0:64, :], writes=[q0b])
            P.dma(q1[64:128, :], Q1T.ap()[h][64:128, :], writes=[q1b])
            vt, vb = C["VT"].next()
            for r in range(2):
                for c in range(NVC):
                    n0 = r * (SH // 128) + c * (VR // 128)
                    vap = V1G[c].ap()[r * VR:(r + 1) * VR, h * 128:(h + 1) * 128]
                    P.dma(vt[:, n0:n0 + VR // 128, :], vap.rearrange("(n p) d -> p n d", p=128), writes=[vb])
            return dict(kq_bufs=[ktb, q0b, q1b], v=(vt, vb), ktf=kt[:].rearrange("p r s -> p (r s)"), q=(q0, q1))

        def qk(e, ctx, s_, c, kt, qb):
            return e.matmul(s_[:], lhsT=ctx["ktf"][:, kt * 128:(kt + 1) * 128], rhs=ctx["q"][c][:, qb * 512:(qb + 1) * 512],
                            start=True, stop=True)

        return dict(load=load, qk=qk)

    diff_C = [None]

    def diff_setup_wrap(P):
        diff_C[0] = diff_setup(P)
        return diff_C[0]

    attn_fm(nc, "pG", S, SH, [diff_head(h) for h in range(8)], 2, DIFF_SCALE, 1, 4, 6, diff_setup_wrap, diff_finish, den_pe=(1,))
    if upto == "G":
        return nc

    outproj_phase("pH", MIXT1, w_o1, X1, False, X2A)
    ffn_phase("pI", 1, X2A, out, True)
    return nc


def _swap_halves(w, width):
    n = w.shape[1] // width
    w4 = w.reshape(w.shape[0], n, 2, width // 2)
    return np.ascontiguousarray(w4[:, :, ::-1, :]).reshape(w.shape[0], n * width)


def prep_inputs(S, ncores, x, norm_attn, norm_ffn, ffn_w_gate, ffn_w_up, ffn_w_down, ab_w_in, hgrn_lower_bound,
                hgrn_out_norm, mla_q_norm, mla_w_uq, mla_kv_norm, mla_w_ukv, ab_w_out, c_w_in,
                diff_lambda_q1, diff_lambda_k1, diff_lambda_q2, diff_lambda_k2, diff_out_norm, c_w_out, final_norm):
    f32 = np.float32
    w_in = np.asarray(ab_w_in[0], f32)
    sp = np.cumsum([0, 512, 512, 512, 512, 512, 384, 256, 64])
    Wq, Wffw, Wfbw, Wi, Wg, Wcq, Wckv, Wkr = [w_in[:, sp[i]:sp[i + 1]] for i in range(8)]
    Wkr_sw = _swap_halves(Wkr, 64)
    uq = np.asarray(mla_w_uq[0], f32).reshape(384, 4, 192)
    uq_n = uq[:, :, :128].reshape(384, 512)
    uq_r = uq[:, :, 128:].reshape(384, 256)
    w_uq = np.concatenate([uq_n, uq_r, _swap_halves(uq_r, 64)], 1)
    ukv = np.asarray(mla_w_ukv[0], f32).reshape(256, 4, 256)
    w_ukv = np.concatenate([ukv[:, :, :128].reshape(256, 512), ukv[:, :, 128:].reshape(256, 512)], 1)
    cw = np.asarray(c_w_in[0], f32)
    cq, ck, cv = cw[:, :1024], cw[:, 1024:2048], cw[:, 2048:]
    w_c = np.concatenate([cq, _swap_halves(cq, 64), ck, _swap_halves(ck, 64), cv], 1)
    inv = (1.0 / (10000.0 ** (np.arange(0, 64, 2, dtype=f32) / f32(64)))).astype(f32)
    p = np.arange(128)
    sign = np.where((p % 64) < 32, -1.0, 1.0).astype(f32)
    rows = np.zeros((8, D), f32)
    rows[0] = norm_attn[0]; rows[1] = norm_attn[1]; rows[2] = norm_ffn[0]; rows[3] = norm_ffn[1]; rows[4] = final_norm
    rows[5, :128] = diff_out_norm[0]
    rows[6, 0:64] = diff_lambda_q1[0]; rows[6, 64:128] = diff_lambda_k1[0]
    rows[6, 128:192] = diff_lambda_q2[0]; rows[6, 192:256] = diff_lambda_k2[0]
    lbraw = np.asarray(hgrn_lower_bound, f32)
    shared = dict(w_uq=w_uq, w_ukv=w_ukv, w_o0=np.asarray(ab_w_out[0], f32), w_c=w_c, w_o1=np.asarray(c_w_out[0], f32),
                  rows=rows)
    for l in range(2):
        shared[f"ffn_g{l}"] = np.asarray(ffn_w_gate[l], f32)
        shared[f"ffn_u{l}"] = np.asarray(ffn_w_up[l], f32)
        shared[f"ffn_d{l}"] = np.asarray(ffn_w_down[l], f32)
    per_r = []
    for r in range(2):
        fd = (Wffw, Wfbw) if r == 0 else (Wfbw, Wffw)
        w_a = np.concatenate([Wq, fd[0], fd[1], Wg, Wcq, Wckv, Wkr, Wkr, Wkr_sw, Wkr_sw, Wi], 1)
        cols = np.zeros((128, 32), f32)
        for d in range(2):
            od = d if r == 0 else 1 - d
            cols[:, d * 4:(d + 1) * 4] = lbraw[od, 0].reshape(4, 128).T
            cols[:, 8 + d * 4:8 + (d + 1) * 4] = lbraw[od, 1].reshape(4, 128).T
        cols[:, 16:20] = np.asarray(hgrn_out_norm[0], f32).reshape(4, 128).T
        cols[:, 20:23] = np.asarray(mla_q_norm[0], f32).reshape(3, 128).T
        cols[:, 23:25] = np.asarray(mla_kv_norm[0], f32).reshape(2, 128).T
        cols[:, 25] = np.asarray(diff_out_norm[0], f32)
        posv = np.arange(S, dtype=f32) if r == 0 else (S - 1 - np.arange(S)).astype(f32)
        ang = (posv[None, :] * inv[p % 32][:, None]).astype(f32)
        per_r.append(dict(w_a=np.ascontiguousarray(w_a), cols=cols, ropec_t=np.cos(ang).astype(f32),
                          ropes_t=(np.sin(ang) * sign[:, None]).astype(f32)))
    in_maps = []
    for c in range(ncores):
        b, r = c // 2, c % 2
        xb = np.asarray(x[b], f32)
        if r == 1:
            xb = xb[::-1]
        m = dict(shared)
        m.update(per_r[r])
        m["x"] = np.ascontiguousarray(xb)
        in_maps.append(m)
    return in_maps


_CACHE = {}


def run(S, ncores, inputs, debug=False, upto=None):
    key = (S, ncores, debug, upto)
    if key not in _CACHE:
        groups = [[2 * i, 2 * i + 1] for i in range(ncores // 2)]
        _CACHE[key] = build(S, groups, debug, upto)
    nc = _CACHE[key]
    in_maps = prep_inputs(S, ncores, **inputs)
    res = run_bass_kernel_spmd(nc, in_maps, core_ids=list(range(ncores)))
    B = ncores // 2
    SH = S // 2
    out = np.zeros((B, S, D), np.float32)
    for c in range(ncores):
        b, r = c // 2, c % 2
        o = res.results[c]["out"]
        if r == 0:
            out[b, :SH] = o
        else:
            out[b, SH:] = o[::-1]
    return out, res


def kernel(**inputs):
    out, _ = run(8192, 8, inputs)
    return out
```
